# Optimizing a Trainium2 kernel written in Bass

```python
import jax, jax.numpy as jnp
from jax import lax
import numpy as np

D_MODEL = 2048
BATCH = 16
SEQ = 2048
DEPTH = 2

GRID_W = 64
Q_BLOCK = 128
ROPE_THETA = 10000.0
EPS = 1e-6

GQA_HEADS = 6
GQA_KV_HEADS = 2
GQA_HEAD_DIM = 128
GQA_WIDTH = GQA_HEADS * GQA_HEAD_DIM
GQA_KV_WIDTH = GQA_KV_HEADS * GQA_HEAD_DIM

MLA_HEADS = 4
MLA_Q_LORA = 512
MLA_KV_LORA = 256
MLA_NOPE_DIM = 128
MLA_ROPE_DIM = 64
MLA_V_DIM = 128
MLA_WIDTH = MLA_HEADS * MLA_V_DIM

SSD_HEADS = 12
SSD_HEAD_DIM = 64
SSD_GROUPS = 2
SSD_STATE = 128
SSD_CONV = 5
SSD_CHUNK = 128
SSD_INNER = SSD_HEADS * SSD_HEAD_DIM
SSD_CONV_DIM = SSD_INNER + 2 * SSD_GROUPS * SSD_STATE

MIX_WIDTH = GQA_WIDTH + MLA_WIDTH + SSD_INNER
IN_SPLITS = (GQA_WIDTH, GQA_KV_WIDTH, GQA_KV_WIDTH, MLA_Q_LORA, MLA_KV_LORA, MLA_ROPE_DIM, SSD_INNER, SSD_CONV_DIM, 2 * SSD_HEADS)
IN_COLS = GQA_WIDTH + 2 * GQA_KV_WIDTH + MLA_Q_LORA + MLA_KV_LORA + MLA_ROPE_DIM + SSD_INNER + SSD_CONV_DIM + 2 * SSD_HEADS

FFN_HIDDEN = -(-8 * D_MODEL // (3 * 256)) * 256

kernel_name = 'hybrid_gqa_mla_ssd_encoder_block'


def rms_norm(x, g):
    xf = x.astype(jnp.float32)
    y = xf * lax.rsqrt(jnp.mean(xf * xf, axis=-1, keepdims=True) + EPS)
    return (y * g).astype(x.dtype)


def axial_rope_tables(seq_len, rot_dim, dtype):
    rows = seq_len // GRID_W
    row_idx = jnp.repeat(jnp.arange(rows), GRID_W).astype(jnp.float32)
    col_idx = jnp.tile(jnp.arange(GRID_W), rows).astype(jnp.float32)
    axis_dim = rot_dim // 2
    inv_freq = jnp.power(ROPE_THETA, -jnp.arange(0, axis_dim, 2, dtype=jnp.float32) / axis_dim)
    ang_r = row_idx[:, None] * inv_freq[None, :]
    ang_c = col_idx[:, None] * inv_freq[None, :]
    return (jnp.cos(ang_r).astype(dtype), jnp.sin(ang_r).astype(dtype),
            jnp.cos(ang_c).astype(dtype), jnp.sin(ang_c).astype(dtype))


def rotate(x, cos, sin):
    x1, x2 = jnp.split(x, 2, axis=-1)
    cos = cos[:, None, :]
    sin = sin[:, None, :]
    return jnp.concatenate([x1 * cos - x2 * sin, x1 * sin + x2 * cos], axis=-1)


def apply_axial_rope(x, tables):
    cos_r, sin_r, cos_c, sin_c = tables
    x_row, x_col = jnp.split(x, 2, axis=-1)
    return jnp.concatenate([rotate(x_row, cos_r, sin_r), rotate(x_col, cos_c, sin_c)], axis=-1)


def blocked_attention(q, k, v, scale):
    b, s, h, dk = q.shape
    hkv, dv = k.shape[2], v.shape[-1]
    rep = h // hkv
    nb = s // Q_BLOCK
    qb = q.reshape(b, nb, Q_BLOCK, hkv, rep, dk).transpose(1, 0, 2, 3, 4, 5)

    def attend(q_blk):
        logits = jnp.einsum('bqgrd,bsgd->bgrqs', q_blk, k).astype(jnp.float32) * scale
        probs = jax.nn.softmax(logits, axis=-1).astype(v.dtype)
        return jnp.einsum('bgrqs,bsgd->bqgrd', probs, v)

    out = lax.map(attend, qb)
    return out.transpose(1, 0, 2, 3, 4, 5).reshape(b, s, h * dv)


def segsum(a):
    t = a.shape[-1]
    a_rep = jnp.broadcast_to(a[..., :, None], a.shape + (t,))
    strict_lower = jnp.tril(jnp.ones((t, t), dtype=bool), k=-1)
    seg = jnp.cumsum(jnp.where(strict_lower, a_rep, 0.0), axis=-2)
    lower = jnp.tril(jnp.ones((t, t), dtype=bool))
    return jnp.where(lower, seg, -jnp.inf)


def ssd_chunked(x, dt, a_neg, bm, cm):
    b, s, h, p = x.shape
    g, n = bm.shape[2], bm.shape[3]
    e = h // g
    nc = s // SSD_CHUNK
    f32 = jnp.float32
    xd = (x.astype(f32) * dt[..., None]).reshape(b, nc, SSD_CHUNK, g, e, p)
    a = (dt * a_neg).reshape(b, nc, SSD_CHUNK, g, e).transpose(0, 3, 4, 1, 2)
    bc = bm.astype(f32).reshape(b, nc, SSD_CHUNK, g, n)
    cc = cm.astype(f32).reshape(b, nc, SSD_CHUNK, g, n)
    a_cs = jnp.cumsum(a, axis=-1)
    cb = jnp.einsum('bclgn,bcsgn->bgcls', cc, bc)
    w_diag = cb[:, :, None] * jnp.exp(segsum(a))
    y_diag = jnp.einsum('bgecls,bcsgep->bclgep', w_diag, xd)
    to_end = jnp.exp(a_cs[..., -1:] - a_cs).transpose(0, 3, 4, 1, 2)
    states = jnp.einsum('bclgn,bclgep->bcgepn', bc, xd * to_end[..., None])
    states = jnp.concatenate([jnp.zeros_like(states[:, :1]), states], axis=1)
    chunk_a = jnp.pad(a_cs[..., -1], ((0, 0), (0, 0), (0, 0), (1, 0)))
    chunk_decay = jnp.exp(segsum(chunk_a))
    states = jnp.einsum('bgezc,bcgepn->bzgepn', chunk_decay, states)[:, :-1]
    from_start = jnp.exp(a_cs).transpose(0, 3, 4, 1, 2)
    y_off = jnp.einsum('bclgn,bcgepn->bclgep', cc, states) * from_start[..., None]
    return (y_diag + y_off).reshape(b, s, h, p)


def depthwise_centred_conv(x, w):
    pad = SSD_CONV // 2
    return lax.conv_general_dilated(x, w[:, None, :], window_strides=(1,), padding=[(pad, pad)],
                                    dimension_numbers=('NWC', 'WIO', 'NWC'), feature_group_count=x.shape[-1])


def gqa_group(q, k, v, q_norm_g, k_norm_g, rope):
    b, s = q.shape[:2]
    q = q.reshape(b, s, GQA_HEADS, GQA_HEAD_DIM)
    k = k.reshape(b, s, GQA_KV_HEADS, GQA_HEAD_DIM)
    v = v.reshape(b, s, GQA_KV_HEADS, GQA_HEAD_DIM)
    q = apply_axial_rope(rms_norm(q, q_norm_g), rope)
    k = apply_axial_rope(rms_norm(k, k_norm_g), rope)
    return blocked_attention(q, k, v, GQA_HEAD_DIM ** -0.5)


def mla_group(c_q, c_kv, k_pe, q_norm_g, w_uq, kv_norm_g, w_ukv, rope):
    b, s = c_q.shape[:2]
    q = (rms_norm(c_q, q_norm_g) @ w_uq).reshape(b, s, MLA_HEADS, MLA_NOPE_DIM + MLA_ROPE_DIM)
    q_nope, q_pe = q[..., :MLA_NOPE_DIM], q[..., MLA_NOPE_DIM:]
    kv = (rms_norm(c_kv, kv_norm_g) @ w_ukv).reshape(b, s, MLA_HEADS, MLA_NOPE_DIM + MLA_V_DIM)
    k_nope, v = kv[..., :MLA_NOPE_DIM], kv[..., MLA_NOPE_DIM:]
    q_pe = apply_axial_rope(q_pe, rope)
    k_pe = apply_axial_rope(k_pe[:, :, None, :], rope)
    q = jnp.concatenate([q_nope, q_pe], axis=-1)
    k = jnp.concatenate([k_nope, jnp.broadcast_to(k_pe, (b, s, MLA_HEADS, MLA_ROPE_DIM))], axis=-1)
    return blocked_attention(q, k, v, (MLA_NOPE_DIM + MLA_ROPE_DIM) ** -0.5)


def ssd_group(z, xbc, dt_raw, conv_w, conv_b, dt_bias, a_log, d_skip, norm_g):
    b, s = z.shape[:2]
    xbc = jax.nn.silu(depthwise_centred_conv(xbc, conv_w) + conv_b)
    xs, bm, cm = jnp.split(xbc, [SSD_INNER, SSD_INNER + SSD_GROUPS * SSD_STATE], axis=-1)
    xs = xs.reshape(b, s, SSD_HEADS, SSD_HEAD_DIM)
    bm = bm.reshape(b, s, SSD_GROUPS, SSD_STATE)
    cm = cm.reshape(b, s, SSD_GROUPS, SSD_STATE)
    dt = jax.nn.softplus(dt_raw.reshape(b, s, 2, SSD_HEADS).astype(jnp.float32) + dt_bias)
    a_neg = -jnp.exp(a_log.astype(jnp.float32))
    y_fwd = ssd_chunked(xs, dt[:, :, 0], a_neg[0], bm, cm)
    flip = lambda t: jnp.flip(t, axis=1)
    y_bwd = flip(ssd_chunked(flip(xs), flip(dt[:, :, 1]), a_neg[1], flip(bm), flip(cm)))
    y = y_fwd + y_bwd + xs * d_skip[:, None]
    y = y.reshape(b, s, SSD_INNER) * jax.nn.silu(z)
    y = rms_norm(y.reshape(b, s, SSD_GROUPS, SSD_INNER // SSD_GROUPS), norm_g.reshape(SSD_GROUPS, -1))
    return y.reshape(b, s, SSD_INNER)


def hybrid_mixer(h, w_in, q_norm_g, k_norm_g, mla_q_norm_g, w_uq, mla_kv_norm_g, w_ukv,
                 conv_w, conv_b, dt_bias, a_log, d_skip, ssd_norm_g, w_out, rope_a, rope_b):
    proj = h @ w_in
    idx = np.cumsum(IN_SPLITS)[:-1].tolist()
    q_a, k_a, v_a, cq_b, ckv_b, kpe_b, z_c, xbc_c, dt_c = jnp.split(proj, idx, axis=-1)
    o_a = gqa_group(q_a, k_a, v_a, q_norm_g, k_norm_g, rope_a)
    o_b = mla_group(cq_b, ckv_b, kpe_b, mla_q_norm_g, w_uq, mla_kv_norm_g, w_ukv, rope_b)
    o_c = ssd_group(z_c, xbc_c, dt_c, conv_w, conv_b, dt_bias, a_log, d_skip, ssd_norm_g)
    o = jnp.concatenate([o_a.astype(h.dtype), o_b.astype(h.dtype), o_c.astype(h.dtype)], axis=-1)
    return o @ w_out


def swiglu_ffn(h, w_gate_up, w_down):
    gate, up = jnp.split(h @ w_gate_up, 2, axis=-1)
    return (jax.nn.silu(gate) * up) @ w_down


def setup_inputs(seed: int = 0) -> dict:
    key = jax.random.key(seed)
    ks = iter(jax.random.split(key, 32))
    f32 = jnp.float32
    D, L = D_MODEL, DEPTH

    def nrm(shape, std):
        return std * jax.random.normal(next(ks), shape, f32)

    def gain(shape):
        return 1.0 + nrm(shape, 0.02)

    dt_init = jnp.exp(jax.random.uniform(next(ks), (L, 2, SSD_HEADS), f32, np.log(1e-3), np.log(1e-1)))
    dt_bias = dt_init + jnp.log(-jnp.expm1(-dt_init))
    a_log = jnp.log(jax.random.uniform(next(ks), (L, 2, SSD_HEADS), f32, 1.0, 16.0))
    return {
        'x': nrm((BATCH, SEQ, D), 1.0),
        'c': nrm((BATCH, D), 1.0),
        'w_ada': nrm((L, D, 6 * D), 0.5 * D ** -0.5),
        'b_ada': nrm((L, 6 * D), 0.01),
        'norm1_g': gain((L, D)),
        'norm2_g': gain((L, D)),
        'w_in': nrm((L, D, IN_COLS), D ** -0.5),
        'q_norm_g': gain((L, GQA_HEAD_DIM)),
        'k_norm_g': gain((L, GQA_HEAD_DIM)),
        'mla_q_norm_g': gain((L, MLA_Q_LORA)),
        'w_uq': nrm((L, MLA_Q_LORA, MLA_HEADS * (MLA_NOPE_DIM + MLA_ROPE_DIM)), MLA_Q_LORA ** -0.5),
        'mla_kv_norm_g': gain((L, MLA_KV_LORA)),
        'w_ukv': nrm((L, MLA_KV_LORA, MLA_HEADS * (MLA_NOPE_DIM + MLA_V_DIM)), MLA_KV_LORA ** -0.5),
        'conv_w': nrm((L, SSD_CONV, SSD_CONV_DIM), SSD_CONV ** -0.5),
        'conv_b': nrm((L, SSD_CONV_DIM), 0.01),
        'dt_bias': dt_bias,
        'a_log': a_log,
        'd_skip': gain((L, SSD_HEADS)),
        'ssd_norm_g': gain((L, SSD_INNER)),
        'w_out': nrm((L, MIX_WIDTH, D), MIX_WIDTH ** -0.5),
        'w_gate_up': nrm((L, D, 2 * FFN_HIDDEN), D ** -0.5),
        'w_down': nrm((L, FFN_HIDDEN, D), FFN_HIDDEN ** -0.5),
        'final_norm_g': gain((D,)),
    }


def reference(x, c, w_ada, b_ada, norm1_g, norm2_g, w_in, q_norm_g, k_norm_g, mla_q_norm_g, w_uq,
              mla_kv_norm_g, w_ukv, conv_w, conv_b, dt_bias, a_log, d_skip, ssd_norm_g, w_out,
              w_gate_up, w_down, final_norm_g):
    seq_len = x.shape[1]
    rope_a = axial_rope_tables(seq_len, GQA_HEAD_DIM, x.dtype)
    rope_b = axial_rope_tables(seq_len, MLA_ROPE_DIM, x.dtype)
    c_act = jax.nn.silu(c)
    for l in range(DEPTH):
        mod = c_act @ w_ada[l] + b_ada[l]
        shift1, scale1, gate1, shift2, scale2, gate2 = [m[:, None, :] for m in jnp.split(mod, 6, axis=-1)]
        h = rms_norm(x, norm1_g[l]) * (1 + scale1) + shift1
        mix = hybrid_mixer(h, w_in[l], q_norm_g[l], k_norm_g[l], mla_q_norm_g[l], w_uq[l], mla_kv_norm_g[l],
                           w_ukv[l], conv_w[l], conv_b[l], dt_bias[l], a_log[l], d_skip[l], ssd_norm_g[l],
                           w_out[l], rope_a, rope_b)
        x = x + gate1 * mix
        h = rms_norm(x, norm2_g[l]) * (1 + scale2) + shift2
        x = x + gate2 * swiglu_ffn(h, w_gate_up[l], w_down[l])
    return rms_norm(x, final_norm_g)
```

```python
import numpy as np
import concourse.bass as bass
import concourse.mybir as mybir
from concourse.bass_utils import run_bass_kernel_spmd

F32 = mybir.dt.float32
BF16 = mybir.dt.bfloat16
AF = mybir.ActivationFunctionType
ALU = mybir.AluOpType
AX = mybir.AxisListType

D = 2048
S = 2048
NT = S // 128
DEPTH = 2
EPS = 1e-6
IN_COLS = 4184
FFN = 5632
NCORES = 8


class U:
    __slots__ = ("w", "r", "name", "excl")

    def __init__(self, name="", excl=False):
        self.w = None
        self.r = {}
        self.name = name
        self.excl = excl


class Op:
    __slots__ = ("eng", "fn", "deps", "dma", "sig", "seq", "dsem", "dval", "dprev")

    def __init__(self, eng, fn, deps, dma):
        self.eng = eng
        self.fn = fn
        self.deps = deps
        self.dma = dma
        self.sig = False
        self.seq = 0
        self.dsem = None
        self.dval = 0
        self.dprev = None


ENGS = ("pe", "act", "dve", "pool", "sp")


class Sched:
    def __init__(self, nc, nds=8):
        self.nc = nc
        self.ops = []
        self.nds = nds

    def add(self, eng, fn, reads=(), writes=(), dma=False):
        i = len(self.ops)
        deps = set()
        xr = [u for u in reads if u.excl]
        if xr:
            reads = [u for u in reads if not u.excl]
            writes = list(writes) + [u for u in xr if u not in writes]
        for u in reads:
            if u.w is not None:
                deps.add(u.w)
        for u in writes:
            if u.w is not None:
                deps.add(u.w)
            deps.update(u.r.values())
        for u in reads:
            u.r[eng if not dma else ("dma", eng, i)] = i
        for u in writes:
            u.w = i
            u.r = {}
        deps.discard(i)
        self.ops.append(Op(eng, fn, deps, dma))
        return i

    def emit(self):
        nc = self.nc
        ops = self.ops
        for o in ops:
            real = []
            for d in o.deps:
                od = ops[d]
                if (not od.dma) and od.eng == o.eng and o.eng == "pe" and not o.dma:
                    continue
                real.append(d)
                if not od.dma:
                    od.sig = True
            o.deps = sorted(real)
        cnt = {e: 0 for e in ENGS}
        dcnt = {e: 0 for e in ENGS}
        for o in ops:
            if o.dma:
                n = dcnt[o.eng]
                dcnt[o.eng] += 1
                o.dsem = (o.eng, n % self.nds)
                o.dval = 16 * (n // self.nds + 1)
            elif o.sig:
                cnt[o.eng] += 1
                o.seq = cnt[o.eng]
        self.counts = (cnt, dcnt)
        sems = {e: nc.alloc_semaphore("s_" + e) for e in ENGS}
        dsems = {}
        for e in ENGS:
            if dcnt[e]:
                for k in range(self.nds):
                    dsems[(e, k)] = nc.alloc_semaphore("d_%s%d" % (e, k))
        per = {e: [o for o in ops if o.eng == e] for e in ENGS}

        def run(ename, eng):
            known = {}
            for o in per[ename]:
                if o.dma and o.dval > 16:
                    key = ("d",) + o.dsem
                    v = o.dval - 16
                    if known.get(key, 0) < v:
                        eng.wait_ge(dsems[o.dsem], v)
                        known[key] = v
                for d in o.deps:
                    od = ops[d]
                    if od.dma:
                        key = ("d",) + od.dsem
                        if known.get(key, 0) < od.dval:
                            eng.wait_ge(dsems[od.dsem], od.dval)
                            known[key] = od.dval
                    else:
                        key = od.eng
                        if known.get(key, 0) < od.seq:
                            eng.wait_ge(sems[od.eng], od.seq)
                            known[key] = od.seq
                ins = o.fn(eng)
                if o.dma:
                    ins.then_inc(dsems[o.dsem], 16)
                elif o.sig:
                    ins.then_inc(sems[ename], 1)
            for k in range(self.nds):
                if (ename, k) in dsems:
                    n = dcnt[ename]
                    last = (n - 1 - k) // self.nds + 1 if n - 1 >= k else 0
                    if last > 0:
                        eng.wait_ge(dsems[(ename, k)], 16 * last)

        with nc.Block() as block:
            @block.tensor
            def _(e):
                run("pe", e)

            @block.scalar
            def _(e):
                run("act", e)

            @block.vector
            def _(e):
                run("dve", e)

            @block.gpsimd
            def _(e):
                run("pool", e)

            @block.sync
            def _(e):
                run("sp", e)


def _units(xs):
    out = []
    for x in xs:
        if x is None:
            continue
        if isinstance(x, U):
            out.append(x)
        elif isinstance(x, (list, tuple)):
            out.extend(_units(x))
        else:
            out.extend(x.u)
    return out


class T:
    def __init__(self, ap, units):
        self.ap = ap
        self.u = units

    def __getitem__(self, k):
        return self.ap[k]


class Arena:
    def __init__(self, nc, nbytes, gran=1024):
        self.t = nc.alloc_sbuf_tensor("arena", [128, nbytes // 4], F32)
        self.gran = gran
        self.units = [U("a%d" % i) for i in range((nbytes + gran - 1) // gran)]
        self.top = 0
        self.nbytes = nbytes

    def alloc(self, free_shape, dtype):
        esz = 2 if dtype == BF16 else 4
        n = int(np.prod(free_shape))
        nb = (n * esz + 63) // 64 * 64
        off = self.top
        assert off + nb <= self.nbytes, ("arena overflow", off, nb, self.nbytes)
        self.top = off + nb
        ap = self.t[:, off // 4:(off + nb) // 4]
        if esz == 2:
            ap = ap.bitcast(BF16)
        ap = ap[:, 0:n]
        if len(free_shape) == 2:
            ap = ap.rearrange("p (a b) -> p a b", a=free_shape[0])
        elif len(free_shape) == 3:
            ap = ap.rearrange("p (a b c) -> p a b c", a=free_shape[0], b=free_shape[1])
        units = self.units[off // self.gran:(off + nb - 1) // self.gran + 1]
        return T(ap, units)

    def mark(self):
        return self.top

    def release(self, m):
        self.top = m


class DT:
    def __init__(self, ap):
        self.ap = ap
        self.us = {}

    def u(self, key=0):
        if key not in self.us:
            self.us[key] = U()
        return self.us[key]

    def __getitem__(self, k):
        return self.ap[k]


class Ctx:
    def __init__(self):
        self.nc = bass.Bass("TRN2", target_bir_lowering=False)
        self.sch = Sched(self.nc)
        self.arena = Arena(self.nc, 207 * 1024)
        self.banks = []
        for i in range(8):
            t = self.nc.alloc_psum_tensor("pb%d" % i, [128, 512], F32)
            self.banks.append(T(t[:, :], [U("pb%d" % i, excl=True)]))
        self.bi = 0
        self.rot = [0, 1, 2, 3, 4, 5]
        self.ins = {}
        self.nscr = 0

    def bank(self):
        b = self.banks[self.rot[self.bi % len(self.rot)]]
        self.bi += 1
        return b

    def dram_in(self, name, shape, dt=F32):
        t = self.nc.dram_tensor(name, list(shape), dt, kind="ExternalInput")
        d = DT(t.ap())
        self.ins[name] = d
        return d

    def dram_out(self, name, shape, dt=F32):
        t = self.nc.dram_tensor(name, list(shape), dt, kind="ExternalOutput")
        return DT(t.ap())

    def scratch(self, shape, dt=F32, name=None):
        self.nscr += 1
        t = self.nc.dram_tensor(name or ("scr%d" % self.nscr), list(shape), dt, kind="Internal")
        return DT(t.ap())

    def dma(self, out, in_, reads, writes, q="sp", **kw):
        self.sch.add(q, lambda e: e.dma_start(out=out, in_=in_, **kw), _units(reads), _units(writes), dma=True)

    def mm(self, out, lhsT, rhs, start, stop, reads, writes):
        self.sch.add("pe", lambda e: e.matmul(out, lhsT, rhs, start=start, stop=stop), _units(reads), _units(writes))

    def tr(self, out, in_, ident, reads, writes):
        self.sch.add("pe", lambda e: e.transpose(out, in_, ident), _units(reads), _units(writes))

    def op(self, eng, fn, reads, writes):
        self.sch.add(eng, fn, _units(reads), _units(writes))

    def act(self, out, in_, func, reads, writes, eng="act", **kw):
        self.sch.add("act", lambda e: e.activation(out=out, in_=in_, func=func, **kw), _units(reads), _units(writes))

    def copy(self, eng, out, in_, reads, writes):
        if eng == "act":
            self.sch.add("act", lambda e: e.copy(out=out, in_=in_), _units(reads), _units(writes))
        else:
            self.sch.add(eng, lambda e: e.tensor_copy(out=out, in_=in_), _units(reads), _units(writes))

    def tt(self, eng, out, in0, in1, op, reads, writes):
        self.sch.add(eng, lambda e: e.tensor_tensor(out=out, in0=in0, in1=in1, op=op), _units(reads), _units(writes))

    def ts(self, eng, out, in0, s1, s2, op0, op1, reads, writes, **kw):
        if op1 is None:
            self.sch.add(eng, lambda e: e.tensor_scalar(out=out, in0=in0, scalar1=s1, scalar2=s2, op0=op0, **kw),
                         _units(reads), _units(writes))
        else:
            self.sch.add(eng, lambda e: e.tensor_scalar(out=out, in0=in0, scalar1=s1, scalar2=s2, op0=op0, op1=op1, **kw),
                         _units(reads), _units(writes))

    def stt(self, eng, out, in0, scalar, in1, op0, op1, reads, writes):
        self.sch.add(eng, lambda e: e.scalar_tensor_tensor(out=out, in0=in0, scalar=scalar, in1=in1, op0=op0, op1=op1),
                     _units(reads), _units(writes))


WSHAPES = {
    "w_ada": (DEPTH, D, 6 * D), "b_ada": (DEPTH, 6 * D), "norm1_g": (DEPTH, D), "norm2_g": (DEPTH, D),
    "w_in": (DEPTH, D, IN_COLS), "q_norm_g": (DEPTH, 128), "k_norm_g": (DEPTH, 128),
    "mla_q_norm_g": (DEPTH, 512), "w_uq": (DEPTH, 512, 768), "mla_kv_norm_g": (DEPTH, 256),
    "w_ukv": (DEPTH, 256, 1024), "conv_w": (DEPTH, 5, 1280), "conv_b": (DEPTH, 1280),
    "dt_bias": (DEPTH, 2, 12), "a_log": (DEPTH, 2, 12), "d_skip": (DEPTH, 12), "ssd_norm_g": (DEPTH, 768),
    "w_out": (DEPTH, D, D), "w_gate_up": (DEPTH, D, 2 * FFN), "w_down": (DEPTH, FFN, D),
    "final_norm_g": (D,),
}


def host_consts():
    import ml_dtypes
    c = {}
    c["ident"] = np.eye(128, dtype=np.float32).astype(ml_dtypes.bfloat16)
    c["identf"] = np.eye(128, dtype=np.float32)
    pos = np.arange(S)
    row = (pos // 64).astype(np.float32)
    col = (pos % 64).astype(np.float32)

    def tab(rot_dim):
        axis_dim = rot_dim // 2
        inv = np.power(np.float32(10000.0), -np.arange(0, axis_dim, 2, dtype=np.float32) / np.float32(axis_dim)).astype(np.float32)
        ar = (row[:, None] * inv[None, :]).astype(np.float32)
        ac = (col[:, None] * inv[None, :]).astype(np.float32)
        cr, sr, cc_, sc = np.cos(ar), np.sin(ar), np.cos(ac), np.sin(ac)
        cosx = np.concatenate([cr, cr, cc_, cc_], axis=1)
        sinx = np.concatenate([-sr, sr, -sc, sc], axis=1)
        return np.stack([cosx, sinx], axis=1).astype(np.float32)

    c["ropeA"] = tab(128)
    c["ropeB"] = tab(64)
    k = np.arange(128)
    c["triF"] = (k[:, None] <= k[None, :]).astype(np.float32)
    c["triR"] = (k[:, None] >= k[None, :]).astype(np.float32)
    sel = np.zeros((128, 24, 128), np.float32)
    for j in range(24):
        sel[j, j, :] = 1.0
    c["sel"] = sel
    mf = np.where(k[None, :] >= k[:, None], 0.0, -30000.0).astype(np.float32)
    mb = np.where(k[None, :] <= k[:, None], 0.0, -30000.0).astype(np.float32)
    c["maskF"] = np.tile(mf, (1, 3)).astype(ml_dtypes.bfloat16)
    c["maskB"] = np.tile(mb, (1, 3)).astype(ml_dtypes.bfloat16)
    return c


def layout_conv(conv_w, conv_b):
    L = conv_w.shape[0]
    o = np.zeros((L, 128, 10, 6), np.float32)
    o[:, :, :, 0:5] = conv_w.reshape(L, 5, 10, 128).transpose(0, 3, 2, 1)
    o[:, :, :, 5] = conv_b.reshape(L, 10, 128).transpose(0, 2, 1)
    return o


class Prog:
    def __init__(self, nseq=2, nlayer=DEPTH, dbg=()):
        self.c = Ctx()
        self.nseq = nseq
        self.nlayer = nlayer
        self.dbg = set(dbg)
        self.dbg_outs = {}
        c = self.c
        self.x = c.dram_in("x", (2, S, D))
        self.cT = c.dram_in("cT", (128, 32))
        self.w = {k: c.dram_in(k, v) for k, v in WSHAPES.items()}
        self.k_ident = c.dram_in("ident", (128, 128), BF16)
        self.k_identf = c.dram_in("identf", (128, 128))
        self.k_ropeA = c.dram_in("ropeA", (S, 2, 128))
        self.k_ropeB = c.dram_in("ropeB", (S, 2, 64))
        self.k_triF = c.dram_in("triF", (128, 128))
        self.k_triR = c.dram_in("triR", (128, 128))
        self.k_sel = c.dram_in("sel", (128, 24, 128))
        self.k_maskF = c.dram_in("maskF", (128, 384), BF16)
        self.k_maskB = c.dram_in("maskB", (128, 384), BF16)
        self.k_convT = c.dram_in("convT", (DEPTH, 128, 10, 6))
        self.out = c.dram_out("out", (2, S, D))

    def scr(self, name, shape, dt=F32):
        if name in self.dbg and name not in self.dbg_outs:
            return self.dbg_out(name, shape, dt)
        return self.c.scratch(shape, dt)

    def dump_mix(self, mixT, n=16):
        if "mix" in self.dbg and "mix" not in self.dbg_outs:
            d = self.dbg_out("mix", (16, 128, S), BF16)
            for k in range(n):
                self.c.dma(d.ap[k], mixT[k][:, :], [mixT[k]], [d.u(k)])

    def dbg_out(self, name, shape, dt=F32):
        d = self.c.dram_out("dbg_" + name, shape, dt)
        self.dbg_outs[name] = d
        return d

    def consts(self):
        c = self.c
        a = c.arena
        self.ident = a.alloc([128], BF16)
        c.dma(self.ident[:, :], self.k_ident[:, :], [], [self.ident])
        self.identf = a.alloc([128], F32)
        c.dma(self.identf[:, :], self.k_identf[:, :], [], [self.identf])
        self.eps = a.alloc([1], F32)
        self.one1 = a.alloc([1], F32)
        c.op("pool", lambda e: e.memset(self.one1[:, :], 1.0), [], [self.one1])
        self.ones = a.alloc([128], BF16)
        c.op("pool", lambda e: e.memset(self.ones[:, :], 1.0), [], [self.ones])
        c.op("pool", lambda e: e.memset(self.eps[:, :], EPS), [], [self.eps])

    def mod_phase(self):
        c = self.c
        a = c.arena
        m = a.mark()
        self.mod = c.scratch((DEPTH, 2, 6 * D), name="mod")
        ct = a.alloc([32], F32)
        c.dma(ct[:, :], self.cT[:, :], [], [ct])
        sg = a.alloc([32], F32)
        c.act(sg[:, :], ct[:, :], AF.Sigmoid, [ct], [sg])
        c.tt("dve", ct[:, :], ct[:, :], sg[:, :], ALU.mult, [ct, sg], [ct])
        ctp = a.alloc([16, 128], F32)
        c.op("pool", lambda e: e.memset(ctp[:, :, :], 0.0), [], [ctp])
        c.copy("dve", ctp[:, :, 0:2], ct[:, :].rearrange("p (k b) -> p k b", b=2), [ct, ctp], [ctp])
        wb = [a.alloc([16, 512], F32) for _ in range(2)]
        bb = [a.alloc([512], F32) for _ in range(2)]
        ob = [a.alloc([512], F32) for _ in range(2)]
        NB = 6 * D // 512
        i = 0
        for l in range(self.nlayer):
            wv = self.w["w_ada"].ap[l].rearrange("(k p) n -> p k n", p=128)
            for nb in range(NB):
                w_t, b_t, o_t = wb[i % 2], bb[i % 2], ob[i % 2]
                i += 1
                cs = slice(nb * 512, (nb + 1) * 512)
                c.dma(w_t[:, :, :], wv[:, :, cs], [], [w_t])
                c.dma(b_t[0:2, :], self.w["b_ada"].ap[l:l + 1, cs].to_broadcast([2, 512]), [], [b_t])
                pb = c.bank()
                for k in range(16):
                    c.mm(pb[:, :], ctp[:, k, :], w_t[:, k, :], k == 0, k == 15, [ctp, w_t], [pb])
                c.tt("dve", o_t[0:2, :], pb[0:2, :], b_t[0:2, :], ALU.add, [pb, b_t], [o_t])
                c.dma(self.mod.ap[l, :, cs], o_t[0:2, :], [o_t], [self.mod.u((l, nb))])
        a.release(m)

    def load_bc(self, tile, l, b, j):
        c = self.c
        src = self.mod.ap[l, b:b + 1, j * D:(j + 1) * D].to_broadcast([128, D])
        c.dma(tile[:, :], src, [self.mod.u((l, 4 * j + q)) for q in range(4)], [tile])

    def load_vec_bc(self, tile, dram_ap_row, n):
        c = self.c
        c.dma(tile[:, 0:n], dram_ap_row.to_broadcast([128, n]), [], [tile])

    def bf_bank(self, pb, n):
        return pb.ap.bitcast(BF16).rearrange("p (a b) -> p a b", a=8)[:, 0:n, :]

    def rstd_from_ss(self, rstd, ss, n):
        c = self.c
        c.ts("dve", rstd, ss, 1.0 / n, EPS, ALU.mult, ALU.add, [], [])

    def norm_phase(self, b, l, xsrc, xkey, gname, jscale, jshift, hT, dst=None, after=None):
        c = self.c
        a = c.arena
        m = a.mark()
        G = a.alloc([D], F32)
        SH = a.alloc([D], F32)
        tmpg = a.alloc([D], F32)
        self.load_bc(G, l, b, jscale)
        self.load_vec_bc(tmpg, self.w[gname].ap[l:l + 1, :], D)
        c.stt("dve", G[:, :], G[:, :], 1.0, tmpg[:, :], ALU.add, ALU.mult, [G, tmpg], [G])
        self.load_bc(SH, l, b, jshift)
        xb = [a.alloc([D], F32) for _ in range(2)]
        junk = a.alloc([D], BF16)
        hb = [a.alloc([D], BF16) for _ in range(2)]
        st = [a.alloc([4], F32) for _ in range(2)]
        for t in range(NT):
            xt, ht, s_ = xb[t % 2], hb[t % 2], st[t % 2]
            c.dma(xt[:, :], xsrc[t * 128:(t + 1) * 128, :], [xkey(t)], [xt])
            c.act(junk[:, :], xt[:, :], AF.Square, [xt], [junk, s_], accum_out=s_[:, 0:1])
            c.act(s_[:, 1:2], s_[:, 0:1], AF.Sqrt, [s_, self.eps], [s_], scale=1.0 / D, bias=self.eps[:, 0:1])
            c.op("dve", lambda e, s_=s_: e.reciprocal(out=s_[:, 2:3], in_=s_[:, 1:2]), [s_], [s_])
            c.stt("dve", xt[:, :], xt[:, :], s_[:, 2:3], G[:, :], ALU.mult, ALU.mult, [xt, s_, G], [xt])
            c.tt("pool", ht[:, :], xt[:, :], SH[:, :], ALU.add, [xt, SH], [ht])
            for half in range(2):
                pb = c.bank()
                pv = self.bf_bank(pb, 8)
                for k in range(8):
                    kc = half * 8 + k
                    c.tr(pv[:, k, :], ht[:, kc * 128:(kc + 1) * 128], self.ident[:, :], [ht, self.ident], [pb])
                if dst is None:
                    c.copy("act" if half == 0 else "dve", hT[t][:, half * 8:(half + 1) * 8, :], pv, [pb], [hT[t]])
                else:
                    dt_, off = dst(t)
                    c.copy("act" if half == 0 else "dve", dt_[:, half * 8:(half + 1) * 8, off:off + 128], pv, [pb], [dt_])
            if after is not None:
                after(t)
        a.release(m)

    def load_w_bf16(self, wt, src):
        self.c.dma(wt, src, [], [], q="pool")

    def in_proj_tm(self, b, l, hT, proj):
        c = self.c
        a = c.arena
        m = a.mark()
        wv = self.w["w_in"].ap[l].rearrange("(k p) n -> p k n", p=128)
        blocks = [(j * 512, j * 512, 512) for j in range(8)] + [(4096, 4096, 88)]
        wb = [a.alloc([16, 512], BF16) for _ in range(2)]
        ob = [a.alloc([512], F32) for _ in range(3)]
        i = 0
        for bi, (wc, pc, n) in enumerate(blocks):
            w_t = wb[bi % 2]
            c.dma(w_t[:, :, 0:n], wv[:, :, wc:wc + n], [], [w_t], q="pool")
            for t in range(NT):
                pb = c.bank()
                for k in range(16):
                    c.mm(pb[:, 0:n], hT[t][:, k, :], w_t[:, k, 0:n], k == 0, k == 15, [hT[t], w_t], [pb])
                o_t = ob[i % 3]
                i += 1
                c.copy("act" if i % 2 else "dve", o_t[:, 0:n], pb[:, 0:n], [pb], [o_t])
                c.dma(proj.ap[t * 128:(t + 1) * 128, pc:pc + n], o_t[:, 0:n], [o_t], [proj.u((t, bi))])
        a.release(m)

    def rms_heads(self, x3, nh, hd, ss, junk3, eng="dve"):
        pass

    def rope_tm(self, x2d, tab, nh, hd, T1, T2, SW, out2d, units_r, units_w, out3=False):
        c = self.c
        q = hd // 4
        n = nh * hd
        x3 = x2d.rearrange("p (h d) -> p h d", h=nh)
        xq = x2d.rearrange("p (g b f) -> p g b f", b=2, f=q)
        swq = SW[:, 0:n].rearrange("p (g b f) -> p g b f", b=2, f=q)
        sw3 = SW[:, 0:n].rearrange("p (h d) -> p h d", h=nh)
        t13 = T1[:, 0:n].rearrange("p (h d) -> p h d", h=nh)
        t23 = T2[:, 0:n].rearrange("p (h d) -> p h d", h=nh)
        cosb = tab[:, 0:1, :].to_broadcast([128, nh, hd])
        sinb = tab[:, 1:2, :].to_broadcast([128, nh, hd])
        c.copy("act", swq[:, :, 0, :], xq[:, :, 1, :], units_r, [SW])
        c.copy("act", swq[:, :, 1, :], xq[:, :, 0, :], units_r, [SW])
        c.tt("dve", t13, x3, cosb, ALU.mult, units_r, [T1])
        c.tt("pool", t23, sw3, sinb, ALU.mult, [SW] + units_r, [T2])
        if out3:
            c.tt("dve", out2d, t13, t23, ALU.add, [T1, T2], units_w)
        else:
            c.tt("dve", out2d, T1[:, 0:n], T2[:, 0:n], ALU.add, [T1, T2], units_w)

    def attn_core(self, heads, mixT, scale):
        c = self.c
        a = c.arena
        m = a.mark()
        LOOK = 2
        pt = [a.alloc([512], BF16) for _ in range(4)]
        rec = [a.alloc([512], F32) for _ in range(2)]
        old_rot = c.rot
        c.rot = [0, 1, 2, 3]
        accs = [(c.banks[4], c.banks[5]), (c.banks[6], c.banks[7])]
        iters = [(hd, qt, kc) for hd in heads for qt in range(4) for kc in range(16)]
        sbs = {}

        def emit_s(i):
            hd, qt, kc = iters[i]
            sb = c.bank()
            np_ = len(hd["kparts"])
            for pi, (kf, qf) in enumerate(hd["kparts"]):
                c.mm(sb[:, :], kf(kc), qf(qt), pi == 0, pi == np_ - 1, hd["reads"], [sb])
            sbs[i] = sb

        def emit_rest(i):
            hd, qt, kc = iters[i]
            sb = sbs.pop(i)
            accO, accD = accs[(i // 16) % 2]
            p_t = pt[i % 4]
            c.act(p_t[:, :], sb[:, :], AF.Exp, [sb], [p_t], scale=scale)
            c.mm(accO[:, :], hd["v_fn"](kc), p_t[:, :], kc == 0, kc == 15, [p_t] + hd["reads"], [accO])
            c.mm(accD[:, :], self.ones[:, :], p_t[:, :], kc == 0, kc == 15, [p_t, self.ones], [accD])
            if kc == 15:
                r_t = rec[(i // 16) % 2]
                c.op("dve", lambda e, r_t=r_t, accD=accD: e.reciprocal(out=r_t[:, :], in_=accD[:, :]), [accD], [r_t])
                mo = mixT[hd["out"]]
                c.tt("dve", mo[:, qt * 512:(qt + 1) * 512], accO[:, :], r_t[:, :], ALU.mult, [accO, r_t], [mo])

        n = len(iters)
        for i in range(n + LOOK):
            if i < n:
                emit_s(i)
            if i - LOOK >= 0:
                emit_rest(i - LOOK)
        c.rot = old_rot
        a.release(m)

    def gqa_phase(self, b, l, proj, mixT):
        c = self.c
        a = c.arena
        m = a.mark()
        qk = a.alloc([8, S], BF16)
        vv = a.alloc([NT, 256], BF16)
        m2 = a.mark()
        gain = a.alloc([8, 128], F32)
        for h in range(8):
            nm = "q_norm_g" if h < 6 else "k_norm_g"
            c.dma(gain[:, h, :], self.w[nm].ap[l:l + 1, :].to_broadcast([128, 128]), [], [gain])
        xb = [a.alloc([1280], F32) for _ in range(2)]
        sqs = [a.alloc([1024], F32) for _ in range(2)]
        st = [a.alloc([32], F32) for _ in range(2)]
        T1s = [a.alloc([1024], F32) for _ in range(2)]
        T2s = [a.alloc([1024], F32) for _ in range(2)]
        SWs = [a.alloc([1024], F32) for _ in range(2)]
        ob = [a.alloc([1024], BF16) for _ in range(2)]
        tb = [a.alloc([256], F32) for _ in range(2)]
        for t in range(NT):
            xt, s_, o_t, tab = xb[t % 2], st[t % 2], ob[t % 2], tb[t % 2]
            sq, T1, T2, SW = sqs[t % 2], T1s[t % 2], T2s[t % 2], SWs[t % 2]
            c.dma(xt[:, :], proj.ap[t * 128:(t + 1) * 128, 0:1280], [proj.u((t, j)) for j in range(3)], [xt])
            c.dma(tab[:, :], self.k_ropeA.ap[t * 128:(t + 1) * 128].rearrange("p a f -> p (a f)"), [], [tab])
            x3 = xt[:, 0:1024].rearrange("p (h d) -> p h d", h=8)
            c.tt("pool", sq[:, :], xt[:, 0:1024], xt[:, 0:1024], ALU.mult, [xt], [sq])
            c.op("dve", lambda e, s_=s_, sq=sq: e.tensor_reduce(out=s_[:, 0:8], in_=sq[:, :].rearrange("p (h d) -> p h d", h=8),
                                                         axis=AX.X, op=ALU.add), [sq], [s_])
            c.act(s_[:, 8:16], s_[:, 0:8], AF.Sqrt, [s_, self.eps], [s_], scale=1.0 / 128, bias=self.eps[:, 0:1])
            c.op("dve", lambda e, s_=s_: e.reciprocal(out=s_[:, 16:24], in_=s_[:, 8:16]), [s_], [s_])
            c.tt("dve", x3, x3, s_[:, 16:24].unsqueeze(2).to_broadcast([128, 8, 128]), ALU.mult, [xt, s_], [xt])
            c.tt("pool", x3, x3, gain[:, :, :], ALU.mult, [xt, gain], [xt])
            self.rope_tm(xt[:, 0:1024], tab[:, :].rearrange("p (a f) -> p a f", a=2), 8, 128, T1, T2, SW, o_t[:, :], [xt, tab], [o_t])
            c.copy("act", vv[:, t, :], xt[:, 1024:1280], [xt], [vv])
            pb = c.bank()
            pv = self.bf_bank(pb, 8)
            for h in range(8):
                c.tr(pv[:, h, :], o_t[:, h * 128:(h + 1) * 128], self.ident[:, :], [o_t, self.ident], [pb])
            c.copy("act", qk[:, :, t * 128:(t + 1) * 128], pv, [pb], [qk])
        a.release(m2)
        heads = []
        for g in range(2):
            for r in range(3):
                h = g * 3 + r
                heads.append(dict(
                    kparts=[(lambda kc, g=g: qk[:, 6 + g, kc * 128:(kc + 1) * 128],
                             lambda qt, h=h: qk[:, h, qt * 512:(qt + 1) * 512])],
                    v_fn=lambda kc, g=g: vv[:, kc, g * 128:(g + 1) * 128],
                    reads=[qk, vv], out=h))
        self.attn_core(heads, mixT, 128 ** -0.5)
        a.release(m)

    def mla_phase(self, b, l, proj, mixT):
        c = self.c
        a = c.arena
        m = a.mark()
        qn = a.alloc([4, S], BF16)
        qp = a.alloc([4, S], BF16)
        kn = a.alloc([4, S], BF16)
        kp = a.alloc([S], BF16)
        vb = a.alloc([NT, 512], BF16)
        m2 = a.mark()
        cqT = a.alloc([4, S], BF16)
        ckvT = a.alloc([2, S], BF16)
        wuq = a.alloc([4, 768], BF16)
        wukv = a.alloc([2, 1024], BF16)
        c.dma(wuq[:, :, :], self.w["w_uq"].ap[l].rearrange("(k p) n -> p k n", p=128), [], [wuq], q="pool")
        c.dma(wukv[:, :, :], self.w["w_ukv"].ap[l].rearrange("(k p) n -> p k n", p=128), [], [wukv], q="pool")
        gq = a.alloc([512], F32)
        gkv = a.alloc([256], F32)
        self.load_vec_bc(gq, self.w["mla_q_norm_g"].ap[l:l + 1, :], 512)
        self.load_vec_bc(gkv, self.w["mla_kv_norm_g"].ap[l:l + 1, :], 256)
        xb = [a.alloc([832], F32) for _ in range(2)]
        junks = [a.alloc([512], BF16) for _ in range(2)]
        st = [a.alloc([8], F32) for _ in range(2)]
        TK = [[a.alloc([64], F32) for _ in range(3)] for _ in range(2)]
        TQ = [[a.alloc([256], F32) for _ in range(3)] for _ in range(2)]
        ob = [a.alloc([896], BF16) for _ in range(2)]
        tb = [a.alloc([128], F32) for _ in range(2)]
        qpb = [a.alloc([256], F32) for _ in range(2)]
        qpo = [a.alloc([4, 128], BF16) for _ in range(2)]
        for o_ in ob:
            c.op("pool", lambda e, o_=o_: e.memset(o_[:, 832:896], 0.0), [o_], [o_])
        for o_ in qpo:
            c.op("pool", lambda e, o_=o_: e.memset(o_[:, :, :], 0.0), [o_], [o_])
        wqpe = a.alloc([4, 256], BF16)
        wv = a.alloc([2, 512], BF16)
        for h in range(4):
            c.copy("dve", wqpe[:, :, h * 64:(h + 1) * 64], wuq[:, :, h * 192 + 128:h * 192 + 192], [wuq, wqpe], [wqpe])
            c.copy("dve", wv[:, :, h * 128:(h + 1) * 128], wukv[:, :, h * 256 + 128:h * 256 + 256], [wukv, wv], [wv])
        for t in range(NT):
            xt, s_, o_t, tab = xb[t % 2], st[t % 2], ob[t % 2], tb[t % 2]
            junk = junks[t % 2]
            c.dma(xt[:, :], proj.ap[t * 128:(t + 1) * 128, 1280:2112], [proj.u((t, j)) for j in (2, 3, 4)], [xt])
            c.dma(tab[:, :], self.k_ropeB.ap[t * 128:(t + 1) * 128].rearrange("p a f -> p (a f)"), [], [tab])
            c.act(junk[:, 0:512], xt[:, 0:512], AF.Square, [xt], [junk, s_], accum_out=s_[:, 0:1])
            c.act(junk[:, 0:256], xt[:, 512:768], AF.Square, [xt], [junk, s_], accum_out=s_[:, 1:2])
            c.ts("dve", s_[:, 0:1], s_[:, 0:1], 0.5, None, ALU.mult, None, [s_], [s_])
            c.act(s_[:, 2:4], s_[:, 0:2], AF.Sqrt, [s_, self.eps], [s_], scale=1.0 / 256, bias=self.eps[:, 0:1])
            c.op("dve", lambda e, s_=s_: e.reciprocal(out=s_[:, 4:6], in_=s_[:, 2:4]), [s_], [s_])
            c.stt("dve", o_t[:, 0:512], xt[:, 0:512], s_[:, 4:5], gq[:, 0:512], ALU.mult, ALU.mult, [xt, s_, gq], [o_t])
            c.stt("dve", o_t[:, 512:768], xt[:, 512:768], s_[:, 5:6], gkv[:, 0:256], ALU.mult, ALU.mult, [xt, s_, gkv], [o_t])
            tabv = tab[:, :].rearrange("p (a f) -> p a f", a=2)
            T1, T2, SW = TK[t % 2]
            self.rope_tm(xt[:, 768:832], tabv, 1, 64, T1, T2, SW, o_t[:, 768:832], [xt, tab], [o_t])
            pb = c.bank()
            pv = self.bf_bank(pb, 8)
            for k in range(6):
                c.tr(pv[:, k, :], o_t[:, k * 128:(k + 1) * 128], self.ident[:, :], [o_t, self.ident], [pb])
            c.tr(pv[:, 6, :], o_t[:, 768:896], self.ident[:, :], [o_t, self.ident], [pb])
            tc_ = slice(t * 128, (t + 1) * 128)
            c.copy("act", cqT[:, :, tc_], pv[:, 0:4, :], [pb], [cqT])
            c.copy("dve", ckvT[:, :, tc_], pv[:, 4:6, :], [pb], [ckvT])
            c.copy("act", kp[:, tc_], pv[:, 6, :], [pb], [kp])
            pq = c.bank()
            for k in range(4):
                c.mm(pq[:, 0:256], cqT[:, k, tc_], wqpe[:, k, :], k == 0, k == 3, [cqT, wqpe], [pq])
            qb_, qo_ = qpb[t % 2], qpo[t % 2]
            c.copy("act", qb_[:, :], pq[:, 0:256], [pq], [qb_])
            T1, T2, SW = TQ[t % 2]
            self.rope_tm(qb_[:, :], tabv, 4, 64, T1, T2, SW, qo_[:, :, 0:64], [qb_, tab], [qo_], out3=True)
            pb2 = c.bank()
            pv2 = self.bf_bank(pb2, 8)
            for h in range(4):
                c.tr(pv2[:, h, :], qo_[:, h, :], self.ident[:, :], [qo_, self.ident], [pb2])
            c.copy("dve", qp[:, :, tc_], pv2[:, 0:4, :], [pb2], [qp])
            pvb = c.bank()
            for k in range(2):
                c.mm(pvb[:, :], ckvT[:, k, tc_], wv[:, k, :], k == 0, k == 1, [ckvT, wv], [pvb])
            c.copy("act", vb[:, t, :], pvb[:, :], [pvb], [vb])
        i = 0
        for h in range(4):
            for tq in range(4):
                ts_ = slice(tq * 512, (tq + 1) * 512)
                p1 = c.bank()
                for k in range(4):
                    c.mm(p1[:, :], wuq[:, k, h * 192:h * 192 + 128], cqT[:, k, ts_], k == 0, k == 3, [wuq, cqT], [p1])
                c.copy("act", qn[:, h, ts_], p1[:, :], [p1], [qn])
                p2 = c.bank()
                for k in range(2):
                    c.mm(p2[:, :], wukv[:, k, h * 256:h * 256 + 128], ckvT[:, k, ts_], k == 0, k == 1, [wukv, ckvT], [p2])
                c.copy("dve", kn[:, h, ts_], p2[:, :], [p2], [kn])
        a.release(m2)
        heads = []
        for h in range(4):
            heads.append(dict(
                kparts=[(lambda kc, h=h: kn[:, h, kc * 128:(kc + 1) * 128], lambda qt, h=h: qn[:, h, qt * 512:(qt + 1) * 512]),
                        (lambda kc: kp[:, kc * 128:(kc + 1) * 128], lambda qt, h=h: qp[:, h, qt * 512:(qt + 1) * 512])],
                v_fn=lambda kc, h=h: vb[:, kc, h * 128:(h + 1) * 128],
                reads=[qn, qp, kn, kp, vb], out=6 + h))
        self.attn_core(heads, mixT, 192 ** -0.5)
        a.release(m)

    def ssd_consts(self):
        c = self.c
        a = c.arena
        self.triF = a.alloc([128], F32)
        self.triR = a.alloc([128], F32)
        self.onesf = a.alloc([128], F32)
        self.sel = a.alloc([24, 128], F32)
        self.maskF = a.alloc([384], BF16)
        self.maskB = a.alloc([384], BF16)
        c.dma(self.triF[:, :], self.k_triF[:, :], [], [self.triF])
        c.dma(self.triR[:, :], self.k_triR[:, :], [], [self.triR])
        c.op("pool", lambda e: e.memset(self.onesf[:, :], 1.0), [], [self.onesf])
        c.dma(self.sel[:, :, :], self.k_sel[:, :, :], [], [self.sel])
        c.dma(self.maskF[:, :], self.k_maskF[:, :], [], [self.maskF])
        c.dma(self.maskB[:, :], self.k_maskB[:, :], [], [self.maskB])

    def ssd_phase(self, b, l, proj, mixT):
        c = self.c
        a = c.arena
        m = a.mark()
        yf_d = c.scratch((S, 768))
        aneg = a.alloc([24], F32)
        dtb = a.alloc([24], F32)
        dsk = a.alloc([12], F32)
        ng = a.alloc([768], F32)
        cw = a.alloc([10, 6], F32)
        c.dma(aneg[:, :], self.w["a_log"].ap[l:l + 1].rearrange("o a h -> o (a h)").to_broadcast([128, 24]), [], [aneg])
        c.act(aneg[:, :], aneg[:, :], AF.Exp, [aneg], [aneg])
        c.ts("dve", aneg[:, :], aneg[:, :], -1.0, None, ALU.mult, None, [aneg], [aneg])
        c.dma(dtb[:, :], self.w["dt_bias"].ap[l:l + 1].rearrange("o a h -> o (a h)").to_broadcast([128, 24]), [], [dtb])
        c.dma(dsk[:, :], self.w["d_skip"].ap[l:l + 1, :].to_broadcast([128, 12]), [], [dsk])
        self.load_vec_bc(ng, self.w["ssd_norm_g"].ap[l:l + 1, :], 768)
        c.dma(cw[:, :, :], self.k_convT.ap[l], [], [cw])
        dt = a.alloc([NT, 24], F32)
        af = a.alloc([NT, 24], F32)
        ab = a.alloc([NT, 24], F32)
        c.op("pool", lambda e: e.memset(af[:, :, :], 0.0), [], [af])
        c.op("pool", lambda e: e.memset(ab[:, :, :], 0.0), [], [ab])
        for t in range(NT):
            c.dma(dt[:, t, :], proj.ap[t * 128:(t + 1) * 128, 4160:4184], [proj.u((t, 8))], [dt])
        c.tt("dve", dt[:, :, :], dt[:, :, :], dtb[:, :].unsqueeze(1).to_broadcast([128, NT, 24]), ALU.add, [dt, dtb], [dt])
        c.act(dt[:, :, :], dt[:, :, :], AF.Exp, [dt], [dt])
        c.act(dt[:, :, :], dt[:, :, :], AF.Ln, [dt, self.one1], [dt], bias=self.one1[:, 0:1])
        c.tt("dve", af[:, :, 0:12], dt[:, :, 0:12], aneg[:, 0:12].unsqueeze(1).to_broadcast([128, NT, 12]), ALU.mult, [dt, aneg, af], [af])
        c.tt("dve", ab[:, :, 12:24], dt[:, :, 12:24], aneg[:, 12:24].unsqueeze(1).to_broadcast([128, NT, 12]), ALU.mult, [dt, aneg, ab], [ab])
        bcT = a.alloc([4, S], BF16)
        xs_tm = a.alloc([NT, 768], BF16)
        b_tm = a.alloc([NT, 256], BF16)
        m2 = a.mark()
        xin = [a.alloc([128], F32) for _ in range(3)]
        xT = [a.alloc([S + 4], F32) for _ in range(2)]
        acc = [a.alloc([S], F32) for _ in range(2)]
        cvo = [a.alloc([S], BF16) for _ in range(2)]
        for xt_ in xT:
            c.op("pool", lambda e, xt_=xt_: e.memset(xt_[:, 0:2], 0.0), [xt_], [xt_])
            c.op("pool", lambda e, xt_=xt_: e.memset(xt_[:, S + 2:S + 4], 0.0), [xt_], [xt_])
        ii = 0
        for blk in range(10):
            x_t, ac, co = xT[blk % 2], acc[blk % 2], cvo[blk % 2]
            for t4 in range(4):
                pb = c.bank()
                for q in range(4):
                    t = t4 * 4 + q
                    xi = xin[ii % 3]
                    ii += 1
                    cs = 2880 + blk * 128
                    c.dma(xi[:, :], proj.ap[t * 128:(t + 1) * 128, cs:cs + 128], [proj.u((t, j)) for j in (5, 6, 7, 8)], [xi])
                    c.mm(pb[:, q * 128:(q + 1) * 128], xi[:, :], self.identf[:, :], True, True, [xi, self.identf], [pb])
                c.copy("act", x_t[:, 2 + t4 * 512:2 + (t4 + 1) * 512], pb[:, :], [pb], [x_t])
            eng = "dve" if blk % 2 == 0 else "pool"
            c.ts(eng, ac[:, :], x_t[:, 0:S], cw[:, blk, 0:1], None, ALU.mult, None, [x_t, cw], [ac])
            for j in range(1, 5):
                c.stt("dve", ac[:, :], x_t[:, j:j + S], cw[:, blk, j:j + 1], ac[:, :], ALU.mult, ALU.add, [x_t, cw, ac], [ac])
            if blk < 6 or blk in (6, 7):
                c.act(co[:, :], ac[:, :], AF.Silu, [ac, cw], [co], bias=cw[:, blk, 5:6])
                for t8 in range(2):
                    pb = c.bank()
                    pv = self.bf_bank(pb, 8)
                    for q in range(8):
                        t = t8 * 8 + q
                        c.tr(pv[:, q, :], co[:, t * 128:(t + 1) * 128], self.ident[:, :], [co, self.ident], [pb])
                    if blk < 6:
                        c.copy("act" if t8 else "dve", xs_tm[:, t8 * 8:(t8 + 1) * 8, blk * 128:(blk + 1) * 128], pv, [pb], [xs_tm])
                    else:
                        c.copy("act" if t8 else "dve", b_tm[:, t8 * 8:(t8 + 1) * 8, (blk - 6) * 128:(blk - 5) * 128], pv, [pb], [b_tm])
            if blk >= 6:
                c.act(bcT[:, blk - 6, :], ac[:, :], AF.Silu, [ac, cw], [bcT], bias=cw[:, blk, 5:6])
        a.release(m2)
        H = [[a.alloc([384], F32) for _ in range(2)] for _ in range(2)]
        Hb = [[a.alloc([384], BF16) for _ in range(2)] for _ in range(2)]
        NS = 4
        apad = [a.alloc([128], F32) for _ in range(NS)]
        for ap_ in apad:
            c.op("pool", lambda e, ap_=ap_: e.memset(ap_[:, :], 0.0), [], [ap_])
        cs_ = [a.alloc([72], F32) for _ in range(NS)]
        ex = [a.alloc([72], F32) for _ in range(NS)]
        pfm = [a.alloc([256], F32) for _ in range(NS)]
        xd = [a.alloc([768], BF16) for _ in range(NS)]
        xdw = [a.alloc([768], BF16) for _ in range(2)] * 2
        xdw = [xdw[0], xdw[0], xdw[1], xdw[1]]
        yo = [a.alloc([768], F32) for _ in range(NS)]
        cbT = [[a.alloc([128], F32) for _ in range(2)] for _ in range(2)]
        eD = [[a.alloc([384], F32) for _ in range(2)] for _ in range(2)]
        wT = [[a.alloc([384], BF16) for _ in range(3)] for _ in range(2)]
        ytmp = [[a.alloc([384], F32)] * 2 for _ in range(2)]
        zb = [a.alloc([768], F32) for _ in range(1)] * 2
        yfb = [a.alloc([768], F32) for _ in range(1)] * 2
        yn = [a.alloc([768], BF16) for _ in range(1)] * 2
        st = [a.alloc([8], F32) for _ in range(2)]
        junk = a.alloc([384], BF16)
        accA, accB = c.banks[6], c.banks[7]
        for d in range(2):
            for g in range(2):
                c.op("pool", lambda e, d=d, g=g: e.memset(H[d][g][:, :], 0.0), [H[d][g]], [H[d][g]])
                c.op("pool", lambda e, d=d, g=g: e.memset(Hb[d][g][:, :], 0.0), [Hb[d][g]], [Hb[d][g]])
        cnt = {"w": [0, 0], "f": 0}

        def chunk_step(d, ch, k, finalize):
            mask = self.maskF if d == 0 else self.maskB
            a_own = af if d == 0 else ab
            tri = self.triF if d == 0 else self.triR
            cst, ext, pf, xd_, xw_ = cs_[k], ex[k], pfm[k], xd[k], xdw[k]
            tc_ = slice(ch * 128, (ch + 1) * 128)
            d0 = d * 12
            ap_ = apad[k]
            c.copy("pool", ap_[:, 0:24], a_own[:, ch, :], [a_own, ap_], [ap_])
            pb = c.bank()
            c.mm(pb[:, 0:24], tri[:, :], a_own[:, ch, 0:24], True, True, [tri, a_own], [pb])
            c.mm(pb[:, 24:48], self.onesf[:, :], a_own[:, ch, 0:24], True, True, [self.onesf, a_own], [pb])
            c.mm(pb[:, 128:256], ap_[:, :], tri[:, :], True, True, [tri, ap_], [pb])
            c.copy("dve", cst[:, 0:48], pb[:, 0:48], [pb], [cst])
            c.copy("act", pf[:, 0:128], pb[:, 128:256], [pb], [pf])
            c.ts("dve", pf[:, 128:256], pb[:, 128:256], -1.0, None, ALU.mult, None, [pb], [pf])
            c.tt("dve", cst[:, 48:72], cst[:, 24:48], cst[:, 0:24], ALU.subtract, [cst], [cst])
            c.act(ext[:, 0:24], cst[:, 0:24], AF.Exp, [cst], [ext])
            c.act(ext[:, 24:48], cst[:, 48:72], AF.Exp, [cst], [ext])
            c.act(ext[:, 48:72], cst[:, 24:48], AF.Exp, [cst], [ext])
            xs3 = xs_tm[:, ch, :].rearrange("p (h q) -> p h q", h=12)
            xd3 = xd_[:, :].rearrange("p (h q) -> p h q", h=12)
            xw3 = xw_[:, :].rearrange("p (h q) -> p h q", h=12)
            c.tt("pool", xd3, xs3, dt[:, ch, d0:d0 + 12].unsqueeze(2).to_broadcast([128, 12, 64]), ALU.mult, [xs_tm, dt], [xd_])
            c.tt("pool", xw3, xd3, ext[:, 24 + d0:36 + d0].unsqueeze(2).to_broadcast([128, 12, 64]), ALU.mult, [xd_, ext], [xw_])
            y_t = yo[k]
            for g in range(2):
                cb = cbT[d][g]
                Hg, Hbg = H[d][g], Hb[d][g]
                pcb = c.bank()
                c.mm(pcb[:, 0:128], bcT[:, g, tc_], bcT[:, 2 + g, tc_], True, True, [bcT], [pcb])
                c.copy("act", cb[:, :], pcb[:, 0:128], [pcb], [cb])
                accY = accA if g == 0 else accB
                for half in range(2):
                    pd = c.bank()
                    for j3 in range(3):
                        hh = d0 + g * 6 + half * 3 + j3
                        reg = pd[:, j3 * 128:(j3 + 1) * 128]
                        c.mm(reg, self.sel[:, hh, :], pf[:, 0:128], j3 == 0, False, [self.sel, pf], [pd])
                        c.mm(reg, pf[:, 128:256], self.sel[:, hh, :], False, False, [self.sel, pf], [pd])
                    c.mm(pd[:, 0:384], self.ident[:, :], mask[:, :], False, True, [self.ident, mask], [pd])
                    e_t = eD[d][half]
                    c.act(e_t[:, :], pd[:, 0:384], AF.Exp, [pd], [e_t])
                    w_t = wT[d][cnt["w"][d] % 3]
                    cnt["w"][d] += 1
                    c.tt("dve", w_t[:, :].rearrange("p (j s) -> p j s", j=3), e_t[:, :].rearrange("p (j s) -> p j s", j=3),
                         cb[:, :].unsqueeze(1).to_broadcast([128, 3, 128]), ALU.mult, [e_t, cb], [w_t])
                    for j3 in range(3):
                        hl = g * 6 + half * 3 + j3
                        c.mm(accY[:, (half * 3 + j3) * 64:(half * 3 + j3 + 1) * 64], w_t[:, j3 * 128:(j3 + 1) * 128],
                             xd_[:, hl * 64:(hl + 1) * 64], True, True, [w_t, xd_], [accY])
                po = c.bank()
                c.mm(po[:, 0:384], bcT[:, 2 + g, tc_], Hbg[:, :], True, True, [bcT, Hbg], [po])
                yt = ytmp[d][g]
                c.tt("dve", yt[:, :].rearrange("p (h q) -> p h q", h=6), po[:, 0:384].rearrange("p (h q) -> p h q", h=6),
                     ext[:, d0 + g * 6:d0 + g * 6 + 6].unsqueeze(2).to_broadcast([128, 6, 64]), ALU.mult, [po, ext], [yt])
                c.tt("dve", y_t[:, g * 384:(g + 1) * 384], accY[:, 0:384], yt[:, :], ALU.add, [accY, yt], [y_t])
                ps_ = c.bank()
                c.mm(ps_[:, 0:384], b_tm[:, ch, g * 128:(g + 1) * 128], xw_[:, g * 384:(g + 1) * 384], True, True,
                     [b_tm, xw_], [ps_])
                c.tt("pool", Hg[:, :].rearrange("p (h q) -> p h q", h=6), Hg[:, :].rearrange("p (h q) -> p h q", h=6),
                     ext[:, 48 + d0 + g * 6:48 + d0 + g * 6 + 6].unsqueeze(2).to_broadcast([128, 6, 64]), ALU.mult,
                     [Hg, ext], [Hg])
                c.tt("dve", Hg[:, :], Hg[:, :], ps_[:, 0:384], ALU.add, [Hg, ps_], [Hg])
                c.copy("pool", Hbg[:, :], Hg[:, :], [Hg], [Hbg])
            if not finalize:
                c.dma(yf_d.ap[tc_, :], y_t[:, :], [y_t], [yf_d.u(ch)])
                return
            kf = cnt["f"] % 2
            cnt["f"] += 1
            z_t, yf_t, yn_t, s_ = zb[kf], yfb[kf], yn[kf], st[kf]
            c.dma(yf_t[:, :], yf_d.ap[tc_, :], [yf_d.u(ch)], [yf_t])
            c.dma(z_t[:, :], proj.ap[tc_, 2112:2880], [proj.u((ch, 4)), proj.u((ch, 5))], [z_t])
            c.tt("dve", y_t[:, :], y_t[:, :], yf_t[:, :], ALU.add, [y_t, yf_t], [y_t])
            c.tt("pool", yf_t[:, :].rearrange("p (h q) -> p h q", h=12), xs_tm[:, ch, :].rearrange("p (h q) -> p h q", h=12),
                 dsk[:, :].unsqueeze(2).to_broadcast([128, 12, 64]), ALU.mult, [xs_tm, dsk], [yf_t])
            c.tt("dve", y_t[:, :], y_t[:, :], yf_t[:, :], ALU.add, [y_t, yf_t], [y_t])
            c.act(z_t[:, :], z_t[:, :], AF.Silu, [z_t], [z_t])
            c.tt("dve", y_t[:, :], y_t[:, :], z_t[:, :], ALU.mult, [y_t, z_t], [y_t])
            for g in range(2):
                c.act(junk[:, :], y_t[:, g * 384:(g + 1) * 384], AF.Square, [y_t], [junk, s_], accum_out=s_[:, g:g + 1])
            c.act(s_[:, 2:4], s_[:, 0:2], AF.Sqrt, [s_, self.eps], [s_], scale=1.0 / 384, bias=self.eps[:, 0:1])
            c.op("dve", lambda e, s_=s_: e.reciprocal(out=s_[:, 4:6], in_=s_[:, 2:4]), [s_], [s_])
            for g in range(2):
                c.stt("dve", yn_t[:, g * 384:(g + 1) * 384], y_t[:, g * 384:(g + 1) * 384], s_[:, 4 + g:5 + g],
                      ng[:, g * 384:(g + 1) * 384], ALU.mult, ALU.mult, [y_t, s_, ng], [yn_t])
            pb = c.bank()
            pv = self.bf_bank(pb, 8)
            for j in range(6):
                c.tr(pv[:, j, :], yn_t[:, j * 128:(j + 1) * 128], self.ident[:, :], [yn_t, self.ident], [pb])
            for j in range(6):
                c.copy("act" if j % 2 else "dve", mixT[10 + j][:, tc_], pv[:, j, :], [pb], [mixT[10 + j]])

        for step in range(NT):
            chunk_step(0, step, (step % 2), step >= NT // 2)
            chunk_step(1, NT - 1 - step, 2 + (step % 2), step >= NT // 2)
        a.release(m)

    def out_phase(self, b, l, mixT, xsrc, xkey, x1):
        c = self.c
        a = c.arena
        m = a.mark()
        GT = a.alloc([D], F32)
        self.load_bc(GT, l, b, 2)
        wv = self.w["w_out"].ap[l].rearrange("(k p) n -> p k n", p=128)
        wb = [a.alloc([16, 512], BF16) for _ in range(2)]
        xb = [a.alloc([512], F32) for _ in range(3)]
        ob = [a.alloc([512], F32) for _ in range(3)]
        i = 0
        for n in range(4):
            w_t = wb[n % 2]
            ns = slice(n * 512, (n + 1) * 512)
            c.dma(w_t[:, :, :], wv[:, :, ns], [], [w_t], q="pool")
            for t in range(NT):
                x_t, o_t = xb[i % 3], ob[i % 3]
                i += 1
                ts_ = slice(t * 128, (t + 1) * 128)
                c.dma(x_t[:, :], xsrc[ts_, ns], [xkey(t)], [x_t])
                pb = c.bank()
                for k in range(16):
                    c.mm(pb[:, :], mixT[k][:, ts_], w_t[:, k, :], k == 0, k == 15, [mixT[k], w_t], [pb])
                c.tt("dve", o_t[:, :], pb[:, :], GT[:, ns], ALU.mult, [pb, GT], [o_t])
                c.tt("dve", o_t[:, :], o_t[:, :], x_t[:, :], ALU.add, [o_t, x_t], [o_t])
                c.dma(x1.ap[ts_, ns], o_t[:, :], [o_t], [x1.u((t, n))])
        a.release(m)

    def norm2_phase(self, b, l, x1, h2T_d):
        c = self.c
        a = c.arena
        m = a.mark()
        stg = [a.alloc([16, 512], BF16) for _ in range(2)]

        def dst(t):
            return stg[(t // 4) % 2], (t % 4) * 128

        def after(t):
            if t % 4 == 3:
                s_ = stg[(t // 4) % 2]
                q = t // 4
                c.dma(h2T_d.ap[:, :, q * 512:(q + 1) * 512], s_[:, :, :], [s_], [h2T_d.u(q)])

        self.norm_phase(b, l, x1.ap, lambda t: [x1.u((t, n)) for n in range(4)], "norm2_g", 4, 3, None, dst=dst, after=after)
        a.release(m)

    def ffn_phase(self, b, l, x1, h2T_d, x2):
        c = self.c
        a = c.arena
        m = a.mark()
        TB = 1024
        wgu = self.w["w_gate_up"].ap[l].rearrange("(k p) n -> p k n", p=128)
        wdn = self.w["w_down"].ap[l].rearrange("(j p) n -> p j n", p=128)
        g2 = a.alloc([16], F32)
        g2row = a.alloc([128], F32)
        c.op("pool", lambda e: e.memset(g2row[:, :], 0.0), [], [g2row])
        c.dma(g2row[0:16, :], self.mod.ap[l, b, 5 * D:6 * D].rearrange("(k p) -> k p", p=128),
              [self.mod.u((l, 20 + q)) for q in range(4)] + [g2row], [g2row])
        pg2 = c.bank()
        c.mm(pg2[:, 0:16], g2row[:, :], self.identf[:, 0:16], True, True, [g2row, self.identf], [pg2])
        c.copy("dve", g2[:, :], pg2[:, 0:16], [pg2], [g2])
        actT = a.alloc([44, TB], BF16)
        sgb = [a.alloc([512], F32) for _ in range(2)]
        xb = [a.alloc([512], F32) for _ in range(2)]
        ob = [a.alloc([512], F32) for _ in range(2)]
        gtb = [a.alloc([512], F32) for _ in range(4)]
        for tb in range(S // TB):
            m2 = a.mark()
            h2 = a.alloc([16, TB], BF16)
            for q in range(TB // 512):
                qq = tb * (TB // 512) + q
                c.dma(h2[:, :, q * 512:(q + 1) * 512], h2T_d.ap[:, :, qq * 512:(qq + 1) * 512], [h2T_d.u(qq)], [h2])
            wg = [a.alloc([16, 256], BF16) for _ in range(3)]
            wu = [a.alloc([16, 256], BF16) for _ in range(3)]
            isg = 0
            for jg in range(22):
                g_t, u_t = wg[jg % 3], wu[jg % 3]
                c.dma(g_t[:, :, :], wgu[:, :, jg * 256:(jg + 1) * 256], [], [g_t], q="pool")
                c.dma(u_t[:, :, :], wgu[:, :, FFN + jg * 256:FFN + (jg + 1) * 256], [], [u_t], q="pool")
                for jj in range(2):
                    j = jg * 2 + jj
                    for tq in range(TB // 512):
                        tqs = slice(tq * 512, (tq + 1) * 512)
                        pg = c.bank()
                        for k in range(16):
                            c.mm(pg[:, :], g_t[:, k, jj * 128:(jj + 1) * 128], h2[:, k, tqs], k == 0, k == 15, [g_t, h2], [pg])
                        pu = c.bank()
                        for k in range(16):
                            c.mm(pu[:, :], u_t[:, k, jj * 128:(jj + 1) * 128], h2[:, k, tqs], k == 0, k == 15, [u_t, h2], [pu])
                        sg = sgb[isg % 2]
                        isg += 1
                        c.act(sg[:, :], pg[:, :], AF.Silu, [pg], [sg])
                        c.tt("dve", actT[:, j, tqs], pu[:, :], sg[:, :], ALU.mult, [pu, sg], [actT])
            a.release(m2)
            m2 = a.mark()
            wd = [a.alloc([44, 256], BF16) for _ in range(3)]
            igt = 0
            ix = 0
            for sl in range(8):
                w_t = wd[sl % 3]
                c.dma(w_t[:, :, :], wdn[:, :, sl * 256:(sl + 1) * 256], [], [w_t], q="pool")
                for tq in range(TB // 512):
                    tqs = slice(tq * 512, (tq + 1) * 512)
                    gts = []
                    for dd in range(2):
                        dblk = sl * 2 + dd
                        pb = c.bank()
                        for j in range(44):
                            c.mm(pb[:, :], w_t[:, j, dd * 128:(dd + 1) * 128], actT[:, j, tqs], j == 0, j == 43, [w_t, actT], [pb])
                        gt = gtb[igt % 4]
                        igt += 1
                        c.act(gt[:, :], pb[:, :], AF.Identity, [pb, g2], [gt], scale=g2[:, dblk:dblk + 1])
                        gts.append(gt)
                    for tt_ in range(4):
                        tok0 = tb * TB + tq * 512 + tt_ * 128
                        tki = tok0 // 128
                        x_t, o_t = xb[ix % 2], ob[ix % 2]
                        ix += 1
                        cs = slice(sl * 256, (sl + 1) * 256)
                        c.dma(x_t[:, 0:256], x1.ap[tok0:tok0 + 128, cs], [x1.u((tki, sl // 2))], [x_t])
                        pt = c.bank()
                        for dd in range(2):
                            c.mm(pt[:, dd * 128:(dd + 1) * 128], gts[dd][:, tt_ * 128:(tt_ + 1) * 128], self.identf[:, :], True, True,
                                 [gts[dd], self.identf], [pt])
                        c.tt("dve", o_t[:, 0:256], pt[:, 0:256], x_t[:, 0:256], ALU.add, [pt, x_t], [o_t])
                        c.dma(x2.ap[tok0:tok0 + 128, cs], o_t[:, 0:256], [o_t], [x2.u((tki, sl))])
            a.release(m2)
        a.release(m)

    def final_phase(self, b, x2, x2key):
        c = self.c
        a = c.arena
        m = a.mark()
        G = a.alloc([D], F32)
        self.load_vec_bc(G, self.w["final_norm_g"].ap.rearrange("(o d) -> o d", o=1), D)
        xb = [a.alloc([D], F32) for _ in range(2)]
        junk = a.alloc([D], BF16)
        st = [a.alloc([4], F32) for _ in range(2)]
        for t in range(NT):
            xt, s_ = xb[t % 2], st[t % 2]
            ts_ = slice(t * 128, (t + 1) * 128)
            c.dma(xt[:, :], x2.ap[ts_, :], x2key(t), [xt])
            c.act(junk[:, :], xt[:, :], AF.Square, [xt], [junk, s_], accum_out=s_[:, 0:1])
            c.act(s_[:, 1:2], s_[:, 0:1], AF.Sqrt, [s_, self.eps], [s_], scale=1.0 / D, bias=self.eps[:, 0:1])
            c.op("dve", lambda e, s_=s_: e.reciprocal(out=s_[:, 2:3], in_=s_[:, 1:2]), [s_], [s_])
            c.stt("dve", xt[:, :], xt[:, :], s_[:, 2:3], G[:, :], ALU.mult, ALU.mult, [xt, s_, G], [xt])
            c.dma(self.out.ap[b, ts_, :], xt[:, :], [xt], [self.out.u((b, t))])
        a.release(m)

    def build(self, stop_after=None):
        c = self.c
        a = c.arena
        self.consts()
        self.mod_phase()
        for b in range(self.nseq):
            xcur = self.x.ap[b]
            xkey = lambda t: []
            for l in range(self.nlayer):
                m = a.mark()
                proj = self.scr("proj", (S, IN_COLS))
                x1 = self.scr("x1", (S, D))
                x2 = self.scr("x2", (S, D))
                h2T_d = self.scr("h2T", (128, 16, S), BF16)
                self.last = dict(proj=proj, x1=x1, x2=x2, h2T_d=h2T_d)
                mixT = [a.alloc([S], BF16) for _ in range(16)]
                self.mixT = mixT
                m1 = a.mark()
                hT = [a.alloc([16, 128], BF16) for _ in range(NT)]
                self.norm_phase(b, l, xcur, xkey, "norm1_g", 1, 0, hT)
                self.in_proj_tm(b, l, hT, proj)
                a.release(m1)
                if stop_after == "in":
                    return
                self.gqa_phase(b, l, proj, mixT)
                if stop_after == "gqa":
                    self.dump_mix(mixT, 6)
                    return
                self.mla_phase(b, l, proj, mixT)
                if stop_after == "mla":
                    self.dump_mix(mixT, 10)
                    return
                m1 = a.mark()
                self.ssd_consts()
                self.ssd_phase(b, l, proj, mixT)
                a.release(m1)
                self.dump_mix(mixT)
                if stop_after == "ssd":
                    return
                self.out_phase(b, l, mixT, xcur, xkey, x1)
                a.release(m)
                if stop_after == "out":
                    return
                self.norm2_phase(b, l, x1, h2T_d)
                self.ffn_phase(b, l, x1, h2T_d, x2)
                xcur = x2.ap
                xkey = (lambda x2: (lambda t: [x2.u((t, n)) for n in range(8)]))(x2)
                if stop_after == "ffn":
                    return
            self.final_phase(b, x2, xkey)


_PROG = None


def _get_prog():
    global _PROG
    if _PROG is None:
        p = Prog()
        p.build()
        p.c.sch.emit()
        _PROG = p
    return _PROG


def make_in_maps(inputs, ncores=NCORES):
    consts = host_consts()
    consts["convT"] = layout_conv(np.asarray(inputs["conv_w"], np.float32), np.asarray(inputs["conv_b"], np.float32))
    shared = {k: np.ascontiguousarray(np.asarray(inputs[k], np.float32)) for k in WSHAPES}
    shared.update(consts)
    x = np.asarray(inputs["x"], np.float32)
    cc = np.asarray(inputs["c"], np.float32)
    maps = []
    for i in range(ncores):
        m = dict(shared)
        m["x"] = np.ascontiguousarray(x[2 * i:2 * i + 2])
        m["cT"] = np.ascontiguousarray(cc[2 * i:2 * i + 2].reshape(2, 16, 128).transpose(2, 1, 0).reshape(128, 32))
        maps.append(m)
    return maps


def kernel(**inputs):
    p = _get_prog()
    maps = make_in_maps(inputs)
    res = run_bass_kernel_spmd(p.c.nc, maps, core_ids=list(range(NCORES)))
    out = np.concatenate([np.asarray(r["out"], np.float32) for r in res.results], axis=0)
    return out
```

```python
import numpy as np
import concourse.bass as bass
import concourse.mybir as mybir
from concourse.bass_utils import run_bass_kernel_spmd

F32 = mybir.dt.float32
BF16 = mybir.dt.bfloat16
AF = mybir.ActivationFunctionType
ALU = mybir.AluOpType
AX = mybir.AxisListType

D = 2048
S = 2048
NT = S // 128
DEPTH = 2
EPS = 1e-6
IN_COLS = 4184
FFN = 5632
NCORES = 8


class U:
    __slots__ = ("w", "r", "name", "excl")

    def __init__(self, name="", excl=False):
        self.w = None
        self.r = {}
        self.name = name
        self.excl = excl


class Op:
    __slots__ = ("eng", "fn", "deps", "dma", "sig", "seq", "dsem", "dval", "dprev")

    def __init__(self, eng, fn, deps, dma):
        self.eng = eng
        self.fn = fn
        self.deps = deps
        self.dma = dma
        self.sig = False
        self.seq = 0
        self.dsem = None
        self.dval = 0
        self.dprev = None


ENGS = ("pe", "act", "dve", "pool", "sp")


class Sched:
    def __init__(self, nc, nds=8):
        self.nc = nc
        self.ops = []
        self.nds = nds

    def add(self, eng, fn, reads=(), writes=(), dma=False):
        i = len(self.ops)
        deps = set()
        xr = [u for u in reads if u.excl]
        if xr:
            reads = [u for u in reads if not u.excl]
            writes = list(writes) + [u for u in xr if u not in writes]
        for u in reads:
            if u.w is not None:
                deps.add(u.w)
        for u in writes:
            if u.w is not None:
                deps.add(u.w)
            deps.update(u.r.values())
        for u in reads:
            u.r[eng if not dma else ("dma", eng, i)] = i
        for u in writes:
            u.w = i
            u.r = {}
        deps.discard(i)
        self.ops.append(Op(eng, fn, deps, dma))
        return i

    def emit(self):
        nc = self.nc
        ops = self.ops
        for o in ops:
            real = []
            for d in o.deps:
                od = ops[d]
                if (not od.dma) and od.eng == o.eng and o.eng == "pe" and not o.dma:
                    continue
                real.append(d)
                if not od.dma:
                    od.sig = True
            o.deps = sorted(real)
        cnt = {e: 0 for e in ENGS}
        dcnt = {e: 0 for e in ENGS}
        for o in ops:
            if o.dma:
                n = dcnt[o.eng]
                dcnt[o.eng] += 1
                o.dsem = (o.eng, n % self.nds)
                o.dval = 16 * (n // self.nds + 1)
            elif o.sig:
                cnt[o.eng] += 1
                o.seq = cnt[o.eng]
        self.counts = (cnt, dcnt)
        sems = {e: nc.alloc_semaphore("s_" + e) for e in ENGS}
        dsems = {}
        for e in ENGS:
            if dcnt[e]:
                for k in range(self.nds):
                    dsems[(e, k)] = nc.alloc_semaphore("d_%s%d" % (e, k))
        per = {e: [o for o in ops if o.eng == e] for e in ENGS}

        def run(ename, eng):
            known = {}
            for o in per[ename]:
                if o.dma and o.dval > 16:
                    key = ("d",) + o.dsem
                    v = o.dval - 16
                    if known.get(key, 0) < v:
                        eng.wait_ge(dsems[o.dsem], v)
                        known[key] = v
                for d in o.deps:
                    od = ops[d]
                    if od.dma:
                        key = ("d",) + od.dsem
                        if known.get(key, 0) < od.dval:
                            eng.wait_ge(dsems[od.dsem], od.dval)
                            known[key] = od.dval
                    else:
                        key = od.eng
                        if known.get(key, 0) < od.seq:
                            eng.wait_ge(sems[od.eng], od.seq)
                            known[key] = od.seq
                ins = o.fn(eng)
                if o.dma:
                    ins.then_inc(dsems[o.dsem], 16)
                elif o.sig:
                    ins.then_inc(sems[ename], 1)
            for k in range(self.nds):
                if (ename, k) in dsems:
                    n = dcnt[ename]
                    last = (n - 1 - k) // self.nds + 1 if n - 1 >= k else 0
                    if last > 0:
                        eng.wait_ge(dsems[(ename, k)], 16 * last)

        with nc.Block() as block:
            @block.tensor
            def _(e):
                run("pe", e)

            @block.scalar
            def _(e):
                run("act", e)

            @block.vector
            def _(e):
                run("dve", e)

            @block.gpsimd
            def _(e):
                run("pool", e)

            @block.sync
            def _(e):
                run("sp", e)


def _units(xs):
    out = []
    for x in xs:
        if x is None:
            continue
        if isinstance(x, U):
            out.append(x)
        elif isinstance(x, (list, tuple)):
            out.extend(_units(x))
        else:
            out.extend(x.u)
    return out


class T:
    def __init__(self, ap, units):
        self.ap = ap
        self.u = units

    def __getitem__(self, k):
        return self.ap[k]


class Arena:
    def __init__(self, nc, nbytes):
        self.t = nc.alloc_sbuf_tensor("arena", [128, nbytes // 4], F32)
        self.top = 0
        self.nbytes = nbytes
        self.recs = []

    def alloc(self, free_shape, dtype):
        esz = 2 if dtype == BF16 else 4
        n = int(np.prod(free_shape))
        nb = (n * esz + 63) // 64 * 64
        off = self.top
        assert off + nb <= self.nbytes, ("arena overflow", off, nb, self.nbytes)
        self.top = off + nb
        ap = self.t[:, off // 4:(off + nb) // 4]
        if esz == 2:
            ap = ap.bitcast(BF16)
        ap = ap[:, 0:n]
        if len(free_shape) == 2:
            ap = ap.rearrange("p (a b) -> p a b", a=free_shape[0])
        elif len(free_shape) == 3:
            ap = ap.rearrange("p (a b c) -> p a b c", a=free_shape[0], b=free_shape[1])
        u = U()
        end = off + nb
        keep = []
        for (o, e, uo) in self.recs:
            if o < end and off < e:
                if uo.w is not None:
                    u.r[("inh", uo.w)] = uo.w
                for v in uo.r.values():
                    u.r[("inh", v)] = v
                if off <= o and e <= end:
                    continue
            keep.append((o, e, uo))
        keep.append((off, end, u))
        self.recs = keep
        return T(ap, [u])

    def mark(self):
        return self.top

    def release(self, m):
        self.top = m


class DT:
    def __init__(self, ap):
        self.ap = ap
        self.us = {}

    def u(self, key=0):
        if key not in self.us:
            self.us[key] = U()
        return self.us[key]

    def __getitem__(self, k):
        return self.ap[k]


class Ctx:
    def __init__(self):
        self.nc = bass.Bass("TRN2", target_bir_lowering=False)
        self.sch = Sched(self.nc)
        self.arena = Arena(self.nc, 207 * 1024)
        self.banks = []
        for i in range(8):
            t = self.nc.alloc_psum_tensor("pb%d" % i, [128, 512], F32)
            self.banks.append(T(t[:, :], [U("pb%d" % i, excl=True)]))
        self.bi = 0
        self.rot = [0, 1, 2, 3, 4, 5]
        self.ins = {}
        self.nscr = 0

    def bank(self):
        b = self.banks[self.rot[self.bi % len(self.rot)]]
        self.bi += 1
        return b

    def dram_in(self, name, shape, dt=F32):
        t = self.nc.dram_tensor(name, list(shape), dt, kind="ExternalInput")
        d = DT(t.ap())
        self.ins[name] = d
        return d

    def dram_out(self, name, shape, dt=F32):
        t = self.nc.dram_tensor(name, list(shape), dt, kind="ExternalOutput")
        return DT(t.ap())

    def scratch(self, shape, dt=F32, name=None):
        self.nscr += 1
        t = self.nc.dram_tensor(name or ("scr%d" % self.nscr), list(shape), dt, kind="Internal")
        return DT(t.ap())

    def dma(self, out, in_, reads, writes, q="sp", **kw):
        self.sch.add(q, lambda e: e.dma_start(out=out, in_=in_, **kw), _units(reads), _units(writes), dma=True)

    def mm(self, out, lhsT, rhs, start, stop, reads, writes):
        self.sch.add("pe", lambda e: e.matmul(out, lhsT, rhs, start=start, stop=stop), _units(reads), _units(writes))

    def tr(self, out, in_, ident, reads, writes):
        self.sch.add("pe", lambda e: e.transpose(out, in_, ident), _units(reads), _units(writes))

    def op(self, eng, fn, reads, writes):
        self.sch.add(eng, fn, _units(reads), _units(writes))

    def act(self, out, in_, func, reads, writes, eng="act", **kw):
        self.sch.add("act", lambda e: e.activation(out=out, in_=in_, func=func, **kw), _units(reads), _units(writes))

    def copy(self, eng, out, in_, reads, writes):
        if eng == "act":
            self.sch.add("act", lambda e: e.copy(out=out, in_=in_), _units(reads), _units(writes))
        else:
            self.sch.add(eng, lambda e: e.tensor_copy(out=out, in_=in_), _units(reads), _units(writes))

    def tt(self, eng, out, in0, in1, op, reads, writes):
        self.sch.add(eng, lambda e: e.tensor_tensor(out=out, in0=in0, in1=in1, op=op), _units(reads), _units(writes))

    def ts(self, eng, out, in0, s1, s2, op0, op1, reads, writes, **kw):
        if op1 is None:
            self.sch.add(eng, lambda e: e.tensor_scalar(out=out, in0=in0, scalar1=s1, scalar2=s2, op0=op0, **kw),
                         _units(reads), _units(writes))
        else:
            self.sch.add(eng, lambda e: e.tensor_scalar(out=out, in0=in0, scalar1=s1, scalar2=s2, op0=op0, op1=op1, **kw),
                         _units(reads), _units(writes))

    def stt(self, eng, out, in0, scalar, in1, op0, op1, reads, writes):
        self.sch.add(eng, lambda e: e.scalar_tensor_tensor(out=out, in0=in0, scalar=scalar, in1=in1, op0=op0, op1=op1),
                     _units(reads), _units(writes))


WSHAPES = {
    "w_ada": (DEPTH, D, 6 * D), "b_ada": (DEPTH, 6 * D), "norm1_g": (DEPTH, D), "norm2_g": (DEPTH, D),
    "w_in": (DEPTH, D, IN_COLS), "q_norm_g": (DEPTH, 128), "k_norm_g": (DEPTH, 128),
    "mla_q_norm_g": (DEPTH, 512), "w_uq": (DEPTH, 512, 768), "mla_kv_norm_g": (DEPTH, 256),
    "w_ukv": (DEPTH, 256, 1024), "conv_w": (DEPTH, 5, 1280), "conv_b": (DEPTH, 1280),
    "dt_bias": (DEPTH, 2, 12), "a_log": (DEPTH, 2, 12), "d_skip": (DEPTH, 12), "ssd_norm_g": (DEPTH, 768),
    "w_out": (DEPTH, D, D), "w_gate_up": (DEPTH, D, 2 * FFN), "w_down": (DEPTH, FFN, D),
    "final_norm_g": (D,),
}


def host_consts():
    import ml_dtypes
    c = {}
    c["ident"] = np.eye(128, dtype=np.float32).astype(ml_dtypes.bfloat16)
    c["identf"] = np.eye(128, dtype=np.float32)
    pos = np.arange(S)
    row = (pos // 64).astype(np.float32)
    col = (pos % 64).astype(np.float32)

    def tab(rot_dim):
        axis_dim = rot_dim // 2
        inv = np.power(np.float32(10000.0), -np.arange(0, axis_dim, 2, dtype=np.float32) / np.float32(axis_dim)).astype(np.float32)
        ar = (row[:, None] * inv[None, :]).astype(np.float32)
        ac = (col[:, None] * inv[None, :]).astype(np.float32)
        cr, sr, cc_, sc = np.cos(ar), np.sin(ar), np.cos(ac), np.sin(ac)
        cosx = np.concatenate([cr, cr, cc_, cc_], axis=1)
        sinx = np.concatenate([-sr, sr, -sc, sc], axis=1)
        return np.stack([cosx, sinx], axis=1).astype(np.float32)

    c["ropeA"] = tab(128)
    c["ropeB"] = tab(64)
    k = np.arange(128)
    c["triF"] = (k[:, None] <= k[None, :]).astype(np.float32)
    c["triR"] = (k[:, None] >= k[None, :]).astype(np.float32)
    sel = np.zeros((128, 24, 128), np.float32)
    for j in range(24):
        sel[j, j, :] = 1.0
    c["sel"] = sel
    mf = np.where(k[None, :] >= k[:, None], 0.0, -30000.0).astype(np.float32)
    mb = np.where(k[None, :] <= k[:, None], 0.0, -30000.0).astype(np.float32)
    c["maskF"] = np.tile(mf, (1, 3)).astype(ml_dtypes.bfloat16)
    c["maskB"] = np.tile(mb, (1, 3)).astype(ml_dtypes.bfloat16)
    return c


def layout_conv(conv_w, conv_b):
    L = conv_w.shape[0]
    o = np.zeros((L, 128, 10, 6), np.float32)
    o[:, :, :, 0:5] = conv_w.reshape(L, 5, 10, 128).transpose(0, 3, 2, 1)
    o[:, :, :, 5] = conv_b.reshape(L, 10, 128).transpose(0, 2, 1)
    return o


class Prog:
    def __init__(self, nseq=2, nlayer=DEPTH, dbg=()):
        self.c = Ctx()
        self.nseq = nseq
        self.nlayer = nlayer
        self.dbg = set(dbg)
        self.dbg_outs = {}
        c = self.c
        self.x = c.dram_in("x", (2, S, D))
        self.cT = c.dram_in("cT", (128, 32))
        self.w = {k: c.dram_in(k, v) for k, v in WSHAPES.items()}
        self.k_ident = c.dram_in("ident", (128, 128), BF16)
        self.k_identf = c.dram_in("identf", (128, 128))
        self.k_ropeA = c.dram_in("ropeA", (S, 2, 128))
        self.k_ropeB = c.dram_in("ropeB", (S, 2, 64))
        self.k_triF = c.dram_in("triF", (128, 128))
        self.k_triR = c.dram_in("triR", (128, 128))
        self.k_sel = c.dram_in("sel", (128, 24, 128))
        self.k_maskF = c.dram_in("maskF", (128, 384), BF16)
        self.k_maskB = c.dram_in("maskB", (128, 384), BF16)
        self.k_convT = c.dram_in("convT", (DEPTH, 128, 10, 6))
        self.out = c.dram_out("out", (2, S, D))

    def scr(self, name, shape, dt=F32):
        if name in self.dbg and name not in self.dbg_outs:
            return self.dbg_out(name, shape, dt)
        return self.c.scratch(shape, dt)

    def dump_mix(self, mixT, n=16):
        if "mix" in self.dbg and "mix" not in self.dbg_outs:
            d = self.dbg_out("mix", (16, 128, S), BF16)
            for k in range(n):
                self.c.dma(d.ap[k], mixT[k][:, :], [mixT[k]], [d.u(k)])

    def dbg_out(self, name, shape, dt=F32):
        d = self.c.dram_out("dbg_" + name, shape, dt)
        self.dbg_outs[name] = d
        return d

    def consts(self):
        c = self.c
        a = c.arena
        self.ident = a.alloc([128], BF16)
        c.dma(self.ident[:, :], self.k_ident[:, :], [], [self.ident])
        self.identf = a.alloc([128], F32)
        c.dma(self.identf[:, :], self.k_identf[:, :], [], [self.identf])
        self.eps = a.alloc([1], F32)
        self.one1 = a.alloc([1], F32)
        c.op("pool", lambda e: e.memset(self.one1[:, :], 1.0), [], [self.one1])
        self.ones = a.alloc([128], BF16)
        c.op("pool", lambda e: e.memset(self.ones[:, :], 1.0), [], [self.ones])
        c.op("pool", lambda e: e.memset(self.eps[:, :], EPS), [], [self.eps])

    def mod_phase(self):
        c = self.c
        a = c.arena
        m = a.mark()
        self.mod = c.scratch((DEPTH, 2, 6 * D), name="mod")
        ct = a.alloc([32], F32)
        c.dma(ct[:, :], self.cT[:, :], [], [ct])
        sg = a.alloc([32], F32)
        c.act(sg[:, :], ct[:, :], AF.Sigmoid, [ct], [sg])
        c.tt("dve", ct[:, :], ct[:, :], sg[:, :], ALU.mult, [ct, sg], [ct])
        ctp = a.alloc([16, 128], F32)
        c.op("pool", lambda e: e.memset(ctp[:, :, :], 0.0), [], [ctp])
        c.copy("dve", ctp[:, :, 0:2], ct[:, :].rearrange("p (k b) -> p k b", b=2), [ct, ctp], [ctp])
        wb = [a.alloc([16, 512], F32) for _ in range(2)]
        bb = [a.alloc([512], F32) for _ in range(2)]
        ob = [a.alloc([512], F32) for _ in range(2)]
        NB = 6 * D // 512
        i = 0
        for l in range(self.nlayer):
            wv = self.w["w_ada"].ap[l].rearrange("(k p) n -> p k n", p=128)
            for nb in range(NB):
                w_t, b_t, o_t = wb[i % 2], bb[i % 2], ob[i % 2]
                i += 1
                cs = slice(nb * 512, (nb + 1) * 512)
                c.dma(w_t[:, :, :], wv[:, :, cs], [], [w_t])
                c.dma(b_t[0:2, :], self.w["b_ada"].ap[l:l + 1, cs].to_broadcast([2, 512]), [], [b_t])
                pb = c.bank()
                for k in range(16):
                    c.mm(pb[:, :], ctp[:, k, :], w_t[:, k, :], k == 0, k == 15, [ctp, w_t], [pb])
                c.tt("dve", o_t[0:2, :], pb[0:2, :], b_t[0:2, :], ALU.add, [pb, b_t], [o_t])
                c.dma(self.mod.ap[l, :, cs], o_t[0:2, :], [o_t], [self.mod.u((l, nb))])
        a.release(m)

    def load_bc(self, tile, l, b, j):
        c = self.c
        src = self.mod.ap[l, b:b + 1, j * D:(j + 1) * D].to_broadcast([128, D])
        c.dma(tile[:, :], src, [self.mod.u((l, 4 * j + q)) for q in range(4)], [tile])

    def load_vec_bc(self, tile, dram_ap_row, n):
        c = self.c
        c.dma(tile[:, 0:n], dram_ap_row.to_broadcast([128, n]), [], [tile])

    def bf_bank(self, pb, n):
        return pb.ap.bitcast(BF16).rearrange("p (a b) -> p a b", a=8)[:, 0:n, :]

    def rstd_from_ss(self, rstd, ss, n):
        c = self.c
        c.ts("dve", rstd, ss, 1.0 / n, EPS, ALU.mult, ALU.add, [], [])

    def norm_phase(self, b, l, xsrc, xkey, gname, jscale, jshift, hT, dst=None, after=None):
        c = self.c
        a = c.arena
        m = a.mark()
        G = a.alloc([D], F32)
        SH = a.alloc([D], F32)
        tmpg = a.alloc([D], F32)
        self.load_bc(G, l, b, jscale)
        self.load_vec_bc(tmpg, self.w[gname].ap[l:l + 1, :], D)
        c.stt("dve", G[:, :], G[:, :], 1.0, tmpg[:, :], ALU.add, ALU.mult, [G, tmpg], [G])
        self.load_bc(SH, l, b, jshift)
        xb = [a.alloc([D], F32) for _ in range(3)]
        junk = a.alloc([D], BF16)
        hb = [a.alloc([D], BF16) for _ in range(2)]
        st = [a.alloc([4], F32) for _ in range(2)]

        def load(t):
            c.dma(xb[t % 3][:, :], xsrc[t * 128:(t + 1) * 128, :], [xkey(t)], [xb[t % 3]])

        def stage_a(t):
            xt, ht, s_ = xb[t % 3], hb[t % 2], st[t % 2]
            c.act(junk[:, :], xt[:, :], AF.Square, [xt], [junk, s_], accum_out=s_[:, 0:1])
            c.act(s_[:, 1:2], s_[:, 0:1], AF.Sqrt, [s_, self.eps], [s_], scale=1.0 / D, bias=self.eps[:, 0:1])
            c.op("dve", lambda e, s_=s_: e.reciprocal(out=s_[:, 2:3], in_=s_[:, 1:2]), [s_], [s_])
            c.stt("dve", xt[:, :], xt[:, :], s_[:, 2:3], G[:, :], ALU.mult, ALU.mult, [xt, s_, G], [xt])
            c.tt("dve", ht[:, :], xt[:, :], SH[:, :], ALU.add, [xt, SH], [ht])

        def stage_b(t):
            ht = hb[t % 2]
            for half in range(2):
                pb = c.bank()
                pv = self.bf_bank(pb, 8)
                for k in range(8):
                    kc = half * 8 + k
                    c.tr(pv[:, k, :], ht[:, kc * 128:(kc + 1) * 128], self.ident[:, :], [ht, self.ident], [pb])
                if dst is None:
                    c.copy("act" if half == 0 else "dve", hT[t][:, half * 8:(half + 1) * 8, :], pv, [pb], [hT[t]])
                else:
                    dt_, off = dst(t)
                    c.copy("act" if half == 0 else "dve", dt_[:, half * 8:(half + 1) * 8, off:off + 128], pv, [pb], [dt_])
            if after is not None:
                after(t)

        load(0)
        load(1)
        for t in range(NT):
            if t + 2 < NT:
                load(t + 2)
            stage_a(t)
            if t >= 1:
                stage_b(t - 1)
        stage_b(NT - 1)
        a.release(m)

    def load_w_bf16(self, wt, src):
        self.c.dma(wt, src, [], [], q="pool")

    def in_proj_tm(self, b, l, hT, proj):
        c = self.c
        a = c.arena
        m = a.mark()
        wv = self.w["w_in"].ap[l].rearrange("(k p) n -> p k n", p=128)
        blocks = [(j * 512, j * 512, 512) for j in range(8)] + [(4096, 4096, 88)]
        wb = [a.alloc([16, 512], BF16) for _ in range(2)]
        ob = [a.alloc([512], F32) for _ in range(3)]
        i = 0
        for bi, (wc, pc, n) in enumerate(blocks):
            w_t = wb[bi % 2]
            c.dma(w_t[:, :, 0:n], wv[:, :, wc:wc + n], [], [w_t], q="pool")
            for t in range(NT):
                pb = c.bank()
                for k in range(16):
                    c.mm(pb[:, 0:n], hT[t][:, k, :], w_t[:, k, 0:n], k == 0, k == 15, [hT[t], w_t], [pb])
                o_t = ob[i % 3]
                i += 1
                c.copy("act" if i % 2 else "dve", o_t[:, 0:n], pb[:, 0:n], [pb], [o_t])
                c.dma(proj.ap[t * 128:(t + 1) * 128, pc:pc + n], o_t[:, 0:n], [o_t], [proj.u((t, bi))])
        a.release(m)

    def rms_heads(self, x3, nh, hd, ss, junk3, eng="dve"):
        pass

    def rope_tm(self, x2d, tab, nh, hd, T1, T2, SW, out2d, units_r, units_w, out3=False):
        c = self.c
        q = hd // 4
        n = nh * hd
        x3 = x2d.rearrange("p (h d) -> p h d", h=nh)
        xq = x2d.rearrange("p (g b f) -> p g b f", b=2, f=q)
        swq = SW[:, 0:n].rearrange("p (g b f) -> p g b f", b=2, f=q)
        sw3 = SW[:, 0:n].rearrange("p (h d) -> p h d", h=nh)
        t13 = T1[:, 0:n].rearrange("p (h d) -> p h d", h=nh)
        t23 = T2[:, 0:n].rearrange("p (h d) -> p h d", h=nh)
        cosb = tab[:, 0:1, :].to_broadcast([128, nh, hd])
        sinb = tab[:, 1:2, :].to_broadcast([128, nh, hd])
        c.copy("act", swq[:, :, 0, :], xq[:, :, 1, :], units_r, [SW])
        c.copy("act", swq[:, :, 1, :], xq[:, :, 0, :], units_r, [SW])
        c.tt("dve", t13, x3, cosb, ALU.mult, units_r, [T1])
        c.tt("dve", t23, sw3, sinb, ALU.mult, [SW] + units_r, [T2])
        if out3:
            c.tt("dve", out2d, t13, t23, ALU.add, [T1, T2], units_w)
        else:
            c.tt("dve", out2d, T1[:, 0:n], T2[:, 0:n], ALU.add, [T1, T2], units_w)

    def attn_core(self, heads, mixT, scale):
        c = self.c
        a = c.arena
        m = a.mark()
        LOOK = 2
        pt = [a.alloc([512], BF16) for _ in range(4)]
        rec = [a.alloc([512], F32) for _ in range(2)]
        old_rot = c.rot
        c.rot = [0, 1, 2, 3]
        accs = [(c.banks[4], c.banks[5]), (c.banks[6], c.banks[7])]
        iters = [(hd, qt, kc) for hd in heads for qt in range(4) for kc in range(16)]
        sbs = {}

        def emit_s(i):
            hd, qt, kc = iters[i]
            sb = c.bank()
            np_ = len(hd["kparts"])
            for pi, (kf, qf) in enumerate(hd["kparts"]):
                c.mm(sb[:, :], kf(kc), qf(qt), pi == 0, pi == np_ - 1, hd["reads"], [sb])
            sbs[i] = sb

        def emit_rest(i):
            hd, qt, kc = iters[i]
            sb = sbs.pop(i)
            accO, accD = accs[(i // 16) % 2]
            p_t = pt[i % 4]
            c.act(p_t[:, :], sb[:, :], AF.Exp, [sb], [p_t], scale=scale)
            c.mm(accO[:, :], hd["v_fn"](kc), p_t[:, :], kc == 0, kc == 15, [p_t] + hd["reads"], [accO])
            c.mm(accD[:, :], self.ones[:, :], p_t[:, :], kc == 0, kc == 15, [p_t, self.ones], [accD])
            if kc == 15:
                r_t = rec[(i // 16) % 2]
                c.op("dve", lambda e, r_t=r_t, accD=accD: e.reciprocal(out=r_t[:, :], in_=accD[:, :]), [accD], [r_t])
                mo = mixT[hd["out"]]
                c.tt("dve", mo[:, qt * 512:(qt + 1) * 512], accO[:, :], r_t[:, :], ALU.mult, [accO, r_t], [mo])

        n = len(iters)
        for i in range(n + LOOK):
            if i < n:
                emit_s(i)
            if i - LOOK >= 0:
                emit_rest(i - LOOK)
        c.rot = old_rot
        a.release(m)

    def gqa_phase(self, b, l, proj, mixT):
        c = self.c
        a = c.arena
        m = a.mark()
        qk = a.alloc([8, S], BF16)
        vv = a.alloc([NT, 256], BF16)
        m2 = a.mark()
        gain = a.alloc([8, 128], F32)
        for h in range(8):
            nm = "q_norm_g" if h < 6 else "k_norm_g"
            c.dma(gain[:, h, :], self.w[nm].ap[l:l + 1, :].to_broadcast([128, 128]), [], [gain])
        xb = [a.alloc([1280], F32) for _ in range(2)]
        sqs = [a.alloc([1024], F32) for _ in range(2)]
        st = [a.alloc([32], F32) for _ in range(2)]
        T1s = [a.alloc([1024], F32) for _ in range(2)]
        T2s = [a.alloc([1024], F32) for _ in range(2)]
        SWs = [a.alloc([1024], F32) for _ in range(2)]
        ob = [a.alloc([1024], BF16) for _ in range(2)]
        tb = [a.alloc([256], F32) for _ in range(2)]
        for t in range(NT):
            xt, s_, o_t, tab = xb[t % 2], st[t % 2], ob[t % 2], tb[t % 2]
            sq, T1, T2, SW = sqs[t % 2], T1s[t % 2], T2s[t % 2], SWs[t % 2]
            c.dma(xt[:, :], proj.ap[t * 128:(t + 1) * 128, 0:1280], [proj.u((t, j)) for j in range(3)], [xt])
            c.dma(tab[:, :], self.k_ropeA.ap[t * 128:(t + 1) * 128].rearrange("p a f -> p (a f)"), [], [tab])
            x3 = xt[:, 0:1024].rearrange("p (h d) -> p h d", h=8)
            c.act(sq[:, :], xt[:, 0:1024], AF.Square, [xt], [sq])
            c.op("dve", lambda e, s_=s_, sq=sq: e.tensor_reduce(out=s_[:, 0:8], in_=sq[:, :].rearrange("p (h d) -> p h d", h=8),
                                                         axis=AX.X, op=ALU.add), [sq], [s_])
            c.act(s_[:, 8:16], s_[:, 0:8], AF.Sqrt, [s_, self.eps], [s_], scale=1.0 / 128, bias=self.eps[:, 0:1])
            c.op("dve", lambda e, s_=s_: e.reciprocal(out=s_[:, 16:24], in_=s_[:, 8:16]), [s_], [s_])
            c.tt("dve", x3, x3, s_[:, 16:24].unsqueeze(2).to_broadcast([128, 8, 128]), ALU.mult, [xt, s_], [xt])
            c.tt("dve", x3, x3, gain[:, :, :], ALU.mult, [xt, gain], [xt])
            self.rope_tm(xt[:, 0:1024], tab[:, :].rearrange("p (a f) -> p a f", a=2), 8, 128, T1, T2, SW, o_t[:, :], [xt, tab], [o_t])
            c.copy("act", vv[:, t, :], xt[:, 1024:1280], [xt], [vv])
            pb = c.bank()
            pv = self.bf_bank(pb, 8)
            for h in range(8):
                c.tr(pv[:, h, :], o_t[:, h * 128:(h + 1) * 128], self.ident[:, :], [o_t, self.ident], [pb])
            c.copy("act", qk[:, :, t * 128:(t + 1) * 128], pv, [pb], [qk])
        a.release(m2)
        heads = []
        for g in range(2):
            for r in range(3):
                h = g * 3 + r
                heads.append(dict(
                    kparts=[(lambda kc, g=g: qk[:, 6 + g, kc * 128:(kc + 1) * 128],
                             lambda qt, h=h: qk[:, h, qt * 512:(qt + 1) * 512])],
                    v_fn=lambda kc, g=g: vv[:, kc, g * 128:(g + 1) * 128],
                    reads=[qk, vv], out=h))
        self.attn_core(heads, mixT, 128 ** -0.5)
        a.release(m)

    def mla_phase(self, b, l, proj, mixT):
        c = self.c
        a = c.arena
        m = a.mark()
        qn = a.alloc([4, S], BF16)
        qp = a.alloc([4, S], BF16)
        kn = a.alloc([4, S], BF16)
        kp = a.alloc([S], BF16)
        vb = a.alloc([NT, 512], BF16)
        m2 = a.mark()
        cqT = a.alloc([4, S], BF16)
        ckvT = a.alloc([2, S], BF16)
        wuq = a.alloc([4, 768], BF16)
        wukv = a.alloc([2, 1024], BF16)
        c.dma(wuq[:, :, :], self.w["w_uq"].ap[l].rearrange("(k p) n -> p k n", p=128), [], [wuq], q="pool")
        c.dma(wukv[:, :, :], self.w["w_ukv"].ap[l].rearrange("(k p) n -> p k n", p=128), [], [wukv], q="pool")
        gq = a.alloc([512], F32)
        gkv = a.alloc([256], F32)
        self.load_vec_bc(gq, self.w["mla_q_norm_g"].ap[l:l + 1, :], 512)
        self.load_vec_bc(gkv, self.w["mla_kv_norm_g"].ap[l:l + 1, :], 256)
        xb = [a.alloc([832], F32) for _ in range(2)]
        junks = [a.alloc([512], BF16) for _ in range(2)]
        st = [a.alloc([8], F32) for _ in range(2)]
        TK = [[a.alloc([64], F32) for _ in range(3)] for _ in range(2)]
        TQ = [[a.alloc([256], F32) for _ in range(3)] for _ in range(2)]
        ob = [a.alloc([896], BF16) for _ in range(2)]
        tb = [a.alloc([128], F32) for _ in range(2)]
        qpb = [a.alloc([256], F32) for _ in range(2)]
        qpo = [a.alloc([4, 128], BF16) for _ in range(2)]
        for o_ in ob:
            c.op("pool", lambda e, o_=o_: e.memset(o_[:, 832:896], 0.0), [o_], [o_])
        for o_ in qpo:
            c.op("pool", lambda e, o_=o_: e.memset(o_[:, :, :], 0.0), [o_], [o_])
        wqpe = a.alloc([4, 256], BF16)
        wv = a.alloc([2, 512], BF16)
        for h in range(4):
            c.copy("dve", wqpe[:, :, h * 64:(h + 1) * 64], wuq[:, :, h * 192 + 128:h * 192 + 192], [wuq, wqpe], [wqpe])
            c.copy("dve", wv[:, :, h * 128:(h + 1) * 128], wukv[:, :, h * 256 + 128:h * 256 + 256], [wukv, wv], [wv])
        for t in range(NT):
            xt, s_, o_t, tab = xb[t % 2], st[t % 2], ob[t % 2], tb[t % 2]
            junk = junks[t % 2]
            c.dma(xt[:, :], proj.ap[t * 128:(t + 1) * 128, 1280:2112], [proj.u((t, j)) for j in (2, 3, 4)], [xt])
            c.dma(tab[:, :], self.k_ropeB.ap[t * 128:(t + 1) * 128].rearrange("p a f -> p (a f)"), [], [tab])
            c.act(junk[:, 0:512], xt[:, 0:512], AF.Square, [xt], [junk, s_], accum_out=s_[:, 0:1])
            c.act(junk[:, 0:256], xt[:, 512:768], AF.Square, [xt], [junk, s_], accum_out=s_[:, 1:2])
            c.ts("dve", s_[:, 0:1], s_[:, 0:1], 0.5, None, ALU.mult, None, [s_], [s_])
            c.act(s_[:, 2:4], s_[:, 0:2], AF.Sqrt, [s_, self.eps], [s_], scale=1.0 / 256, bias=self.eps[:, 0:1])
            c.op("dve", lambda e, s_=s_: e.reciprocal(out=s_[:, 4:6], in_=s_[:, 2:4]), [s_], [s_])
            c.stt("dve", o_t[:, 0:512], xt[:, 0:512], s_[:, 4:5], gq[:, 0:512], ALU.mult, ALU.mult, [xt, s_, gq], [o_t])
            c.stt("dve", o_t[:, 512:768], xt[:, 512:768], s_[:, 5:6], gkv[:, 0:256], ALU.mult, ALU.mult, [xt, s_, gkv], [o_t])
            tabv = tab[:, :].rearrange("p (a f) -> p a f", a=2)
            T1, T2, SW = TK[t % 2]
            self.rope_tm(xt[:, 768:832], tabv, 1, 64, T1, T2, SW, o_t[:, 768:832], [xt, tab], [o_t])
            pb = c.bank()
            pv = self.bf_bank(pb, 8)
            for k in range(6):
                c.tr(pv[:, k, :], o_t[:, k * 128:(k + 1) * 128], self.ident[:, :], [o_t, self.ident], [pb])
            c.tr(pv[:, 6, :], o_t[:, 768:896], self.ident[:, :], [o_t, self.ident], [pb])
            tc_ = slice(t * 128, (t + 1) * 128)
            c.copy("act", cqT[:, :, tc_], pv[:, 0:4, :], [pb], [cqT])
            c.copy("dve", ckvT[:, :, tc_], pv[:, 4:6, :], [pb], [ckvT])
            c.copy("act", kp[:, tc_], pv[:, 6, :], [pb], [kp])
            pq = c.bank()
            for k in range(4):
                c.mm(pq[:, 0:256], cqT[:, k, tc_], wqpe[:, k, :], k == 0, k == 3, [cqT, wqpe], [pq])
            qb_, qo_ = qpb[t % 2], qpo[t % 2]
            c.copy("act", qb_[:, :], pq[:, 0:256], [pq], [qb_])
            T1, T2, SW = TQ[t % 2]
            self.rope_tm(qb_[:, :], tabv, 4, 64, T1, T2, SW, qo_[:, :, 0:64], [qb_, tab], [qo_], out3=True)
            pb2 = c.bank()
            pv2 = self.bf_bank(pb2, 8)
            for h in range(4):
                c.tr(pv2[:, h, :], qo_[:, h, :], self.ident[:, :], [qo_, self.ident], [pb2])
            c.copy("dve", qp[:, :, tc_], pv2[:, 0:4, :], [pb2], [qp])
            pvb = c.bank()
            for k in range(2):
                c.mm(pvb[:, :], ckvT[:, k, tc_], wv[:, k, :], k == 0, k == 1, [ckvT, wv], [pvb])
            c.copy("act", vb[:, t, :], pvb[:, :], [pvb], [vb])
        i = 0
        for h in range(4):
            for tq in range(4):
                ts_ = slice(tq * 512, (tq + 1) * 512)
                p1 = c.bank()
                for k in range(4):
                    c.mm(p1[:, :], wuq[:, k, h * 192:h * 192 + 128], cqT[:, k, ts_], k == 0, k == 3, [wuq, cqT], [p1])
                c.copy("act", qn[:, h, ts_], p1[:, :], [p1], [qn])
                p2 = c.bank()
                for k in range(2):
                    c.mm(p2[:, :], wukv[:, k, h * 256:h * 256 + 128], ckvT[:, k, ts_], k == 0, k == 1, [wukv, ckvT], [p2])
                c.copy("dve", kn[:, h, ts_], p2[:, :], [p2], [kn])
        a.release(m2)
        heads = []
        for h in range(4):
            heads.append(dict(
                kparts=[(lambda kc, h=h: kn[:, h, kc * 128:(kc + 1) * 128], lambda qt, h=h: qn[:, h, qt * 512:(qt + 1) * 512]),
                        (lambda kc: kp[:, kc * 128:(kc + 1) * 128], lambda qt, h=h: qp[:, h, qt * 512:(qt + 1) * 512])],
                v_fn=lambda kc, h=h: vb[:, kc, h * 128:(h + 1) * 128],
                reads=[qn, qp, kn, kp, vb], out=6 + h))
        self.attn_core(heads, mixT, 192 ** -0.5)
        a.release(m)

    def ssd_consts(self):
        c = self.c
        a = c.arena
        self.triF = a.alloc([128], F32)
        self.triR = a.alloc([128], F32)
        self.onesf = a.alloc([128], F32)
        self.sel = a.alloc([24, 128], F32)
        self.maskF = a.alloc([384], BF16)
        self.maskB = a.alloc([384], BF16)
        c.dma(self.triF[:, :], self.k_triF[:, :], [], [self.triF])
        c.dma(self.triR[:, :], self.k_triR[:, :], [], [self.triR])
        c.op("pool", lambda e: e.memset(self.onesf[:, :], 1.0), [], [self.onesf])
        c.dma(self.sel[:, :, :], self.k_sel[:, :, :], [], [self.sel])
        c.dma(self.maskF[:, :], self.k_maskF[:, :], [], [self.maskF])
        c.dma(self.maskB[:, :], self.k_maskB[:, :], [], [self.maskB])

    def ssd_phase(self, b, l, proj, mixT):
        c = self.c
        a = c.arena
        m = a.mark()
        yf_d = c.scratch((S, 768))
        aneg = a.alloc([24], F32)
        dtb = a.alloc([24], F32)
        dsk = a.alloc([12], F32)
        ng = a.alloc([768], F32)
        cw = a.alloc([10, 6], F32)
        c.dma(aneg[:, :], self.w["a_log"].ap[l:l + 1].rearrange("o a h -> o (a h)").to_broadcast([128, 24]), [], [aneg])
        c.act(aneg[:, :], aneg[:, :], AF.Exp, [aneg], [aneg])
        c.ts("dve", aneg[:, :], aneg[:, :], -1.0, None, ALU.mult, None, [aneg], [aneg])
        c.dma(dtb[:, :], self.w["dt_bias"].ap[l:l + 1].rearrange("o a h -> o (a h)").to_broadcast([128, 24]), [], [dtb])
        c.dma(dsk[:, :], self.w["d_skip"].ap[l:l + 1, :].to_broadcast([128, 12]), [], [dsk])
        self.load_vec_bc(ng, self.w["ssd_norm_g"].ap[l:l + 1, :], 768)
        c.dma(cw[:, :, :], self.k_convT.ap[l], [], [cw])
        dt = a.alloc([NT, 24], F32)
        af = a.alloc([NT, 24], F32)
        ab = a.alloc([NT, 24], F32)
        c.op("pool", lambda e: e.memset(af[:, :, :], 0.0), [], [af])
        c.op("pool", lambda e: e.memset(ab[:, :, :], 0.0), [], [ab])
        for t in range(NT):
            c.dma(dt[:, t, :], proj.ap[t * 128:(t + 1) * 128, 4160:4184], [proj.u((t, 8))], [dt])
        c.tt("dve", dt[:, :, :], dt[:, :, :], dtb[:, :].unsqueeze(1).to_broadcast([128, NT, 24]), ALU.add, [dt, dtb], [dt])
        c.act(dt[:, :, :], dt[:, :, :], AF.Exp, [dt], [dt])
        c.act(dt[:, :, :], dt[:, :, :], AF.Ln, [dt, self.one1], [dt], bias=self.one1[:, 0:1])
        c.tt("dve", af[:, :, 0:12], dt[:, :, 0:12], aneg[:, 0:12].unsqueeze(1).to_broadcast([128, NT, 12]), ALU.mult, [dt, aneg, af], [af])
        c.tt("dve", ab[:, :, 12:24], dt[:, :, 12:24], aneg[:, 12:24].unsqueeze(1).to_broadcast([128, NT, 12]), ALU.mult, [dt, aneg, ab], [ab])
        bcT = a.alloc([4, S], BF16)
        xs_tm = a.alloc([NT, 768], BF16)
        b_tm = a.alloc([NT, 256], BF16)
        m2 = a.mark()
        xin = [a.alloc([8, 128], F32) for _ in range(2)]
        xT = [a.alloc([S + 4], F32) for _ in range(2)]
        acc = [a.alloc([S], F32)] * 2
        cvo = [a.alloc([S], BF16) for _ in range(2)]
        for xt_ in xT:
            c.op("pool", lambda e, xt_=xt_: e.memset(xt_[:, 0:2], 0.0), [xt_], [xt_])
            c.op("pool", lambda e, xt_=xt_: e.memset(xt_[:, S + 2:S + 4], 0.0), [xt_], [xt_])
        ii = 0
        for blk in range(10):
            x_t, ac, co = xT[blk % 2], acc[blk % 2], cvo[blk % 2]
            cs = 2880 + blk * 128
            for t8 in range(2):
                xi = xin[ii % 2]
                ii += 1
                c.dma(xi[:, :, :], proj.ap[t8 * 1024:(t8 + 1) * 1024, cs:cs + 128].rearrange("(t p) c -> p t c", p=128),
                      [proj.u((t, j)) for t in range(t8 * 8, t8 * 8 + 8) for j in (5, 6, 7, 8)], [xi])
                for t4 in range(2):
                    pb = c.bank()
                    for q in range(4):
                        c.mm(pb[:, q * 128:(q + 1) * 128], xi[:, t4 * 4 + q, :], self.identf[:, :], True, True, [xi, self.identf], [pb])
                    o0 = 2 + (t8 * 2 + t4) * 512
                    c.copy("act", x_t[:, o0:o0 + 512], pb[:, :], [pb], [x_t])
            c.ts("dve", ac[:, :], x_t[:, 0:S], cw[:, blk, 0:1], None, ALU.mult, None, [x_t, cw], [ac])
            for j in range(1, 5):
                c.stt("dve", ac[:, :], x_t[:, j:j + S], cw[:, blk, j:j + 1], ac[:, :], ALU.mult, ALU.add, [x_t, cw, ac], [ac])
            if blk < 6 or blk in (6, 7):
                c.act(co[:, :], ac[:, :], AF.Silu, [ac, cw], [co], bias=cw[:, blk, 5:6])
                for t8 in range(2):
                    pb = c.bank()
                    pv = self.bf_bank(pb, 8)
                    for q in range(8):
                        t = t8 * 8 + q
                        c.tr(pv[:, q, :], co[:, t * 128:(t + 1) * 128], self.ident[:, :], [co, self.ident], [pb])
                    if blk < 6:
                        c.copy("act" if t8 else "dve", xs_tm[:, t8 * 8:(t8 + 1) * 8, blk * 128:(blk + 1) * 128], pv, [pb], [xs_tm])
                    else:
                        c.copy("act" if t8 else "dve", b_tm[:, t8 * 8:(t8 + 1) * 8, (blk - 6) * 128:(blk - 5) * 128], pv, [pb], [b_tm])
            if blk >= 6:
                c.act(bcT[:, blk - 6, :], ac[:, :], AF.Silu, [ac, cw], [bcT], bias=cw[:, blk, 5:6])
        a.release(m2)
        H = [[a.alloc([384], F32) for _ in range(2)] for _ in range(2)]
        Hb = [[a.alloc([384], BF16) for _ in range(2)] for _ in range(2)]
        NS = 4
        apad = [a.alloc([128], F32) for _ in range(NS)]
        for ap_ in apad:
            c.op("pool", lambda e, ap_=ap_: e.memset(ap_[:, :], 0.0), [], [ap_])
        cs_ = [a.alloc([72], F32) for _ in range(NS)]
        ex = [a.alloc([72], F32) for _ in range(NS)]
        pfm = [a.alloc([256], F32) for _ in range(NS)]
        xd = [a.alloc([768], BF16) for _ in range(NS)]
        xdw = [a.alloc([768], BF16) for _ in range(2)] * 2
        xdw = [xdw[0], xdw[0], xdw[1], xdw[1]]
        yo = [a.alloc([768], F32) for _ in range(NS)]
        cbT = [[a.alloc([128], F32) for _ in range(2)] for _ in range(2)]
        eD = [[a.alloc([384], F32) for _ in range(2)] for _ in range(2)]
        wT = [[a.alloc([384], BF16) for _ in range(3)] for _ in range(2)]
        ytmp = [[a.alloc([384], F32)] * 2 for _ in range(2)]
        zb = [a.alloc([768], F32) for _ in range(2)]
        yfb = [a.alloc([768], F32) for _ in range(2)]
        yn = [a.alloc([768], BF16) for _ in range(2)]
        st = [a.alloc([8], F32) for _ in range(2)]
        junk = a.alloc([384], BF16)
        accA, accB = c.banks[6], c.banks[7]
        for d in range(2):
            for g in range(2):
                c.op("pool", lambda e, d=d, g=g: e.memset(H[d][g][:, :], 0.0), [H[d][g]], [H[d][g]])
                c.op("pool", lambda e, d=d, g=g: e.memset(Hb[d][g][:, :], 0.0), [Hb[d][g]], [Hb[d][g]])
        cnt = {"w": [0, 0], "f": 0}

        def chunk_step(d, ch, k, finalize):
            mask = self.maskF if d == 0 else self.maskB
            a_own = af if d == 0 else ab
            tri = self.triF if d == 0 else self.triR
            cst, ext, pf, xd_, xw_ = cs_[k], ex[k], pfm[k], xd[k], xdw[k]
            tc_ = slice(ch * 128, (ch + 1) * 128)
            d0 = d * 12
            ap_ = apad[k]
            c.copy("act", ap_[:, 0:24], a_own[:, ch, :], [a_own, ap_], [ap_])
            pb = c.bank()
            c.mm(pb[:, 0:24], tri[:, :], a_own[:, ch, 0:24], True, True, [tri, a_own], [pb])
            c.mm(pb[:, 24:48], self.onesf[:, :], a_own[:, ch, 0:24], True, True, [self.onesf, a_own], [pb])
            c.mm(pb[:, 128:256], ap_[:, :], tri[:, :], True, True, [tri, ap_], [pb])
            yield
            c.copy("dve", cst[:, 0:48], pb[:, 0:48], [pb], [cst])
            c.copy("act", pf[:, 0:128], pb[:, 128:256], [pb], [pf])
            c.ts("dve", pf[:, 128:256], pb[:, 128:256], -1.0, None, ALU.mult, None, [pb], [pf])
            c.tt("dve", cst[:, 48:72], cst[:, 24:48], cst[:, 0:24], ALU.subtract, [cst], [cst])
            c.act(ext[:, 0:24], cst[:, 0:24], AF.Exp, [cst], [ext])
            c.act(ext[:, 24:48], cst[:, 48:72], AF.Exp, [cst], [ext])
            c.act(ext[:, 48:72], cst[:, 24:48], AF.Exp, [cst], [ext])
            xs3 = xs_tm[:, ch, :].rearrange("p (h q) -> p h q", h=12)
            xd3 = xd_[:, :].rearrange("p (h q) -> p h q", h=12)
            xw3 = xw_[:, :].rearrange("p (h q) -> p h q", h=12)
            c.tt("pool", xd3, xs3, dt[:, ch, d0:d0 + 12].unsqueeze(2).to_broadcast([128, 12, 64]), ALU.mult, [xs_tm, dt], [xd_])
            c.tt("pool", xw3, xd3, ext[:, 24 + d0:36 + d0].unsqueeze(2).to_broadcast([128, 12, 64]), ALU.mult, [xd_, ext], [xw_])
            yield
            y_t = yo[k]
            for g in range(2):
                cb = cbT[d][g]
                Hg, Hbg = H[d][g], Hb[d][g]
                pcb = c.bank()
                c.mm(pcb[:, 0:128], bcT[:, g, tc_], bcT[:, 2 + g, tc_], True, True, [bcT], [pcb])
                yield
                c.copy("act", cb[:, :], pcb[:, 0:128], [pcb], [cb])
                accY = (accA if g == 0 else accB) if d == 0 else (c.banks[4] if g == 0 else c.banks[5])
                for half in range(2):
                    pd = c.bank()
                    for j3 in range(3):
                        hh = d0 + g * 6 + half * 3 + j3
                        reg = pd[:, j3 * 128:(j3 + 1) * 128]
                        c.mm(reg, self.sel[:, hh, :], pf[:, 0:128], j3 == 0, False, [self.sel, pf], [pd])
                        c.mm(reg, pf[:, 128:256], self.sel[:, hh, :], False, False, [self.sel, pf], [pd])
                    c.mm(pd[:, 0:384], self.ident[:, :], mask[:, :], False, True, [self.ident, mask], [pd])
                    yield
                    e_t = eD[d][half]
                    c.act(e_t[:, :], pd[:, 0:384], AF.Exp, [pd], [e_t])
                    w_t = wT[d][cnt["w"][d] % 3]
                    cnt["w"][d] += 1
                    c.tt("dve", w_t[:, :].rearrange("p (j s) -> p j s", j=3), e_t[:, :].rearrange("p (j s) -> p j s", j=3),
                         cb[:, :].unsqueeze(1).to_broadcast([128, 3, 128]), ALU.mult, [e_t, cb], [w_t])
                    yield
                    for j3 in range(3):
                        hl = g * 6 + half * 3 + j3
                        c.mm(accY[:, (half * 3 + j3) * 64:(half * 3 + j3 + 1) * 64], w_t[:, j3 * 128:(j3 + 1) * 128],
                             xd_[:, hl * 64:(hl + 1) * 64], True, True, [w_t, xd_], [accY])
                yield
                po = c.bank()
                c.mm(po[:, 0:384], bcT[:, 2 + g, tc_], Hbg[:, :], True, True, [bcT, Hbg], [po])
                yt = ytmp[d][g]
                c.tt("dve", yt[:, :].rearrange("p (h q) -> p h q", h=6), po[:, 0:384].rearrange("p (h q) -> p h q", h=6),
                     ext[:, d0 + g * 6:d0 + g * 6 + 6].unsqueeze(2).to_broadcast([128, 6, 64]), ALU.mult, [po, ext], [yt])
                c.tt("dve", y_t[:, g * 384:(g + 1) * 384], accY[:, 0:384], yt[:, :], ALU.add, [accY, yt], [y_t])
                ps_ = c.bank()
                c.mm(ps_[:, 0:384], b_tm[:, ch, g * 128:(g + 1) * 128], xw_[:, g * 384:(g + 1) * 384], True, True,
                     [b_tm, xw_], [ps_])
                yield
                c.tt("dve", Hg[:, :].rearrange("p (h q) -> p h q", h=6), Hg[:, :].rearrange("p (h q) -> p h q", h=6),
                     ext[:, 48 + d0 + g * 6:48 + d0 + g * 6 + 6].unsqueeze(2).to_broadcast([128, 6, 64]), ALU.mult,
                     [Hg, ext], [Hg])
                c.tt("dve", Hg[:, :], Hg[:, :], ps_[:, 0:384], ALU.add, [Hg, ps_], [Hg])
                c.copy("act", Hbg[:, :], Hg[:, :], [Hg], [Hbg])
                yield
            if not finalize:
                c.dma(yf_d.ap[tc_, :], y_t[:, :], [y_t], [yf_d.u(ch)])
                return
            kf = cnt["f"] % 2
            cnt["f"] += 1
            z_t, yf_t, yn_t, s_ = zb[kf], yfb[kf], yn[kf], st[kf]
            c.dma(yf_t[:, :], yf_d.ap[tc_, :], [yf_d.u(ch)], [yf_t])
            c.dma(z_t[:, :], proj.ap[tc_, 2112:2880], [proj.u((ch, 4)), proj.u((ch, 5))], [z_t])
            c.tt("dve", y_t[:, :], y_t[:, :], yf_t[:, :], ALU.add, [y_t, yf_t], [y_t])
            c.tt("pool", yf_t[:, :].rearrange("p (h q) -> p h q", h=12), xs_tm[:, ch, :].rearrange("p (h q) -> p h q", h=12),
                 dsk[:, :].unsqueeze(2).to_broadcast([128, 12, 64]), ALU.mult, [xs_tm, dsk], [yf_t])
            c.tt("dve", y_t[:, :], y_t[:, :], yf_t[:, :], ALU.add, [y_t, yf_t], [y_t])
            c.act(z_t[:, :], z_t[:, :], AF.Silu, [z_t], [z_t])
            c.tt("dve", y_t[:, :], y_t[:, :], z_t[:, :], ALU.mult, [y_t, z_t], [y_t])
            for g in range(2):
                c.act(junk[:, :], y_t[:, g * 384:(g + 1) * 384], AF.Square, [y_t], [junk, s_], accum_out=s_[:, g:g + 1])
            c.act(s_[:, 2:4], s_[:, 0:2], AF.Sqrt, [s_, self.eps], [s_], scale=1.0 / 384, bias=self.eps[:, 0:1])
            c.op("dve", lambda e, s_=s_: e.reciprocal(out=s_[:, 4:6], in_=s_[:, 2:4]), [s_], [s_])
            for g in range(2):
                c.stt("dve", yn_t[:, g * 384:(g + 1) * 384], y_t[:, g * 384:(g + 1) * 384], s_[:, 4 + g:5 + g],
                      ng[:, g * 384:(g + 1) * 384], ALU.mult, ALU.mult, [y_t, s_, ng], [yn_t])
            yield
            pb = c.bank()
            pv = self.bf_bank(pb, 8)
            for j in range(6):
                c.tr(pv[:, j, :], yn_t[:, j * 128:(j + 1) * 128], self.ident[:, :], [yn_t, self.ident], [pb])
            yield
            for j in range(6):
                c.copy("act" if j % 2 else "dve", mixT[10 + j][:, tc_], pv[:, j, :], [pb], [mixT[10 + j]])

        def run_pair(g0, g1):
            gens = [g0, g1]
            alive = [True, True]
            while alive[0] or alive[1]:
                for i_ in (0, 1):
                    if alive[i_]:
                        try:
                            next(gens[i_])
                        except StopIteration:
                            alive[i_] = False

        old_rot = c.rot
        c.rot = [0, 1, 2, 3]
        for step in range(NT):
            run_pair(chunk_step(0, step, (step % 2), step >= NT // 2),
                     chunk_step(1, NT - 1 - step, 2 + (step % 2), step >= NT // 2))
        c.rot = old_rot
        a.release(m)

    def out_phase(self, b, l, mixT, xsrc, xkey, x1):
        c = self.c
        a = c.arena
        m = a.mark()
        GT = a.alloc([D], F32)
        self.load_bc(GT, l, b, 2)
        wv = self.w["w_out"].ap[l].rearrange("(k p) n -> p k n", p=128)
        wb = [a.alloc([16, 512], BF16) for _ in range(2)]
        xb = [a.alloc([512], F32) for _ in range(4)]
        ob = [a.alloc([512], F32) for _ in range(3)]
        its = [(n, t) for n in range(4) for t in range(NT)]

        def load(i):
            n, t = its[i]
            c.dma(xb[i % 4][:, :], xsrc[t * 128:(t + 1) * 128, n * 512:(n + 1) * 512], [xkey(t)], [xb[i % 4]])

        load(0)
        load(1)
        for i, (n, t) in enumerate(its):
            if i + 2 < len(its):
                load(i + 2)
            w_t = wb[n % 2]
            ns = slice(n * 512, (n + 1) * 512)
            if t == 0:
                c.dma(w_t[:, :, :], wv[:, :, ns], [], [w_t], q="pool")
            x_t, o_t = xb[i % 4], ob[i % 3]
            ts_ = slice(t * 128, (t + 1) * 128)
            pb = c.bank()
            for k in range(16):
                c.mm(pb[:, :], mixT[k][:, ts_], w_t[:, k, :], k == 0, k == 15, [mixT[k], w_t], [pb])
            c.tt("dve", o_t[:, :], pb[:, :], GT[:, ns], ALU.mult, [pb, GT], [o_t])
            c.tt("dve", o_t[:, :], o_t[:, :], x_t[:, :], ALU.add, [o_t, x_t], [o_t])
            c.dma(x1.ap[ts_, ns], o_t[:, :], [o_t], [x1.u((t, n))])
        a.release(m)

    def norm2_phase(self, b, l, x1, h2T_d):
        c = self.c
        a = c.arena
        m = a.mark()
        stg = [a.alloc([16, 512], BF16) for _ in range(2)]

        def dst(t):
            return stg[(t // 4) % 2], (t % 4) * 128

        def after(t):
            if t % 4 == 3:
                s_ = stg[(t // 4) % 2]
                q = t // 4
                c.dma(h2T_d.ap[:, :, q * 512:(q + 1) * 512], s_[:, :, :], [s_], [h2T_d.u(q)])

        self.norm_phase(b, l, x1.ap, lambda t: [x1.u((t, n)) for n in range(4)], "norm2_g", 4, 3, None, dst=dst, after=after)
        a.release(m)

    def ffn_phase(self, b, l, x1, h2T_d, x2):
        c = self.c
        a = c.arena
        m = a.mark()
        TB = 1024
        wgu = self.w["w_gate_up"].ap[l].rearrange("(k p) n -> p k n", p=128)
        wdn = self.w["w_down"].ap[l].rearrange("(j p) n -> p j n", p=128)
        g2 = a.alloc([16], F32)
        g2row = a.alloc([128], F32)
        c.op("pool", lambda e: e.memset(g2row[:, :], 0.0), [], [g2row])
        c.dma(g2row[0:16, :], self.mod.ap[l, b, 5 * D:6 * D].rearrange("(k p) -> k p", p=128),
              [self.mod.u((l, 20 + q)) for q in range(4)] + [g2row], [g2row])
        pg2 = c.bank()
        c.mm(pg2[:, 0:16], g2row[:, :], self.identf[:, 0:16], True, True, [g2row, self.identf], [pg2])
        c.copy("dve", g2[:, :], pg2[:, 0:16], [pg2], [g2])
        actT = a.alloc([44, TB], BF16)
        sgb = [a.alloc([512], F32) for _ in range(2)]
        xb = [a.alloc([256], F32) for _ in range(4)]
        ob = [a.alloc([256], F32) for _ in range(2)]
        gtb = [a.alloc([512], F32) for _ in range(4)]
        for tb in range(S // TB):
            m2 = a.mark()
            h2 = a.alloc([16, TB], BF16)
            for q in range(TB // 512):
                qq = tb * (TB // 512) + q
                c.dma(h2[:, :, q * 512:(q + 1) * 512], h2T_d.ap[:, :, qq * 512:(qq + 1) * 512], [h2T_d.u(qq)], [h2])
            wg = [a.alloc([16, 256], BF16) for _ in range(3)]
            wu = [a.alloc([16, 256], BF16) for _ in range(3)]
            isg = 0
            for jg in range(22):
                g_t, u_t = wg[jg % 3], wu[jg % 3]
                c.dma(g_t[:, :, :], wgu[:, :, jg * 256:(jg + 1) * 256], [], [g_t], q="pool")
                c.dma(u_t[:, :, :], wgu[:, :, FFN + jg * 256:FFN + (jg + 1) * 256], [], [u_t], q="pool")
                for jj in range(2):
                    j = jg * 2 + jj
                    for tq in range(TB // 512):
                        tqs = slice(tq * 512, (tq + 1) * 512)
                        pg = c.bank()
                        for k in range(16):
                            c.mm(pg[:, :], g_t[:, k, jj * 128:(jj + 1) * 128], h2[:, k, tqs], k == 0, k == 15, [g_t, h2], [pg])
                        pu = c.bank()
                        for k in range(16):
                            c.mm(pu[:, :], u_t[:, k, jj * 128:(jj + 1) * 128], h2[:, k, tqs], k == 0, k == 15, [u_t, h2], [pu])
                        sg = sgb[isg % 2]
                        isg += 1
                        c.act(sg[:, :], pg[:, :], AF.Silu, [pg], [sg])
                        c.tt("dve", actT[:, j, tqs], pu[:, :], sg[:, :], ALU.mult, [pu, sg], [actT])
            a.release(m2)
            m2 = a.mark()
            wd = [a.alloc([44, 256], BF16) for _ in range(3)]
            igt = 0
            dits = [(sl, tq, tt_) for sl in range(8) for tq in range(TB // 512) for tt_ in range(4)]

            def dload(i):
                sl, tq, tt_ = dits[i]
                tok0 = tb * TB + tq * 512 + tt_ * 128
                c.dma(xb[i % 4][:, 0:256], x1.ap[tok0:tok0 + 128, sl * 256:(sl + 1) * 256], [x1.u((tok0 // 128, sl // 2))], [xb[i % 4]])

            dload(0)
            dload(1)
            gts = []
            for i, (sl, tq, tt_) in enumerate(dits):
                if i + 2 < len(dits):
                    dload(i + 2)
                w_t = wd[sl % 3]
                tqs = slice(tq * 512, (tq + 1) * 512)
                if tq == 0 and tt_ == 0:
                    c.dma(w_t[:, :, :], wdn[:, :, sl * 256:(sl + 1) * 256], [], [w_t], q="pool")
                if tt_ == 0:
                    gts = []
                    for dd in range(2):
                        dblk = sl * 2 + dd
                        pb = c.bank()
                        for j in range(44):
                            c.mm(pb[:, :], w_t[:, j, dd * 128:(dd + 1) * 128], actT[:, j, tqs], j == 0, j == 43, [w_t, actT], [pb])
                        gt = gtb[igt % 4]
                        igt += 1
                        c.act(gt[:, :], pb[:, :], AF.Identity, [pb, g2], [gt], scale=g2[:, dblk:dblk + 1])
                        gts.append(gt)
                tok0 = tb * TB + tq * 512 + tt_ * 128
                tki = tok0 // 128
                x_t, o_t = xb[i % 4], ob[i % 2]
                cs = slice(sl * 256, (sl + 1) * 256)
                pt = c.bank()
                for dd in range(2):
                    c.mm(pt[:, dd * 128:(dd + 1) * 128], gts[dd][:, tt_ * 128:(tt_ + 1) * 128], self.identf[:, :], True, True,
                         [gts[dd], self.identf], [pt])
                c.tt("dve", o_t[:, 0:256], pt[:, 0:256], x_t[:, 0:256], ALU.add, [pt, x_t], [o_t])
                c.dma(x2.ap[tok0:tok0 + 128, cs], o_t[:, 0:256], [o_t], [x2.u((tki, sl))])
            a.release(m2)
        a.release(m)

    def final_phase(self, b, x2, x2key):
        c = self.c
        a = c.arena
        m = a.mark()
        G = a.alloc([D], F32)
        self.load_vec_bc(G, self.w["final_norm_g"].ap.rearrange("(o d) -> o d", o=1), D)
        xb = [a.alloc([D], F32) for _ in range(2)]
        junk = a.alloc([D], BF16)
        st = [a.alloc([4], F32) for _ in range(2)]
        for t in range(NT):
            xt, s_ = xb[t % 2], st[t % 2]
            ts_ = slice(t * 128, (t + 1) * 128)
            c.dma(xt[:, :], x2.ap[ts_, :], x2key(t), [xt])
            c.act(junk[:, :], xt[:, :], AF.Square, [xt], [junk, s_], accum_out=s_[:, 0:1])
            c.act(s_[:, 1:2], s_[:, 0:1], AF.Sqrt, [s_, self.eps], [s_], scale=1.0 / D, bias=self.eps[:, 0:1])
            c.op("dve", lambda e, s_=s_: e.reciprocal(out=s_[:, 2:3], in_=s_[:, 1:2]), [s_], [s_])
            c.stt("dve", xt[:, :], xt[:, :], s_[:, 2:3], G[:, :], ALU.mult, ALU.mult, [xt, s_, G], [xt])
            c.dma(self.out.ap[b, ts_, :], xt[:, :], [xt], [self.out.u((b, t))])
        a.release(m)

    def build(self, stop_after=None):
        c = self.c
        a = c.arena
        self.consts()
        self.mod_phase()
        for b in range(self.nseq):
            xcur = self.x.ap[b]
            xkey = lambda t: []
            for l in range(self.nlayer):
                m = a.mark()
                proj = self.scr("proj", (S, IN_COLS))
                x1 = self.scr("x1", (S, D))
                x2 = self.scr("x2", (S, D))
                h2T_d = self.scr("h2T", (128, 16, S), BF16)
                self.last = dict(proj=proj, x1=x1, x2=x2, h2T_d=h2T_d)
                mixT = [a.alloc([S], BF16) for _ in range(16)]
                self.mixT = mixT
                m1 = a.mark()
                hT = [a.alloc([16, 128], BF16) for _ in range(NT)]
                self.norm_phase(b, l, xcur, xkey, "norm1_g", 1, 0, hT)
                self.in_proj_tm(b, l, hT, proj)
                a.release(m1)
                if stop_after == "in":
                    return
                self.gqa_phase(b, l, proj, mixT)
                if stop_after == "gqa":
                    self.dump_mix(mixT, 6)
                    return
                self.mla_phase(b, l, proj, mixT)
                if stop_after == "mla":
                    self.dump_mix(mixT, 10)
                    return
                m1 = a.mark()
                self.ssd_consts()
                self.ssd_phase(b, l, proj, mixT)
                a.release(m1)
                self.dump_mix(mixT)
                if stop_after == "ssd":
                    return
                self.out_phase(b, l, mixT, xcur, xkey, x1)
                a.release(m)
                if stop_after == "out":
                    return
                self.norm2_phase(b, l, x1, h2T_d)
                self.ffn_phase(b, l, x1, h2T_d, x2)
                xcur = x2.ap
                xkey = (lambda x2: (lambda t: [x2.u((t, n)) for n in range(8)]))(x2)
                if stop_after == "ffn":
                    return
            self.final_phase(b, x2, xkey)


_PROG = None


def _get_prog():
    global _PROG
    if _PROG is None:
        p = Prog()
        p.build()
        p.c.sch.emit()
        _PROG = p
    return _PROG


def make_in_maps(inputs, ncores=NCORES):
    consts = host_consts()
    consts["convT"] = layout_conv(np.asarray(inputs["conv_w"], np.float32), np.asarray(inputs["conv_b"], np.float32))
    shared = {k: np.ascontiguousarray(np.asarray(inputs[k], np.float32)) for k in WSHAPES}
    shared.update(consts)
    x = np.asarray(inputs["x"], np.float32)
    cc = np.asarray(inputs["c"], np.float32)
    maps = []
    for i in range(ncores):
        m = dict(shared)
        m["x"] = np.ascontiguousarray(x[2 * i:2 * i + 2])
        m["cT"] = np.ascontiguousarray(cc[2 * i:2 * i + 2].reshape(2, 16, 128).transpose(2, 1, 0).reshape(128, 32))
        maps.append(m)
    return maps


def kernel(**inputs):
    p = _get_prog()
    maps = make_in_maps(inputs)
    res = run_bass_kernel_spmd(p.c.nc, maps, core_ids=list(range(NCORES)))
    out = np.concatenate([np.asarray(r["out"], np.float32) for r in res.results], axis=0)
    return out
```

```python
import numpy as np
import concourse.bass as bass
import concourse.mybir as mybir
from concourse.bass_utils import run_bass_kernel_spmd

F32 = mybir.dt.float32
BF16 = mybir.dt.bfloat16
AF = mybir.ActivationFunctionType
ALU = mybir.AluOpType
AX = mybir.AxisListType

D = 2048
S = 2048
NT = S // 128
DEPTH = 2
EPS = 1e-6
IN_COLS = 4184
FFN = 5632
NCORES = 8


class U:
    __slots__ = ("w", "r", "name", "excl")

    def __init__(self, name="", excl=False):
        self.w = None
        self.r = {}
        self.name = name
        self.excl = excl


class Op:
    __slots__ = ("eng", "fn", "deps", "dma", "sig", "seq", "dsem", "dval", "dprev")

    def __init__(self, eng, fn, deps, dma):
        self.eng = eng
        self.fn = fn
        self.deps = deps
        self.dma = dma
        self.sig = False
        self.seq = 0
        self.dsem = None
        self.dval = 0
        self.dprev = None


ENGS = ("pe", "act", "dve", "pool", "sp")


class Sched:
    def __init__(self, nc, nds=8):
        self.nc = nc
        self.ops = []
        self.nds = nds

    def add(self, eng, fn, reads=(), writes=(), dma=False):
        i = len(self.ops)
        deps = set()
        xr = [u for u in reads if u.excl]
        if xr:
            reads = [u for u in reads if not u.excl]
            writes = list(writes) + [u for u in xr if u not in writes]
        for u in reads:
            if u.w is not None:
                deps.add(u.w)
        for u in writes:
            if u.w is not None:
                deps.add(u.w)
            deps.update(u.r.values())
        for u in reads:
            u.r[eng if not dma else ("dma", eng, i)] = i
        for u in writes:
            u.w = i
            u.r = {}
        deps.discard(i)
        self.ops.append(Op(eng, fn, deps, dma))
        return i

    def emit(self):
        nc = self.nc
        ops = self.ops
        for o in ops:
            real = []
            for d in o.deps:
                od = ops[d]
                if (not od.dma) and od.eng == o.eng and o.eng == "pe" and not o.dma:
                    continue
                real.append(d)
                if not od.dma:
                    od.sig = True
            o.deps = sorted(real)
        cnt = {e: 0 for e in ENGS}
        dcnt = {e: 0 for e in ENGS}
        for o in ops:
            if o.dma:
                n = dcnt[o.eng]
                dcnt[o.eng] += 1
                o.dsem = (o.eng, n % self.nds)
                o.dval = 16 * (n // self.nds + 1)
            elif o.sig:
                cnt[o.eng] += 1
                o.seq = cnt[o.eng]
        self.counts = (cnt, dcnt)
        sems = {e: nc.alloc_semaphore("s_" + e) for e in ENGS}
        dsems = {}
        for e in ENGS:
            if dcnt[e]:
                for k in range(self.nds):
                    dsems[(e, k)] = nc.alloc_semaphore("d_%s%d" % (e, k))
        per = {e: [o for o in ops if o.eng == e] for e in ENGS}

        def run(ename, eng):
            known = {}
            for o in per[ename]:
                if o.dma and o.dval > 16:
                    key = ("d",) + o.dsem
                    v = o.dval - 16
                    if known.get(key, 0) < v:
                        eng.wait_ge(dsems[o.dsem], v)
                        known[key] = v
                for d in o.deps:
                    od = ops[d]
                    if od.dma:
                        key = ("d",) + od.dsem
                        if known.get(key, 0) < od.dval:
                            eng.wait_ge(dsems[od.dsem], od.dval)
                            known[key] = od.dval
                    else:
                        key = od.eng
                        if known.get(key, 0) < od.seq:
                            eng.wait_ge(sems[od.eng], od.seq)
                            known[key] = od.seq
                ins = o.fn(eng)
                if o.dma:
                    ins.then_inc(dsems[o.dsem], 16)
                elif o.sig:
                    ins.then_inc(sems[ename], 1)
            for k in range(self.nds):
                if (ename, k) in dsems:
                    n = dcnt[ename]
                    last = (n - 1 - k) // self.nds + 1 if n - 1 >= k else 0
                    if last > 0:
                        eng.wait_ge(dsems[(ename, k)], 16 * last)

        with nc.Block() as block:
            @block.tensor
            def _(e):
                run("pe", e)

            @block.scalar
            def _(e):
                run("act", e)

            @block.vector
            def _(e):
                run("dve", e)

            @block.gpsimd
            def _(e):
                run("pool", e)

            @block.sync
            def _(e):
                run("sp", e)


def _units(xs):
    out = []
    for x in xs:
        if x is None:
            continue
        if isinstance(x, U):
            out.append(x)
        elif isinstance(x, (list, tuple)):
            out.extend(_units(x))
        else:
            out.extend(x.u)
    return out


class T:
    def __init__(self, ap, units):
        self.ap = ap
        self.u = units

    def __getitem__(self, k):
        return self.ap[k]


class Arena:
    def __init__(self, nc, nbytes):
        self.t = nc.alloc_sbuf_tensor("arena", [128, nbytes // 4], F32)
        self.top = 0
        self.nbytes = nbytes
        self.recs = []

    def alloc(self, free_shape, dtype):
        esz = 2 if dtype == BF16 else 4
        n = int(np.prod(free_shape))
        nb = (n * esz + 63) // 64 * 64
        off = self.top
        assert off + nb <= self.nbytes, ("arena overflow", off, nb, self.nbytes)
        self.top = off + nb
        ap = self.t[:, off // 4:(off + nb) // 4]
        if esz == 2:
            ap = ap.bitcast(BF16)
        ap = ap[:, 0:n]
        if len(free_shape) == 2:
            ap = ap.rearrange("p (a b) -> p a b", a=free_shape[0])
        elif len(free_shape) == 3:
            ap = ap.rearrange("p (a b c) -> p a b c", a=free_shape[0], b=free_shape[1])
        u = U()
        end = off + nb
        keep = []
        for (o, e, uo) in self.recs:
            if o < end and off < e:
                if uo.w is not None:
                    u.r[("inh", uo.w)] = uo.w
                for v in uo.r.values():
                    u.r[("inh", v)] = v
                if off <= o and e <= end:
                    continue
            keep.append((o, e, uo))
        keep.append((off, end, u))
        self.recs = keep
        return T(ap, [u])

    def mark(self):
        return self.top

    def release(self, m):
        self.top = m


class DT:
    def __init__(self, ap):
        self.ap = ap
        self.us = {}

    def u(self, key=0):
        if key not in self.us:
            self.us[key] = U()
        return self.us[key]

    def __getitem__(self, k):
        return self.ap[k]


class Ctx:
    def __init__(self):
        self.nc = bass.Bass("TRN2", target_bir_lowering=False)
        self.sch = Sched(self.nc)
        self.arena = Arena(self.nc, 207 * 1024)
        self.banks = []
        for i in range(8):
            t = self.nc.alloc_psum_tensor("pb%d" % i, [128, 512], F32)
            self.banks.append(T(t[:, :], [U("pb%d" % i, excl=True)]))
        self.bi = 0
        self.rot = [0, 1, 2, 3, 4, 5]
        self.ins = {}
        self.nscr = 0

    def bank(self):
        b = self.banks[self.rot[self.bi % len(self.rot)]]
        self.bi += 1
        return b

    def dram_in(self, name, shape, dt=F32):
        t = self.nc.dram_tensor(name, list(shape), dt, kind="ExternalInput")
        d = DT(t.ap())
        self.ins[name] = d
        return d

    def dram_out(self, name, shape, dt=F32):
        t = self.nc.dram_tensor(name, list(shape), dt, kind="ExternalOutput")
        return DT(t.ap())

    def scratch(self, shape, dt=F32, name=None):
        self.nscr += 1
        t = self.nc.dram_tensor(name or ("scr%d" % self.nscr), list(shape), dt, kind="Internal")
        return DT(t.ap())

    def dma(self, out, in_, reads, writes, q="sp", **kw):
        self.sch.add(q, lambda e: e.dma_start(out=out, in_=in_, **kw), _units(reads), _units(writes), dma=True)

    def mm(self, out, lhsT, rhs, start, stop, reads, writes):
        self.sch.add("pe", lambda e: e.matmul(out, lhsT, rhs, start=start, stop=stop), _units(reads), _units(writes))

    def tr(self, out, in_, ident, reads, writes):
        self.sch.add("pe", lambda e: e.transpose(out, in_, ident), _units(reads), _units(writes))

    def op(self, eng, fn, reads, writes):
        self.sch.add(eng, fn, _units(reads), _units(writes))

    def act(self, out, in_, func, reads, writes, eng="act", **kw):
        self.sch.add("act", lambda e: e.activation(out=out, in_=in_, func=func, **kw), _units(reads), _units(writes))

    def copy(self, eng, out, in_, reads, writes):
        if eng == "act":
            self.sch.add("act", lambda e: e.copy(out=out, in_=in_), _units(reads), _units(writes))
        else:
            self.sch.add(eng, lambda e: e.tensor_copy(out=out, in_=in_), _units(reads), _units(writes))

    def tt(self, eng, out, in0, in1, op, reads, writes):
        self.sch.add(eng, lambda e: e.tensor_tensor(out=out, in0=in0, in1=in1, op=op), _units(reads), _units(writes))

    def ts(self, eng, out, in0, s1, s2, op0, op1, reads, writes, **kw):
        if op1 is None:
            self.sch.add(eng, lambda e: e.tensor_scalar(out=out, in0=in0, scalar1=s1, scalar2=s2, op0=op0, **kw),
                         _units(reads), _units(writes))
        else:
            self.sch.add(eng, lambda e: e.tensor_scalar(out=out, in0=in0, scalar1=s1, scalar2=s2, op0=op0, op1=op1, **kw),
                         _units(reads), _units(writes))

    def stt(self, eng, out, in0, scalar, in1, op0, op1, reads, writes):
        self.sch.add(eng, lambda e: e.scalar_tensor_tensor(out=out, in0=in0, scalar=scalar, in1=in1, op0=op0, op1=op1),
                     _units(reads), _units(writes))


WSHAPES = {
    "w_ada": (DEPTH, D, 6 * D), "b_ada": (DEPTH, 6 * D), "norm1_g": (DEPTH, D), "norm2_g": (DEPTH, D),
    "w_in": (DEPTH, D, IN_COLS), "q_norm_g": (DEPTH, 128), "k_norm_g": (DEPTH, 128),
    "mla_q_norm_g": (DEPTH, 512), "w_uq": (DEPTH, 512, 768), "mla_kv_norm_g": (DEPTH, 256),
    "w_ukv": (DEPTH, 256, 1024), "conv_w": (DEPTH, 5, 1280), "conv_b": (DEPTH, 1280),
    "dt_bias": (DEPTH, 2, 12), "a_log": (DEPTH, 2, 12), "d_skip": (DEPTH, 12), "ssd_norm_g": (DEPTH, 768),
    "w_out": (DEPTH, D, D), "w_gate_up": (DEPTH, D, 2 * FFN), "w_down": (DEPTH, FFN, D),
    "final_norm_g": (D,),
}


def host_consts():
    import ml_dtypes
    c = {}
    c["ident"] = np.eye(128, dtype=np.float32).astype(ml_dtypes.bfloat16)
    c["identf"] = np.eye(128, dtype=np.float32)
    pos = np.arange(S)
    row = (pos // 64).astype(np.float32)
    col = (pos % 64).astype(np.float32)

    def tab(rot_dim):
        axis_dim = rot_dim // 2
        inv = np.power(np.float32(10000.0), -np.arange(0, axis_dim, 2, dtype=np.float32) / np.float32(axis_dim)).astype(np.float32)
        ar = (row[:, None] * inv[None, :]).astype(np.float32)
        ac = (col[:, None] * inv[None, :]).astype(np.float32)
        cr, sr, cc_, sc = np.cos(ar), np.sin(ar), np.cos(ac), np.sin(ac)
        cosx = np.concatenate([cr, cr, cc_, cc_], axis=1)
        sinx = np.concatenate([-sr, sr, -sc, sc], axis=1)
        return np.stack([cosx, sinx], axis=1).astype(np.float32)

    c["ropeA"] = tab(128)
    c["ropeB"] = tab(64)
    k = np.arange(128)
    c["triF"] = (k[:, None] <= k[None, :]).astype(np.float32)
    c["triR"] = (k[:, None] >= k[None, :]).astype(np.float32)
    sel = np.zeros((128, 24, 128), np.float32)
    for j in range(24):
        sel[j, j, :] = 1.0
    c["sel"] = sel
    mf = np.where(k[None, :] >= k[:, None], 0.0, -30000.0).astype(np.float32)
    mb = np.where(k[None, :] <= k[:, None], 0.0, -30000.0).astype(np.float32)
    c["maskF"] = np.tile(mf, (1, 3)).astype(ml_dtypes.bfloat16)
    c["maskB"] = np.tile(mb, (1, 3)).astype(ml_dtypes.bfloat16)
    return c


def layout_conv(conv_w, conv_b):
    L = conv_w.shape[0]
    o = np.zeros((L, 128, 10, 6), np.float32)
    o[:, :, :, 0:5] = conv_w.reshape(L, 5, 10, 128).transpose(0, 3, 2, 1)
    o[:, :, :, 5] = conv_b.reshape(L, 10, 128).transpose(0, 2, 1)
    return o


class Prog:
    def __init__(self, nseq=2, nlayer=DEPTH, dbg=()):
        self.c = Ctx()
        self.nseq = nseq
        self.nlayer = nlayer
        self.dbg = set(dbg)
        self.dbg_outs = {}
        c = self.c
        self.x = c.dram_in("x", (2, S, D))
        self.cT = c.dram_in("cT", (128, 32))
        self.w = {k: c.dram_in(k, v) for k, v in WSHAPES.items()}
        self.k_ident = c.dram_in("ident", (128, 128), BF16)
        self.k_identf = c.dram_in("identf", (128, 128))
        self.k_ropeA = c.dram_in("ropeA", (S, 2, 128))
        self.k_ropeB = c.dram_in("ropeB", (S, 2, 64))
        self.k_triF = c.dram_in("triF", (128, 128))
        self.k_triR = c.dram_in("triR", (128, 128))
        self.k_sel = c.dram_in("sel", (128, 24, 128))
        self.k_maskF = c.dram_in("maskF", (128, 384), BF16)
        self.k_maskB = c.dram_in("maskB", (128, 384), BF16)
        self.k_convT = c.dram_in("convT", (DEPTH, 128, 10, 6))
        self.out = c.dram_out("out", (2, S, D))

    def scr(self, name, shape, dt=F32):
        if name in self.dbg and name not in self.dbg_outs:
            return self.dbg_out(name, shape, dt)
        return self.c.scratch(shape, dt)

    def dump_mix(self, mixT, n=16):
        if "mix" in self.dbg and "mix" not in self.dbg_outs:
            d = self.dbg_out("mix", (16, 128, S), BF16)
            for k in range(n):
                self.c.dma(d.ap[k], mixT[k][:, :], [mixT[k]], [d.u(k)])

    def dbg_out(self, name, shape, dt=F32):
        d = self.c.dram_out("dbg_" + name, shape, dt)
        self.dbg_outs[name] = d
        return d

    def consts(self):
        c = self.c
        a = c.arena
        self.ident = a.alloc([128], BF16)
        c.dma(self.ident[:, :], self.k_ident[:, :], [], [self.ident])
        self.identf = a.alloc([128], F32)
        c.dma(self.identf[:, :], self.k_identf[:, :], [], [self.identf])
        self.eps = a.alloc([1], F32)
        self.one1 = a.alloc([1], F32)
        c.op("pool", lambda e: e.memset(self.one1[:, :], 1.0), [], [self.one1])
        self.ones = a.alloc([128], BF16)
        c.op("pool", lambda e: e.memset(self.ones[:, :], 1.0), [], [self.ones])
        c.op("pool", lambda e: e.memset(self.eps[:, :], EPS), [], [self.eps])

    def mod_phase(self):
        c = self.c
        a = c.arena
        m = a.mark()
        self.mod = c.scratch((DEPTH, 2, 6 * D), name="mod")
        ct = a.alloc([32], F32)
        c.dma(ct[:, :], self.cT[:, :], [], [ct])
        sg = a.alloc([32], F32)
        c.act(sg[:, :], ct[:, :], AF.Sigmoid, [ct], [sg])
        c.tt("dve", ct[:, :], ct[:, :], sg[:, :], ALU.mult, [ct, sg], [ct])
        ctp = a.alloc([16, 128], BF16)
        c.op("pool", lambda e: e.memset(ctp[:, :, :], 0.0), [], [ctp])
        c.copy("dve", ctp[:, :, 0:2], ct[:, :].rearrange("p (k b) -> p k b", b=2), [ct, ctp], [ctp])
        wb = [a.alloc([16, 512], BF16) for _ in range(3)]
        bb = [a.alloc([512], F32) for _ in range(2)]
        ob = [a.alloc([512], F32) for _ in range(2)]
        NB = 6 * D // 512
        i = 0
        for l in range(self.nlayer):
            wv = self.w["w_ada"].ap[l].rearrange("(k p) n -> p k n", p=128)
            for nb in range(NB):
                w_t, b_t, o_t = wb[i % 3], bb[i % 2], ob[i % 2]
                i += 1
                cs = slice(nb * 512, (nb + 1) * 512)
                c.dma(w_t[:, :, :], wv[:, :, cs], [], [w_t], q="pool")
                c.dma(b_t[0:2, :], self.w["b_ada"].ap[l:l + 1, cs].to_broadcast([2, 512]), [], [b_t])
                pb = c.bank()
                for k in range(16):
                    c.mm(pb[:, :], ctp[:, k, :], w_t[:, k, :], k == 0, k == 15, [ctp, w_t], [pb])
                c.tt("dve", o_t[0:2, :], pb[0:2, :], b_t[0:2, :], ALU.add, [pb, b_t], [o_t])
                c.dma(self.mod.ap[l, :, cs], o_t[0:2, :], [o_t], [self.mod.u((l, nb))])
        a.release(m)

    def load_bc(self, tile, l, b, j):
        c = self.c
        src = self.mod.ap[l, b:b + 1, j * D:(j + 1) * D].to_broadcast([128, D])
        c.dma(tile[:, :], src, [self.mod.u((l, 4 * j + q)) for q in range(4)], [tile])

    def load_vec_bc(self, tile, dram_ap_row, n):
        c = self.c
        c.dma(tile[:, 0:n], dram_ap_row.to_broadcast([128, n]), [], [tile])

    def bf_bank(self, pb, n):
        return pb.ap.bitcast(BF16).rearrange("p (a b) -> p a b", a=8)[:, 0:n, :]

    def rstd_from_ss(self, rstd, ss, n):
        c = self.c
        c.ts("dve", rstd, ss, 1.0 / n, EPS, ALU.mult, ALU.add, [], [])

    def norm_phase(self, b, l, xsrc, xkey, gname, jscale, jshift, hT, dst=None, after=None):
        c = self.c
        a = c.arena
        m = a.mark()
        G = a.alloc([D], F32)
        SH = a.alloc([D], F32)
        tmpg = a.alloc([D], F32)
        self.load_bc(G, l, b, jscale)
        self.load_vec_bc(tmpg, self.w[gname].ap[l:l + 1, :], D)
        c.stt("dve", G[:, :], G[:, :], 1.0, tmpg[:, :], ALU.add, ALU.mult, [G, tmpg], [G])
        self.load_bc(SH, l, b, jshift)
        xb = [a.alloc([D], F32) for _ in range(3)]
        junk = a.alloc([D], BF16)
        hb = [a.alloc([D], BF16) for _ in range(2)]
        st = [a.alloc([4], F32) for _ in range(2)]

        def load(t):
            c.dma(xb[t % 3][:, :], xsrc[t * 128:(t + 1) * 128, :], [xkey(t)], [xb[t % 3]])

        def stage_a(t):
            xt, ht, s_ = xb[t % 3], hb[t % 2], st[t % 2]
            c.act(junk[:, :], xt[:, :], AF.Square, [xt], [junk, s_], accum_out=s_[:, 0:1])
            c.act(s_[:, 1:2], s_[:, 0:1], AF.Sqrt, [s_, self.eps], [s_], scale=1.0 / D, bias=self.eps[:, 0:1])
            c.op("dve", lambda e, s_=s_: e.reciprocal(out=s_[:, 2:3], in_=s_[:, 1:2]), [s_], [s_])
            c.stt("dve", xt[:, :], xt[:, :], s_[:, 2:3], G[:, :], ALU.mult, ALU.mult, [xt, s_, G], [xt])
            c.tt("dve", ht[:, :], xt[:, :], SH[:, :], ALU.add, [xt, SH], [ht])

        def stage_b(t):
            ht = hb[t % 2]
            for half in range(2):
                pb = c.bank()
                pv = self.bf_bank(pb, 8)
                for k in range(8):
                    kc = half * 8 + k
                    c.tr(pv[:, k, :], ht[:, kc * 128:(kc + 1) * 128], self.ident[:, :], [ht, self.ident], [pb])
                if dst is None:
                    c.copy("act" if half == 0 else "dve", hT[t][:, half * 8:(half + 1) * 8, :], pv, [pb], [hT[t]])
                else:
                    dt_, off = dst(t)
                    c.copy("act" if half == 0 else "dve", dt_[:, half * 8:(half + 1) * 8, off:off + 128], pv, [pb], [dt_])
            if after is not None:
                after(t)

        load(0)
        load(1)
        for t in range(NT):
            if t + 2 < NT:
                load(t + 2)
            stage_a(t)
            if t >= 1:
                stage_b(t - 1)
        stage_b(NT - 1)
        a.release(m)

    def load_w_bf16(self, wt, src):
        self.c.dma(wt, src, [], [], q="pool")

    def in_proj_tm(self, b, l, hT, proj):
        c = self.c
        a = c.arena
        m = a.mark()
        wv = self.w["w_in"].ap[l].rearrange("(k p) n -> p k n", p=128)
        blocks = [(j * 512, j * 512, 512) for j in range(8)] + [(4096, 4096, 88)]
        wb = [a.alloc([16, 512], BF16) for _ in range(2)]
        ob = [a.alloc([512], F32) for _ in range(3)]
        i = 0
        for bi, (wc, pc, n) in enumerate(blocks):
            w_t = wb[bi % 2]
            c.dma(w_t[:, :, 0:n], wv[:, :, wc:wc + n], [], [w_t], q="pool")
            for t in range(NT):
                pb = c.bank()
                for k in range(16):
                    c.mm(pb[:, 0:n], hT[t][:, k, :], w_t[:, k, 0:n], k == 0, k == 15, [hT[t], w_t], [pb])
                o_t = ob[i % 3]
                i += 1
                c.copy("act" if i % 2 else "dve", o_t[:, 0:n], pb[:, 0:n], [pb], [o_t])
                c.dma(proj.ap[t * 128:(t + 1) * 128, pc:pc + n], o_t[:, 0:n], [o_t], [proj.u((t, bi))])
        a.release(m)

    def rms_heads(self, x3, nh, hd, ss, junk3, eng="dve"):
        pass

    def rope_tm(self, x2d, tab, nh, hd, T1, T2, SW, out2d, units_r, units_w, out3=False):
        c = self.c
        q = hd // 4
        n = nh * hd
        x3 = x2d.rearrange("p (h d) -> p h d", h=nh)
        xq = x2d.rearrange("p (g b f) -> p g b f", b=2, f=q)
        swq = SW[:, 0:n].rearrange("p (g b f) -> p g b f", b=2, f=q)
        sw3 = SW[:, 0:n].rearrange("p (h d) -> p h d", h=nh)
        t13 = T1[:, 0:n].rearrange("p (h d) -> p h d", h=nh)
        t23 = T2[:, 0:n].rearrange("p (h d) -> p h d", h=nh)
        cosb = tab[:, 0:1, :].to_broadcast([128, nh, hd])
        sinb = tab[:, 1:2, :].to_broadcast([128, nh, hd])
        c.copy("act", swq[:, :, 0, :], xq[:, :, 1, :], units_r, [SW])
        c.copy("act", swq[:, :, 1, :], xq[:, :, 0, :], units_r, [SW])
        c.tt("dve", t13, x3, cosb, ALU.mult, units_r, [T1])
        c.tt("dve", t23, sw3, sinb, ALU.mult, [SW] + units_r, [T2])
        if out3:
            c.tt("dve", out2d, t13, t23, ALU.add, [T1, T2], units_w)
        else:
            c.tt("dve", out2d, T1[:, 0:n], T2[:, 0:n], ALU.add, [T1, T2], units_w)

    def attn_core(self, heads, mixT, scale):
        c = self.c
        a = c.arena
        m = a.mark()
        LOOK = 2
        pt = [a.alloc([512], BF16) for _ in range(4)]
        rec = [a.alloc([512], F32) for _ in range(2)]
        old_rot = c.rot
        c.rot = [0, 1, 2, 3]
        accs = [(c.banks[4], c.banks[5]), (c.banks[6], c.banks[7])]
        iters = [(hd, qt, kc) for hd in heads for qt in range(4) for kc in range(16)]
        sbs = {}

        def emit_s(i):
            hd, qt, kc = iters[i]
            sb = c.bank()
            np_ = len(hd["kparts"])
            for pi, (kf, qf) in enumerate(hd["kparts"]):
                c.mm(sb[:, :], kf(kc), qf(qt), pi == 0, pi == np_ - 1, hd["reads"], [sb])
            sbs[i] = sb

        def emit_rest(i):
            hd, qt, kc = iters[i]
            sb = sbs.pop(i)
            accO, accD = accs[(i // 16) % 2]
            p_t = pt[i % 4]
            c.act(p_t[:, :], sb[:, :], AF.Exp, [sb], [p_t], scale=scale)
            c.mm(accO[:, :], hd["v_fn"](kc), p_t[:, :], kc == 0, kc == 15, [p_t] + hd["reads"], [accO])
            c.mm(accD[:, :], self.ones[:, :], p_t[:, :], kc == 0, kc == 15, [p_t, self.ones], [accD])
            if kc == 15:
                r_t = rec[(i // 16) % 2]
                c.op("dve", lambda e, r_t=r_t, accD=accD: e.reciprocal(out=r_t[:, :], in_=accD[:, :]), [accD], [r_t])
                mo = mixT[hd["out"]]
                c.tt("dve", mo[:, qt * 512:(qt + 1) * 512], accO[:, :], r_t[:, :], ALU.mult, [accO, r_t], [mo])

        n = len(iters)
        for i in range(n + LOOK):
            if i < n:
                emit_s(i)
            if i - LOOK >= 0:
                emit_rest(i - LOOK)
        c.rot = old_rot
        a.release(m)

    def gqa_phase(self, b, l, proj, mixT):
        c = self.c
        a = c.arena
        m = a.mark()
        qk = a.alloc([8, S], BF16)
        vv = a.alloc([NT, 256], BF16)
        m2 = a.mark()
        gain = a.alloc([8, 128], F32)
        for h in range(8):
            nm = "q_norm_g" if h < 6 else "k_norm_g"
            c.dma(gain[:, h, :], self.w[nm].ap[l:l + 1, :].to_broadcast([128, 128]), [], [gain])
        xb = [a.alloc([1280], F32) for _ in range(2)]
        sqs = [a.alloc([1024], F32) for _ in range(2)]
        st = [a.alloc([32], F32) for _ in range(2)]
        T1s = [a.alloc([1024], F32) for _ in range(2)]
        T2s = [a.alloc([1024], F32) for _ in range(2)]
        SWs = [a.alloc([1024], F32) for _ in range(2)]
        ob = [a.alloc([1024], BF16) for _ in range(2)]
        tb = [a.alloc([256], F32) for _ in range(2)]
        def stage_a(t):
            xt, s_, o_t, tab = xb[t % 2], st[t % 2], ob[t % 2], tb[t % 2]
            sq, T1, T2, SW = sqs[t % 2], T1s[t % 2], T2s[t % 2], SWs[t % 2]
            c.dma(xt[:, :], proj.ap[t * 128:(t + 1) * 128, 0:1280], [proj.u((t, j)) for j in range(3)], [xt])
            c.dma(tab[:, :], self.k_ropeA.ap[t * 128:(t + 1) * 128].rearrange("p a f -> p (a f)"), [], [tab])
            x3 = xt[:, 0:1024].rearrange("p (h d) -> p h d", h=8)
            c.act(sq[:, :], xt[:, 0:1024], AF.Square, [xt], [sq])
            c.op("dve", lambda e, s_=s_, sq=sq: e.tensor_reduce(out=s_[:, 0:8], in_=sq[:, :].rearrange("p (h d) -> p h d", h=8),
                                                         axis=AX.X, op=ALU.add), [sq], [s_])
            c.act(s_[:, 8:16], s_[:, 0:8], AF.Sqrt, [s_, self.eps], [s_], scale=1.0 / 128, bias=self.eps[:, 0:1])
            c.op("dve", lambda e, s_=s_: e.reciprocal(out=s_[:, 16:24], in_=s_[:, 8:16]), [s_], [s_])
            c.tt("dve", x3, x3, s_[:, 16:24].unsqueeze(2).to_broadcast([128, 8, 128]), ALU.mult, [xt, s_], [xt])
            c.tt("dve", x3, x3, gain[:, :, :], ALU.mult, [xt, gain], [xt])
            self.rope_tm(xt[:, 0:1024], tab[:, :].rearrange("p (a f) -> p a f", a=2), 8, 128, T1, T2, SW, o_t[:, :], [xt, tab], [o_t])
            c.copy("act", vv[:, t, :], xt[:, 1024:1280], [xt], [vv])

        def stage_b(t):
            o_t = ob[t % 2]
            pb = c.bank()
            pv = self.bf_bank(pb, 8)
            for h in range(8):
                c.tr(pv[:, h, :], o_t[:, h * 128:(h + 1) * 128], self.ident[:, :], [o_t, self.ident], [pb])
            c.copy("act", qk[:, :, t * 128:(t + 1) * 128], pv, [pb], [qk])

        stage_a(0)
        for t in range(NT):
            if t + 1 < NT:
                stage_a(t + 1)
            stage_b(t)
        a.release(m2)
        heads = []
        for g in range(2):
            for r in range(3):
                h = g * 3 + r
                heads.append(dict(
                    kparts=[(lambda kc, g=g: qk[:, 6 + g, kc * 128:(kc + 1) * 128],
                             lambda qt, h=h: qk[:, h, qt * 512:(qt + 1) * 512])],
                    v_fn=lambda kc, g=g: vv[:, kc, g * 128:(g + 1) * 128],
                    reads=[qk, vv], out=h))
        self.attn_core(heads, mixT, 128 ** -0.5)
        a.release(m)

    def mla_phase(self, b, l, proj, mixT):
        c = self.c
        a = c.arena
        m = a.mark()
        qn = a.alloc([4, S], BF16)
        qp = a.alloc([4, S], BF16)
        kn = a.alloc([4, S], BF16)
        kp = a.alloc([S], BF16)
        vb = a.alloc([NT, 512], BF16)
        m2 = a.mark()
        cqT = a.alloc([4, S], BF16)
        ckvT = a.alloc([2, S], BF16)
        wuq = a.alloc([4, 768], BF16)
        wukv = a.alloc([2, 1024], BF16)
        c.dma(wuq[:, :, :], self.w["w_uq"].ap[l].rearrange("(k p) n -> p k n", p=128), [], [wuq], q="pool")
        c.dma(wukv[:, :, :], self.w["w_ukv"].ap[l].rearrange("(k p) n -> p k n", p=128), [], [wukv], q="pool")
        gq = a.alloc([512], F32)
        gkv = a.alloc([256], F32)
        self.load_vec_bc(gq, self.w["mla_q_norm_g"].ap[l:l + 1, :], 512)
        self.load_vec_bc(gkv, self.w["mla_kv_norm_g"].ap[l:l + 1, :], 256)
        xb = [a.alloc([832], F32) for _ in range(2)]
        junks = [a.alloc([512], BF16) for _ in range(2)]
        st = [a.alloc([8], F32) for _ in range(2)]
        TK = [[a.alloc([64], F32) for _ in range(3)] for _ in range(2)]
        TQ = [[a.alloc([256], F32) for _ in range(3)] for _ in range(2)]
        ob = [a.alloc([896], BF16) for _ in range(2)]
        tb = [a.alloc([128], F32) for _ in range(2)]
        qpb = [a.alloc([256], F32) for _ in range(2)]
        qpo = [a.alloc([4, 128], BF16) for _ in range(2)]
        for o_ in ob:
            c.op("pool", lambda e, o_=o_: e.memset(o_[:, 832:896], 0.0), [o_], [o_])
        for o_ in qpo:
            c.op("pool", lambda e, o_=o_: e.memset(o_[:, :, :], 0.0), [o_], [o_])
        wqpe = a.alloc([4, 256], BF16)
        wv = a.alloc([2, 512], BF16)
        for h in range(4):
            c.copy("dve", wqpe[:, :, h * 64:(h + 1) * 64], wuq[:, :, h * 192 + 128:h * 192 + 192], [wuq, wqpe], [wqpe])
            c.copy("dve", wv[:, :, h * 128:(h + 1) * 128], wukv[:, :, h * 256 + 128:h * 256 + 256], [wukv, wv], [wv])
        def stage_a(t):
            xt, s_, o_t, tab = xb[t % 2], st[t % 2], ob[t % 2], tb[t % 2]
            junk = junks[t % 2]
            c.dma(xt[:, :], proj.ap[t * 128:(t + 1) * 128, 1280:2112], [proj.u((t, j)) for j in (2, 3, 4)], [xt])
            c.dma(tab[:, :], self.k_ropeB.ap[t * 128:(t + 1) * 128].rearrange("p a f -> p (a f)"), [], [tab])
            c.act(junk[:, 0:512], xt[:, 0:512], AF.Square, [xt], [junk, s_], accum_out=s_[:, 0:1])
            c.act(junk[:, 0:256], xt[:, 512:768], AF.Square, [xt], [junk, s_], accum_out=s_[:, 1:2])
            c.ts("dve", s_[:, 0:1], s_[:, 0:1], 0.5, None, ALU.mult, None, [s_], [s_])
            c.act(s_[:, 2:4], s_[:, 0:2], AF.Sqrt, [s_, self.eps], [s_], scale=1.0 / 256, bias=self.eps[:, 0:1])
            c.op("dve", lambda e, s_=s_: e.reciprocal(out=s_[:, 4:6], in_=s_[:, 2:4]), [s_], [s_])
            c.stt("dve", o_t[:, 0:512], xt[:, 0:512], s_[:, 4:5], gq[:, 0:512], ALU.mult, ALU.mult, [xt, s_, gq], [o_t])
            c.stt("dve", o_t[:, 512:768], xt[:, 512:768], s_[:, 5:6], gkv[:, 0:256], ALU.mult, ALU.mult, [xt, s_, gkv], [o_t])
            tabv = tab[:, :].rearrange("p (a f) -> p a f", a=2)
            T1, T2, SW = TK[t % 2]
            self.rope_tm(xt[:, 768:832], tabv, 1, 64, T1, T2, SW, o_t[:, 768:832], [xt, tab], [o_t])

        def stage_b(t):
            o_t, tab = ob[t % 2], tb[t % 2]
            tabv = tab[:, :].rearrange("p (a f) -> p a f", a=2)
            pb = c.bank()
            pv = self.bf_bank(pb, 8)
            for k in range(6):
                c.tr(pv[:, k, :], o_t[:, k * 128:(k + 1) * 128], self.ident[:, :], [o_t, self.ident], [pb])
            c.tr(pv[:, 6, :], o_t[:, 768:896], self.ident[:, :], [o_t, self.ident], [pb])
            tc_ = slice(t * 128, (t + 1) * 128)
            c.copy("act", cqT[:, :, tc_], pv[:, 0:4, :], [pb], [cqT])
            c.copy("dve", ckvT[:, :, tc_], pv[:, 4:6, :], [pb], [ckvT])
            c.copy("act", kp[:, tc_], pv[:, 6, :], [pb], [kp])
            pq = c.bank()
            for k in range(4):
                c.mm(pq[:, 0:256], cqT[:, k, tc_], wqpe[:, k, :], k == 0, k == 3, [cqT, wqpe], [pq])
            qb_, qo_ = qpb[t % 2], qpo[t % 2]
            c.copy("act", qb_[:, :], pq[:, 0:256], [pq], [qb_])
            T1, T2, SW = TQ[t % 2]
            self.rope_tm(qb_[:, :], tabv, 4, 64, T1, T2, SW, qo_[:, :, 0:64], [qb_, tab], [qo_], out3=True)
            pb2 = c.bank()
            pv2 = self.bf_bank(pb2, 8)
            for h in range(4):
                c.tr(pv2[:, h, :], qo_[:, h, :], self.ident[:, :], [qo_, self.ident], [pb2])
            c.copy("dve", qp[:, :, tc_], pv2[:, 0:4, :], [pb2], [qp])
            pvb = c.bank()
            for k in range(2):
                c.mm(pvb[:, :], ckvT[:, k, tc_], wv[:, k, :], k == 0, k == 1, [ckvT, wv], [pvb])
            c.copy("act", vb[:, t, :], pvb[:, :], [pvb], [vb])
        stage_a(0)
        for t in range(NT):
            if t + 1 < NT:
                stage_a(t + 1)
            stage_b(t)
        i = 0
        for h in range(4):
            for tq in range(4):
                ts_ = slice(tq * 512, (tq + 1) * 512)
                p1 = c.bank()
                for k in range(4):
                    c.mm(p1[:, :], wuq[:, k, h * 192:h * 192 + 128], cqT[:, k, ts_], k == 0, k == 3, [wuq, cqT], [p1])
                c.copy("act", qn[:, h, ts_], p1[:, :], [p1], [qn])
                p2 = c.bank()
                for k in range(2):
                    c.mm(p2[:, :], wukv[:, k, h * 256:h * 256 + 128], ckvT[:, k, ts_], k == 0, k == 1, [wukv, ckvT], [p2])
                c.copy("dve", kn[:, h, ts_], p2[:, :], [p2], [kn])
        a.release(m2)
        heads = []
        for h in range(4):
            heads.append(dict(
                kparts=[(lambda kc, h=h: kn[:, h, kc * 128:(kc + 1) * 128], lambda qt, h=h: qn[:, h, qt * 512:(qt + 1) * 512]),
                        (lambda kc: kp[:, kc * 128:(kc + 1) * 128], lambda qt, h=h: qp[:, h, qt * 512:(qt + 1) * 512])],
                v_fn=lambda kc, h=h: vb[:, kc, h * 128:(h + 1) * 128],
                reads=[qn, qp, kn, kp, vb], out=6 + h))
        self.attn_core(heads, mixT, 192 ** -0.5)
        a.release(m)

    def ssd_consts(self):
        c = self.c
        a = c.arena
        self.triF = a.alloc([128], F32)
        self.triR = a.alloc([128], F32)
        self.onesf = a.alloc([128], F32)
        self.sel = a.alloc([24, 128], F32)
        self.maskF = a.alloc([384], BF16)
        self.maskB = a.alloc([384], BF16)
        c.dma(self.triF[:, :], self.k_triF[:, :], [], [self.triF])
        c.dma(self.triR[:, :], self.k_triR[:, :], [], [self.triR])
        c.op("pool", lambda e: e.memset(self.onesf[:, :], 1.0), [], [self.onesf])
        c.dma(self.sel[:, :, :], self.k_sel[:, :, :], [], [self.sel])
        c.dma(self.maskF[:, :], self.k_maskF[:, :], [], [self.maskF])
        c.dma(self.maskB[:, :], self.k_maskB[:, :], [], [self.maskB])

    def ssd_phase(self, b, l, proj, mixT):
        c = self.c
        a = c.arena
        m = a.mark()
        yf_d = c.scratch((S, 768))
        aneg = a.alloc([24], F32)
        dtb = a.alloc([24], F32)
        dsk = a.alloc([12], F32)
        ng = a.alloc([768], F32)
        cw = a.alloc([10, 6], F32)
        c.dma(aneg[:, :], self.w["a_log"].ap[l:l + 1].rearrange("o a h -> o (a h)").to_broadcast([128, 24]), [], [aneg])
        c.act(aneg[:, :], aneg[:, :], AF.Exp, [aneg], [aneg])
        c.ts("dve", aneg[:, :], aneg[:, :], -1.0, None, ALU.mult, None, [aneg], [aneg])
        c.dma(dtb[:, :], self.w["dt_bias"].ap[l:l + 1].rearrange("o a h -> o (a h)").to_broadcast([128, 24]), [], [dtb])
        c.dma(dsk[:, :], self.w["d_skip"].ap[l:l + 1, :].to_broadcast([128, 12]), [], [dsk])
        self.load_vec_bc(ng, self.w["ssd_norm_g"].ap[l:l + 1, :], 768)
        c.dma(cw[:, :, :], self.k_convT.ap[l], [], [cw])
        dt = a.alloc([NT, 24], F32)
        af = a.alloc([NT, 24], F32)
        ab = a.alloc([NT, 24], F32)
        c.op("pool", lambda e: e.memset(af[:, :, :], 0.0), [], [af])
        c.op("pool", lambda e: e.memset(ab[:, :, :], 0.0), [], [ab])
        for t in range(NT):
            c.dma(dt[:, t, :], proj.ap[t * 128:(t + 1) * 128, 4160:4184], [proj.u((t, 8))], [dt])
        c.tt("dve", dt[:, :, :], dt[:, :, :], dtb[:, :].unsqueeze(1).to_broadcast([128, NT, 24]), ALU.add, [dt, dtb], [dt])
        c.act(dt[:, :, :], dt[:, :, :], AF.Exp, [dt], [dt])
        c.act(dt[:, :, :], dt[:, :, :], AF.Ln, [dt, self.one1], [dt], bias=self.one1[:, 0:1])
        c.tt("dve", af[:, :, 0:12], dt[:, :, 0:12], aneg[:, 0:12].unsqueeze(1).to_broadcast([128, NT, 12]), ALU.mult, [dt, aneg, af], [af])
        c.tt("dve", ab[:, :, 12:24], dt[:, :, 12:24], aneg[:, 12:24].unsqueeze(1).to_broadcast([128, NT, 12]), ALU.mult, [dt, aneg, ab], [ab])
        bcT = a.alloc([4, S], BF16)
        xs_tm = a.alloc([NT, 768], BF16)
        b_tm = a.alloc([NT, 256], BF16)
        m2 = a.mark()
        xin = [a.alloc([8, 128], F32) for _ in range(2)]
        xT = [a.alloc([S + 4], F32) for _ in range(2)]
        acc = [a.alloc([S], F32)] * 2
        cvo = [a.alloc([S], BF16) for _ in range(2)]
        for xt_ in xT:
            c.op("pool", lambda e, xt_=xt_: e.memset(xt_[:, 0:2], 0.0), [xt_], [xt_])
            c.op("pool", lambda e, xt_=xt_: e.memset(xt_[:, S + 2:S + 4], 0.0), [xt_], [xt_])
        ii = 0
        for blk in range(10):
            x_t, ac, co = xT[blk % 2], acc[blk % 2], cvo[blk % 2]
            cs = 2880 + blk * 128
            for t8 in range(2):
                xi = xin[ii % 2]
                ii += 1
                c.dma(xi[:, :, :], proj.ap[t8 * 1024:(t8 + 1) * 1024, cs:cs + 128].rearrange("(t p) c -> p t c", p=128),
                      [proj.u((t, j)) for t in range(t8 * 8, t8 * 8 + 8) for j in (5, 6, 7, 8)], [xi])
                for t4 in range(2):
                    pb = c.bank()
                    for q in range(4):
                        c.mm(pb[:, q * 128:(q + 1) * 128], xi[:, t4 * 4 + q, :], self.identf[:, :], True, True, [xi, self.identf], [pb])
                    o0 = 2 + (t8 * 2 + t4) * 512
                    c.copy("act", x_t[:, o0:o0 + 512], pb[:, :], [pb], [x_t])
            c.ts("dve", ac[:, :], x_t[:, 0:S], cw[:, blk, 0:1], None, ALU.mult, None, [x_t, cw], [ac])
            for j in range(1, 5):
                c.stt("dve", ac[:, :], x_t[:, j:j + S], cw[:, blk, j:j + 1], ac[:, :], ALU.mult, ALU.add, [x_t, cw, ac], [ac])
            if blk < 6 or blk in (6, 7):
                c.act(co[:, :], ac[:, :], AF.Silu, [ac, cw], [co], bias=cw[:, blk, 5:6])
                for t8 in range(2):
                    pb = c.bank()
                    pv = self.bf_bank(pb, 8)
                    for q in range(8):
                        t = t8 * 8 + q
                        c.tr(pv[:, q, :], co[:, t * 128:(t + 1) * 128], self.ident[:, :], [co, self.ident], [pb])
                    if blk < 6:
                        c.copy("act" if t8 else "dve", xs_tm[:, t8 * 8:(t8 + 1) * 8, blk * 128:(blk + 1) * 128], pv, [pb], [xs_tm])
                    else:
                        c.copy("act" if t8 else "dve", b_tm[:, t8 * 8:(t8 + 1) * 8, (blk - 6) * 128:(blk - 5) * 128], pv, [pb], [b_tm])
            if blk >= 6:
                c.act(bcT[:, blk - 6, :], ac[:, :], AF.Silu, [ac, cw], [bcT], bias=cw[:, blk, 5:6])
        a.release(m2)
        H = [[a.alloc([384], F32) for _ in range(2)] for _ in range(2)]
        Hb = [[a.alloc([384], BF16) for _ in range(2)] for _ in range(2)]
        NS = 4
        apad = [a.alloc([128], F32) for _ in range(NS)]
        for ap_ in apad:
            c.op("pool", lambda e, ap_=ap_: e.memset(ap_[:, :], 0.0), [], [ap_])
        cs_ = [a.alloc([72], F32) for _ in range(NS)]
        ex = [a.alloc([72], F32) for _ in range(NS)]
        pfm = [a.alloc([256], F32) for _ in range(NS)]
        xd = [a.alloc([768], BF16) for _ in range(NS)]
        xdw = [a.alloc([768], BF16) for _ in range(2)] * 2
        xdw = [xdw[0], xdw[0], xdw[1], xdw[1]]
        yo = [a.alloc([768], F32) for _ in range(NS)]
        cbT = [[a.alloc([128], F32) for _ in range(2)] for _ in range(2)]
        eD = [[a.alloc([384], F32) for _ in range(2)] for _ in range(2)]
        wT = [[a.alloc([384], BF16) for _ in range(3)] for _ in range(2)]
        ytmp = [[a.alloc([384], F32)] * 2 for _ in range(2)]
        zb = [a.alloc([768], F32) for _ in range(2)]
        yfb = [a.alloc([768], F32) for _ in range(2)]
        yn = [a.alloc([768], BF16) for _ in range(2)]
        st = [a.alloc([8], F32) for _ in range(2)]
        junk = a.alloc([384], BF16)
        accA, accB = c.banks[6], c.banks[7]
        for d in range(2):
            for g in range(2):
                c.op("pool", lambda e, d=d, g=g: e.memset(H[d][g][:, :], 0.0), [H[d][g]], [H[d][g]])
                c.op("pool", lambda e, d=d, g=g: e.memset(Hb[d][g][:, :], 0.0), [Hb[d][g]], [Hb[d][g]])
        cnt = {"w": [0, 0], "f": 0}

        def chunk_step(d, ch, k, finalize):
            mask = self.maskF if d == 0 else self.maskB
            a_own = af if d == 0 else ab
            tri = self.triF if d == 0 else self.triR
            cst, ext, pf, xd_, xw_ = cs_[k], ex[k], pfm[k], xd[k], xdw[k]
            tc_ = slice(ch * 128, (ch + 1) * 128)
            d0 = d * 12
            ap_ = apad[k]
            c.copy("act", ap_[:, 0:24], a_own[:, ch, :], [a_own, ap_], [ap_])
            pb = c.bank()
            c.mm(pb[:, 0:24], tri[:, :], a_own[:, ch, 0:24], True, True, [tri, a_own], [pb])
            c.mm(pb[:, 24:48], self.onesf[:, :], a_own[:, ch, 0:24], True, True, [self.onesf, a_own], [pb])
            c.mm(pb[:, 128:256], ap_[:, :], tri[:, :], True, True, [tri, ap_], [pb])
            yield
            c.copy("dve", cst[:, 0:48], pb[:, 0:48], [pb], [cst])
            c.copy("act", pf[:, 0:128], pb[:, 128:256], [pb], [pf])
            c.ts("dve", pf[:, 128:256], pb[:, 128:256], -1.0, None, ALU.mult, None, [pb], [pf])
            c.tt("dve", cst[:, 48:72], cst[:, 24:48], cst[:, 0:24], ALU.subtract, [cst], [cst])
            c.act(ext[:, 0:24], cst[:, 0:24], AF.Exp, [cst], [ext])
            c.act(ext[:, 24:48], cst[:, 48:72], AF.Exp, [cst], [ext])
            c.act(ext[:, 48:72], cst[:, 24:48], AF.Exp, [cst], [ext])
            xs3 = xs_tm[:, ch, :].rearrange("p (h q) -> p h q", h=12)
            xd3 = xd_[:, :].rearrange("p (h q) -> p h q", h=12)
            xw3 = xw_[:, :].rearrange("p (h q) -> p h q", h=12)
            c.tt("pool", xd3, xs3, dt[:, ch, d0:d0 + 12].unsqueeze(2).to_broadcast([128, 12, 64]), ALU.mult, [xs_tm, dt], [xd_])
            c.tt("pool", xw3, xd3, ext[:, 24 + d0:36 + d0].unsqueeze(2).to_broadcast([128, 12, 64]), ALU.mult, [xd_, ext], [xw_])
            yield
            y_t = yo[k]
            for g in range(2):
                cb = cbT[d][g]
                Hg, Hbg = H[d][g], Hb[d][g]
                pcb = c.bank()
                c.mm(pcb[:, 0:128], bcT[:, g, tc_], bcT[:, 2 + g, tc_], True, True, [bcT], [pcb])
                yield
                c.copy("act", cb[:, :], pcb[:, 0:128], [pcb], [cb])
                accY = (accA if g == 0 else accB) if d == 0 else (c.banks[4] if g == 0 else c.banks[5])
                for half in range(2):
                    pd = c.bank()
                    for j3 in range(3):
                        hh = d0 + g * 6 + half * 3 + j3
                        reg = pd[:, j3 * 128:(j3 + 1) * 128]
                        c.mm(reg, self.sel[:, hh, :], pf[:, 0:128], j3 == 0, False, [self.sel, pf], [pd])
                        c.mm(reg, pf[:, 128:256], self.sel[:, hh, :], False, False, [self.sel, pf], [pd])
                    c.mm(pd[:, 0:384], self.ident[:, :], mask[:, :], False, True, [self.ident, mask], [pd])
                    yield
                    e_t = eD[d][half]
                    c.act(e_t[:, :], pd[:, 0:384], AF.Exp, [pd], [e_t])
                    w_t = wT[d][cnt["w"][d] % 3]
                    cnt["w"][d] += 1
                    c.tt("dve", w_t[:, :].rearrange("p (j s) -> p j s", j=3), e_t[:, :].rearrange("p (j s) -> p j s", j=3),
                         cb[:, :].unsqueeze(1).to_broadcast([128, 3, 128]), ALU.mult, [e_t, cb], [w_t])
                    yield
                    for j3 in range(3):
                        hl = g * 6 + half * 3 + j3
                        c.mm(accY[:, (half * 3 + j3) * 64:(half * 3 + j3 + 1) * 64], w_t[:, j3 * 128:(j3 + 1) * 128],
                             xd_[:, hl * 64:(hl + 1) * 64], True, True, [w_t, xd_], [accY])
                yield
                po = c.bank()
                c.mm(po[:, 0:384], bcT[:, 2 + g, tc_], Hbg[:, :], True, True, [bcT, Hbg], [po])
                yt = ytmp[d][g]
                c.tt("dve", yt[:, :].rearrange("p (h q) -> p h q", h=6), po[:, 0:384].rearrange("p (h q) -> p h q", h=6),
                     ext[:, d0 + g * 6:d0 + g * 6 + 6].unsqueeze(2).to_broadcast([128, 6, 64]), ALU.mult, [po, ext], [yt])
                c.tt("dve", y_t[:, g * 384:(g + 1) * 384], accY[:, 0:384], yt[:, :], ALU.add, [accY, yt], [y_t])
                ps_ = c.bank()
                c.mm(ps_[:, 0:384], b_tm[:, ch, g * 128:(g + 1) * 128], xw_[:, g * 384:(g + 1) * 384], True, True,
                     [b_tm, xw_], [ps_])
                yield
                c.tt("dve", Hg[:, :].rearrange("p (h q) -> p h q", h=6), Hg[:, :].rearrange("p (h q) -> p h q", h=6),
                     ext[:, 48 + d0 + g * 6:48 + d0 + g * 6 + 6].unsqueeze(2).to_broadcast([128, 6, 64]), ALU.mult,
                     [Hg, ext], [Hg])
                c.tt("dve", Hg[:, :], Hg[:, :], ps_[:, 0:384], ALU.add, [Hg, ps_], [Hg])
                c.copy("act", Hbg[:, :], Hg[:, :], [Hg], [Hbg])
                yield
            if not finalize:
                c.dma(yf_d.ap[tc_, :], y_t[:, :], [y_t], [yf_d.u(ch)])
                return
            kf = cnt["f"] % 2
            cnt["f"] += 1
            z_t, yf_t, yn_t, s_ = zb[kf], yfb[kf], yn[kf], st[kf]
            c.dma(yf_t[:, :], yf_d.ap[tc_, :], [yf_d.u(ch)], [yf_t])
            c.dma(z_t[:, :], proj.ap[tc_, 2112:2880], [proj.u((ch, 4)), proj.u((ch, 5))], [z_t])
            c.tt("dve", y_t[:, :], y_t[:, :], yf_t[:, :], ALU.add, [y_t, yf_t], [y_t])
            c.tt("pool", yf_t[:, :].rearrange("p (h q) -> p h q", h=12), xs_tm[:, ch, :].rearrange("p (h q) -> p h q", h=12),
                 dsk[:, :].unsqueeze(2).to_broadcast([128, 12, 64]), ALU.mult, [xs_tm, dsk], [yf_t])
            c.tt("dve", y_t[:, :], y_t[:, :], yf_t[:, :], ALU.add, [y_t, yf_t], [y_t])
            c.act(z_t[:, :], z_t[:, :], AF.Silu, [z_t], [z_t])
            c.tt("dve", y_t[:, :], y_t[:, :], z_t[:, :], ALU.mult, [y_t, z_t], [y_t])
            for g in range(2):
                c.act(junk[:, :], y_t[:, g * 384:(g + 1) * 384], AF.Square, [y_t], [junk, s_], accum_out=s_[:, g:g + 1])
            c.act(s_[:, 2:4], s_[:, 0:2], AF.Sqrt, [s_, self.eps], [s_], scale=1.0 / 384, bias=self.eps[:, 0:1])
            c.op("dve", lambda e, s_=s_: e.reciprocal(out=s_[:, 4:6], in_=s_[:, 2:4]), [s_], [s_])
            for g in range(2):
                c.stt("dve", yn_t[:, g * 384:(g + 1) * 384], y_t[:, g * 384:(g + 1) * 384], s_[:, 4 + g:5 + g],
                      ng[:, g * 384:(g + 1) * 384], ALU.mult, ALU.mult, [y_t, s_, ng], [yn_t])
            yield
            pb = c.bank()
            pv = self.bf_bank(pb, 8)
            for j in range(6):
                c.tr(pv[:, j, :], yn_t[:, j * 128:(j + 1) * 128], self.ident[:, :], [yn_t, self.ident], [pb])
            yield
            for j in range(6):
                c.copy("act" if j % 2 else "dve", mixT[10 + j][:, tc_], pv[:, j, :], [pb], [mixT[10 + j]])

        def run_pair(g0, g1):
            gens = [g0, g1]
            alive = [True, True]
            while alive[0] or alive[1]:
                for i_ in (0, 1):
                    if alive[i_]:
                        try:
                            next(gens[i_])
                        except StopIteration:
                            alive[i_] = False

        old_rot = c.rot
        c.rot = [0, 1, 2, 3]
        for step in range(NT):
            run_pair(chunk_step(0, step, (step % 2), step >= NT // 2),
                     chunk_step(1, NT - 1 - step, 2 + (step % 2), step >= NT // 2))
        c.rot = old_rot
        a.release(m)

    def out_phase(self, b, l, mixT, xsrc, xkey, x1):
        c = self.c
        a = c.arena
        m = a.mark()
        GT = a.alloc([D], F32)
        self.load_bc(GT, l, b, 2)
        wv = self.w["w_out"].ap[l].rearrange("(k p) n -> p k n", p=128)
        wb = [a.alloc([16, 512], BF16) for _ in range(2)]
        xb = [a.alloc([512], F32) for _ in range(4)]
        ob = [a.alloc([512], F32) for _ in range(3)]
        its = [(n, t) for n in range(4) for t in range(NT)]

        def load(i):
            n, t = its[i]
            c.dma(xb[i % 4][:, :], xsrc[t * 128:(t + 1) * 128, n * 512:(n + 1) * 512], [xkey(t)], [xb[i % 4]])

        load(0)
        load(1)
        for i, (n, t) in enumerate(its):
            if i + 2 < len(its):
                load(i + 2)
            w_t = wb[n % 2]
            ns = slice(n * 512, (n + 1) * 512)
            if t == 0:
                c.dma(w_t[:, :, :], wv[:, :, ns], [], [w_t], q="pool")
            x_t, o_t = xb[i % 4], ob[i % 3]
            ts_ = slice(t * 128, (t + 1) * 128)
            pb = c.bank()
            for k in range(16):
                c.mm(pb[:, :], mixT[k][:, ts_], w_t[:, k, :], k == 0, k == 15, [mixT[k], w_t], [pb])
            c.tt("dve", o_t[:, :], pb[:, :], GT[:, ns], ALU.mult, [pb, GT], [o_t])
            c.tt("dve", o_t[:, :], o_t[:, :], x_t[:, :], ALU.add, [o_t, x_t], [o_t])
            c.dma(x1.ap[ts_, ns], o_t[:, :], [o_t], [x1.u((t, n))])
        a.release(m)

    def norm2_phase(self, b, l, x1, h2T_d):
        c = self.c
        a = c.arena
        m = a.mark()
        stg = [a.alloc([16, 512], BF16) for _ in range(2)]

        def dst(t):
            return stg[(t // 4) % 2], (t % 4) * 128

        def after(t):
            if t % 4 == 3:
                s_ = stg[(t // 4) % 2]
                q = t // 4
                c.dma(h2T_d.ap[:, :, q * 512:(q + 1) * 512], s_[:, :, :], [s_], [h2T_d.u(q)])

        self.norm_phase(b, l, x1.ap, lambda t: [x1.u((t, n)) for n in range(4)], "norm2_g", 4, 3, None, dst=dst, after=after)
        a.release(m)

    def ffn_phase(self, b, l, x1, h2T_d, x2):
        c = self.c
        a = c.arena
        m = a.mark()
        TB = 1024
        wgu = self.w["w_gate_up"].ap[l].rearrange("(k p) n -> p k n", p=128)
        wdn = self.w["w_down"].ap[l].rearrange("(j p) n -> p j n", p=128)
        g2 = a.alloc([16], F32)
        g2row = a.alloc([128], F32)
        c.op("pool", lambda e: e.memset(g2row[:, :], 0.0), [], [g2row])
        c.dma(g2row[0:16, :], self.mod.ap[l, b, 5 * D:6 * D].rearrange("(k p) -> k p", p=128),
              [self.mod.u((l, 20 + q)) for q in range(4)] + [g2row], [g2row])
        pg2 = c.bank()
        c.mm(pg2[:, 0:16], g2row[:, :], self.identf[:, 0:16], True, True, [g2row, self.identf], [pg2])
        c.copy("dve", g2[:, :], pg2[:, 0:16], [pg2], [g2])
        actT = a.alloc([44, TB], BF16)
        sgb = [a.alloc([512], F32) for _ in range(2)]
        xb = [a.alloc([256], F32) for _ in range(4)]
        ob = [a.alloc([256], F32) for _ in range(2)]
        gtb = [a.alloc([512], F32) for _ in range(4)]
        for tb in range(S // TB):
            m2 = a.mark()
            h2 = a.alloc([16, TB], BF16)
            for q in range(TB // 512):
                qq = tb * (TB // 512) + q
                c.dma(h2[:, :, q * 512:(q + 1) * 512], h2T_d.ap[:, :, qq * 512:(qq + 1) * 512], [h2T_d.u(qq)], [h2])
            wg = [a.alloc([16, 256], BF16) for _ in range(3)]
            wu = [a.alloc([16, 256], BF16) for _ in range(3)]
            isg = 0
            for jg in range(22):
                g_t, u_t = wg[jg % 3], wu[jg % 3]
                c.dma(g_t[:, :, :], wgu[:, :, jg * 256:(jg + 1) * 256], [], [g_t], q="pool")
                c.dma(u_t[:, :, :], wgu[:, :, FFN + jg * 256:FFN + (jg + 1) * 256], [], [u_t], q="pool")
                for jj in range(2):
                    j = jg * 2 + jj
                    for tq in range(TB // 512):
                        tqs = slice(tq * 512, (tq + 1) * 512)
                        pg = c.bank()
                        for k in range(16):
                            c.mm(pg[:, :], g_t[:, k, jj * 128:(jj + 1) * 128], h2[:, k, tqs], k == 0, k == 15, [g_t, h2], [pg])
                        pu = c.bank()
                        for k in range(16):
                            c.mm(pu[:, :], u_t[:, k, jj * 128:(jj + 1) * 128], h2[:, k, tqs], k == 0, k == 15, [u_t, h2], [pu])
                        sg = sgb[isg % 2]
                        isg += 1
                        c.act(sg[:, :], pg[:, :], AF.Silu, [pg], [sg])
                        c.tt("dve", actT[:, j, tqs], pu[:, :], sg[:, :], ALU.mult, [pu, sg], [actT])
            a.release(m2)
            m2 = a.mark()
            wd = [a.alloc([44, 256], BF16) for _ in range(3)]
            igt = 0
            dits = [(sl, tq, tt_) for sl in range(8) for tq in range(TB // 512) for tt_ in range(4)]

            def dload(i):
                sl, tq, tt_ = dits[i]
                tok0 = tb * TB + tq * 512 + tt_ * 128
                c.dma(xb[i % 4][:, 0:256], x1.ap[tok0:tok0 + 128, sl * 256:(sl + 1) * 256], [x1.u((tok0 // 128, sl // 2))], [xb[i % 4]])

            dload(0)
            dload(1)
            gts = []
            for i, (sl, tq, tt_) in enumerate(dits):
                if i + 2 < len(dits):
                    dload(i + 2)
                w_t = wd[sl % 3]
                tqs = slice(tq * 512, (tq + 1) * 512)
                if tq == 0 and tt_ == 0:
                    c.dma(w_t[:, :, :], wdn[:, :, sl * 256:(sl + 1) * 256], [], [w_t], q="pool")
                if tt_ == 0:
                    gts = []
                    for dd in range(2):
                        dblk = sl * 2 + dd
                        pb = c.bank()
                        for j in range(44):
                            c.mm(pb[:, :], w_t[:, j, dd * 128:(dd + 1) * 128], actT[:, j, tqs], j == 0, j == 43, [w_t, actT], [pb])
                        gt = gtb[igt % 4]
                        igt += 1
                        c.act(gt[:, :], pb[:, :], AF.Identity, [pb, g2], [gt], scale=g2[:, dblk:dblk + 1])
                        gts.append(gt)
                tok0 = tb * TB + tq * 512 + tt_ * 128
                tki = tok0 // 128
                x_t, o_t = xb[i % 4], ob[i % 2]
                cs = slice(sl * 256, (sl + 1) * 256)
                pt = c.bank()
                for dd in range(2):
                    c.mm(pt[:, dd * 128:(dd + 1) * 128], gts[dd][:, tt_ * 128:(tt_ + 1) * 128], self.identf[:, :], True, True,
                         [gts[dd], self.identf], [pt])
                c.tt("dve", o_t[:, 0:256], pt[:, 0:256], x_t[:, 0:256], ALU.add, [pt, x_t], [o_t])
                c.dma(x2.ap[tok0:tok0 + 128, cs], o_t[:, 0:256], [o_t], [x2.u((tki, sl))])
            a.release(m2)
        a.release(m)

    def final_phase(self, b, x2, x2key):
        c = self.c
        a = c.arena
        m = a.mark()
        G = a.alloc([D], F32)
        self.load_vec_bc(G, self.w["final_norm_g"].ap.rearrange("(o d) -> o d", o=1), D)
        xb = [a.alloc([D], F32) for _ in range(2)]
        junk = a.alloc([D], BF16)
        st = [a.alloc([4], F32) for _ in range(2)]
        for t in range(NT):
            xt, s_ = xb[t % 2], st[t % 2]
            ts_ = slice(t * 128, (t + 1) * 128)
            c.dma(xt[:, :], x2.ap[ts_, :], x2key(t), [xt])
            c.act(junk[:, :], xt[:, :], AF.Square, [xt], [junk, s_], accum_out=s_[:, 0:1])
            c.act(s_[:, 1:2], s_[:, 0:1], AF.Sqrt, [s_, self.eps], [s_], scale=1.0 / D, bias=self.eps[:, 0:1])
            c.op("dve", lambda e, s_=s_: e.reciprocal(out=s_[:, 2:3], in_=s_[:, 1:2]), [s_], [s_])
            c.stt("dve", xt[:, :], xt[:, :], s_[:, 2:3], G[:, :], ALU.mult, ALU.mult, [xt, s_, G], [xt])
            c.dma(self.out.ap[b, ts_, :], xt[:, :], [xt], [self.out.u((b, t))])
        a.release(m)

    def build(self, stop_after=None):
        c = self.c
        a = c.arena
        self.consts()
        self.mod_phase()
        for b in range(self.nseq):
            xcur = self.x.ap[b]
            xkey = lambda t: []
            for l in range(self.nlayer):
                m = a.mark()
                proj = self.scr("proj", (S, IN_COLS))
                x1 = self.scr("x1", (S, D))
                x2 = self.scr("x2", (S, D))
                h2T_d = self.scr("h2T", (128, 16, S), BF16)
                self.last = dict(proj=proj, x1=x1, x2=x2, h2T_d=h2T_d)
                mixT = [a.alloc([S], BF16) for _ in range(16)]
                self.mixT = mixT
                m1 = a.mark()
                hT = [a.alloc([16, 128], BF16) for _ in range(NT)]
                self.norm_phase(b, l, xcur, xkey, "norm1_g", 1, 0, hT)
                self.in_proj_tm(b, l, hT, proj)
                a.release(m1)
                if stop_after == "in":
                    return
                self.gqa_phase(b, l, proj, mixT)
                if stop_after == "gqa":
                    self.dump_mix(mixT, 6)
                    return
                self.mla_phase(b, l, proj, mixT)
                if stop_after == "mla":
                    self.dump_mix(mixT, 10)
                    return
                m1 = a.mark()
                self.ssd_consts()
                self.ssd_phase(b, l, proj, mixT)
                a.release(m1)
                self.dump_mix(mixT)
                if stop_after == "ssd":
                    return
                self.out_phase(b, l, mixT, xcur, xkey, x1)
                a.release(m)
                if stop_after == "out":
                    return
                self.norm2_phase(b, l, x1, h2T_d)
                self.ffn_phase(b, l, x1, h2T_d, x2)
                xcur = x2.ap
                xkey = (lambda x2: (lambda t: [x2.u((t, n)) for n in range(8)]))(x2)
                if stop_after == "ffn":
                    return
            self.final_phase(b, x2, xkey)


_PROG = None


def _get_prog():
    global _PROG
    if _PROG is None:
        p = Prog()
        p.build()
        p.c.sch.emit()
        _PROG = p
    return _PROG


def make_in_maps(inputs, ncores=NCORES):
    consts = host_consts()
    consts["convT"] = layout_conv(np.asarray(inputs["conv_w"], np.float32), np.asarray(inputs["conv_b"], np.float32))
    shared = {k: np.ascontiguousarray(np.asarray(inputs[k], np.float32)) for k in WSHAPES}
    shared.update(consts)
    x = np.asarray(inputs["x"], np.float32)
    cc = np.asarray(inputs["c"], np.float32)
    maps = []
    for i in range(ncores):
        m = dict(shared)
        m["x"] = np.ascontiguousarray(x[2 * i:2 * i + 2])
        m["cT"] = np.ascontiguousarray(cc[2 * i:2 * i + 2].reshape(2, 16, 128).transpose(2, 1, 0).reshape(128, 32))
        maps.append(m)
    return maps


def kernel(**inputs):
    p = _get_prog()
    maps = make_in_maps(inputs)
    res = run_bass_kernel_spmd(p.c.nc, maps, core_ids=list(range(NCORES)))
    out = np.concatenate([np.asarray(r["out"], np.float32) for r in res.results], axis=0)
    return out
```

```python
import numpy as np
import concourse.bass as bass
import concourse.mybir as mybir
from concourse.bass_utils import run_bass_kernel_spmd

F32 = mybir.dt.float32
BF16 = mybir.dt.bfloat16
AF = mybir.ActivationFunctionType
ALU = mybir.AluOpType
AX = mybir.AxisListType

D = 2048
S = 2048
NT = S // 128
DEPTH = 2
EPS = 1e-6
IN_COLS = 4184
FFN = 5632
NCORES = 8


class U:
    __slots__ = ("w", "r", "name", "excl")

    def __init__(self, name="", excl=False):
        self.w = None
        self.r = {}
        self.name = name
        self.excl = excl


class Op:
    __slots__ = ("eng", "fn", "deps", "dma", "sig", "seq", "dsem", "dval", "dprev")

    def __init__(self, eng, fn, deps, dma):
        self.eng = eng
        self.fn = fn
        self.deps = deps
        self.dma = dma
        self.sig = False
        self.seq = 0
        self.dsem = None
        self.dval = 0
        self.dprev = None


ENGS = ("pe", "act", "dve", "pool", "sp")


class Sched:
    def __init__(self, nc, nds=8):
        self.nc = nc
        self.ops = []
        self.nds = nds

    def add(self, eng, fn, reads=(), writes=(), dma=False):
        i = len(self.ops)
        deps = set()
        xr = [u for u in reads if u.excl]
        if xr:
            reads = [u for u in reads if not u.excl]
            writes = list(writes) + [u for u in xr if u not in writes]
        for u in reads:
            if u.w is not None:
                deps.add(u.w)
        for u in writes:
            if u.w is not None:
                deps.add(u.w)
            deps.update(u.r.values())
        for u in reads:
            u.r[eng if not dma else ("dma", eng, i)] = i
        for u in writes:
            u.w = i
            u.r = {}
        deps.discard(i)
        self.ops.append(Op(eng, fn, deps, dma))
        return i

    def emit(self):
        nc = self.nc
        ops = self.ops
        for o in ops:
            real = []
            for d in o.deps:
                od = ops[d]
                if (not od.dma) and od.eng == o.eng and o.eng == "pe" and not o.dma:
                    continue
                real.append(d)
                if not od.dma:
                    od.sig = True
            o.deps = sorted(real)
        cnt = {e: 0 for e in ENGS}
        dcnt = {e: 0 for e in ENGS}
        for o in ops:
            if o.dma:
                n = dcnt[o.eng]
                dcnt[o.eng] += 1
                o.dsem = (o.eng, n % self.nds)
                o.dval = 16 * (n // self.nds + 1)
            elif o.sig:
                cnt[o.eng] += 1
                o.seq = cnt[o.eng]
        self.counts = (cnt, dcnt)
        sems = {e: nc.alloc_semaphore("s_" + e) for e in ENGS}
        dsems = {}
        for e in ENGS:
            if dcnt[e]:
                for k in range(self.nds):
                    dsems[(e, k)] = nc.alloc_semaphore("d_%s%d" % (e, k))
        per = {e: [o for o in ops if o.eng == e] for e in ENGS}

        def run(ename, eng):
            known = {}
            for o in per[ename]:
                if o.dma and o.dval > 16:
                    key = ("d",) + o.dsem
                    v = o.dval - 16
                    if known.get(key, 0) < v:
                        eng.wait_ge(dsems[o.dsem], v)
                        known[key] = v
                for d in o.deps:
                    od = ops[d]
                    if od.dma:
                        key = ("d",) + od.dsem
                        if known.get(key, 0) < od.dval:
                            eng.wait_ge(dsems[od.dsem], od.dval)
                            known[key] = od.dval
                    else:
                        key = od.eng
                        if known.get(key, 0) < od.seq:
                            eng.wait_ge(sems[od.eng], od.seq)
                            known[key] = od.seq
                ins = o.fn(eng)
                if o.dma:
                    ins.then_inc(dsems[o.dsem], 16)
                elif o.sig:
                    ins.then_inc(sems[ename], 1)
            for k in range(self.nds):
                if (ename, k) in dsems:
                    n = dcnt[ename]
                    last = (n - 1 - k) // self.nds + 1 if n - 1 >= k else 0
                    if last > 0:
                        eng.wait_ge(dsems[(ename, k)], 16 * last)

        with nc.Block() as block:
            @block.tensor
            def _(e):
                run("pe", e)

            @block.scalar
            def _(e):
                run("act", e)

            @block.vector
            def _(e):
                run("dve", e)

            @block.gpsimd
            def _(e):
                run("pool", e)

            @block.sync
            def _(e):
                run("sp", e)


def _units(xs):
    out = []
    for x in xs:
        if x is None:
            continue
        if isinstance(x, U):
            out.append(x)
        elif isinstance(x, (list, tuple)):
            out.extend(_units(x))
        else:
            out.extend(x.u)
    return out


class T:
    def __init__(self, ap, units):
        self.ap = ap
        self.u = units

    def __getitem__(self, k):
        return self.ap[k]


class Arena:
    def __init__(self, nc, nbytes):
        self.t = nc.alloc_sbuf_tensor("arena", [128, nbytes // 4], F32)
        self.top = 0
        self.nbytes = nbytes
        self.recs = []

    def alloc(self, free_shape, dtype):
        esz = 2 if dtype == BF16 else 4
        n = int(np.prod(free_shape))
        nb = (n * esz + 63) // 64 * 64
        off = self.top
        assert off + nb <= self.nbytes, ("arena overflow", off, nb, self.nbytes)
        self.top = off + nb
        ap = self.t[:, off // 4:(off + nb) // 4]
        if esz == 2:
            ap = ap.bitcast(BF16)
        ap = ap[:, 0:n]
        if len(free_shape) == 2:
            ap = ap.rearrange("p (a b) -> p a b", a=free_shape[0])
        elif len(free_shape) == 3:
            ap = ap.rearrange("p (a b c) -> p a b c", a=free_shape[0], b=free_shape[1])
        u = U()
        end = off + nb
        keep = []
        for (o, e, uo) in self.recs:
            if o < end and off < e:
                if uo.w is not None:
                    u.r[("inh", uo.w)] = uo.w
                for v in uo.r.values():
                    u.r[("inh", v)] = v
                if off <= o and e <= end:
                    continue
            keep.append((o, e, uo))
        keep.append((off, end, u))
        self.recs = keep
        return T(ap, [u])

    def mark(self):
        return self.top

    def release(self, m):
        self.top = m


class DT:
    def __init__(self, ap):
        self.ap = ap
        self.us = {}

    def u(self, key=0):
        if key not in self.us:
            self.us[key] = U()
        return self.us[key]

    def __getitem__(self, k):
        return self.ap[k]


class Ctx:
    def __init__(self):
        self.nc = bass.Bass("TRN2", target_bir_lowering=False)
        self.sch = Sched(self.nc)
        self.arena = Arena(self.nc, 207 * 1024)
        self.banks = []
        for i in range(8):
            t = self.nc.alloc_psum_tensor("pb%d" % i, [128, 512], F32)
            self.banks.append(T(t[:, :], [U("pb%d" % i, excl=True)]))
        self.bi = 0
        self.rot = [0, 1, 2, 3, 4, 5]
        self.ins = {}
        self.nscr = 0

    def bank(self):
        b = self.banks[self.rot[self.bi % len(self.rot)]]
        self.bi += 1
        return b

    def dram_in(self, name, shape, dt=F32):
        t = self.nc.dram_tensor(name, list(shape), dt, kind="ExternalInput")
        d = DT(t.ap())
        self.ins[name] = d
        return d

    def dram_out(self, name, shape, dt=F32):
        t = self.nc.dram_tensor(name, list(shape), dt, kind="ExternalOutput")
        return DT(t.ap())

    def scratch(self, shape, dt=F32, name=None):
        self.nscr += 1
        t = self.nc.dram_tensor(name or ("scr%d" % self.nscr), list(shape), dt, kind="Internal")
        return DT(t.ap())

    def dma(self, out, in_, reads, writes, q="sp", **kw):
        self.sch.add(q, lambda e: e.dma_start(out=out, in_=in_, **kw), _units(reads), _units(writes), dma=True)

    def mm(self, out, lhsT, rhs, start, stop, reads, writes):
        self.sch.add("pe", lambda e: e.matmul(out, lhsT, rhs, start=start, stop=stop), _units(reads), _units(writes))

    def tr(self, out, in_, ident, reads, writes):
        self.sch.add("pe", lambda e: e.transpose(out, in_, ident), _units(reads), _units(writes))

    def op(self, eng, fn, reads, writes):
        self.sch.add(eng, fn, _units(reads), _units(writes))

    def act(self, out, in_, func, reads, writes, eng="act", **kw):
        self.sch.add("act", lambda e: e.activation(out=out, in_=in_, func=func, **kw), _units(reads), _units(writes))

    def copy(self, eng, out, in_, reads, writes):
        if eng == "act":
            self.sch.add("act", lambda e: e.copy(out=out, in_=in_), _units(reads), _units(writes))
        else:
            self.sch.add(eng, lambda e: e.tensor_copy(out=out, in_=in_), _units(reads), _units(writes))

    def tt(self, eng, out, in0, in1, op, reads, writes):
        self.sch.add(eng, lambda e: e.tensor_tensor(out=out, in0=in0, in1=in1, op=op), _units(reads), _units(writes))

    def ts(self, eng, out, in0, s1, s2, op0, op1, reads, writes, **kw):
        if op1 is None:
            self.sch.add(eng, lambda e: e.tensor_scalar(out=out, in0=in0, scalar1=s1, scalar2=s2, op0=op0, **kw),
                         _units(reads), _units(writes))
        else:
            self.sch.add(eng, lambda e: e.tensor_scalar(out=out, in0=in0, scalar1=s1, scalar2=s2, op0=op0, op1=op1, **kw),
                         _units(reads), _units(writes))

    def stt(self, eng, out, in0, scalar, in1, op0, op1, reads, writes):
        self.sch.add(eng, lambda e: e.scalar_tensor_tensor(out=out, in0=in0, scalar=scalar, in1=in1, op0=op0, op1=op1),
                     _units(reads), _units(writes))


WSHAPES = {
    "w_ada": (DEPTH, D, 6 * D), "b_ada": (DEPTH, 6 * D), "norm1_g": (DEPTH, D), "norm2_g": (DEPTH, D),
    "w_in": (DEPTH, D, IN_COLS), "q_norm_g": (DEPTH, 128), "k_norm_g": (DEPTH, 128),
    "mla_q_norm_g": (DEPTH, 512), "w_uq": (DEPTH, 512, 768), "mla_kv_norm_g": (DEPTH, 256),
    "w_ukv": (DEPTH, 256, 1024), "conv_w": (DEPTH, 5, 1280), "conv_b": (DEPTH, 1280),
    "dt_bias": (DEPTH, 2, 12), "a_log": (DEPTH, 2, 12), "d_skip": (DEPTH, 12), "ssd_norm_g": (DEPTH, 768),
    "w_out": (DEPTH, D, D), "w_gate_up": (DEPTH, D, 2 * FFN), "w_down": (DEPTH, FFN, D),
    "final_norm_g": (D,),
}


def host_consts():
    import ml_dtypes
    c = {}
    c["ident"] = np.eye(128, dtype=np.float32).astype(ml_dtypes.bfloat16)
    c["identf"] = np.eye(128, dtype=np.float32)
    pos = np.arange(S)
    row = (pos // 64).astype(np.float32)
    col = (pos % 64).astype(np.float32)

    def tab(rot_dim):
        axis_dim = rot_dim // 2
        inv = np.power(np.float32(10000.0), -np.arange(0, axis_dim, 2, dtype=np.float32) / np.float32(axis_dim)).astype(np.float32)
        ar = (row[:, None] * inv[None, :]).astype(np.float32)
        ac = (col[:, None] * inv[None, :]).astype(np.float32)
        cr, sr, cc_, sc = np.cos(ar), np.sin(ar), np.cos(ac), np.sin(ac)
        cosx = np.concatenate([cr, cr, cc_, cc_], axis=1)
        sinx = np.concatenate([-sr, sr, -sc, sc], axis=1)
        return np.stack([cosx, sinx], axis=1).astype(np.float32)

    c["ropeA"] = tab(128)
    c["ropeB"] = tab(64)
    k = np.arange(128)
    c["triF"] = (k[:, None] <= k[None, :]).astype(np.float32)
    c["triR"] = (k[:, None] >= k[None, :]).astype(np.float32)
    sel = np.zeros((128, 24, 128), np.float32)
    for j in range(24):
        sel[j, j, :] = 1.0
    c["sel"] = sel
    mf = np.where(k[None, :] >= k[:, None], 0.0, -30000.0).astype(np.float32)
    mb = np.where(k[None, :] <= k[:, None], 0.0, -30000.0).astype(np.float32)
    c["maskF"] = np.tile(mf, (1, 3)).astype(ml_dtypes.bfloat16)
    c["maskB"] = np.tile(mb, (1, 3)).astype(ml_dtypes.bfloat16)
    return c


def layout_conv(conv_w, conv_b):
    L = conv_w.shape[0]
    o = np.zeros((L, 128, 10, 6), np.float32)
    o[:, :, :, 0:5] = conv_w.reshape(L, 5, 10, 128).transpose(0, 3, 2, 1)
    o[:, :, :, 5] = conv_b.reshape(L, 10, 128).transpose(0, 2, 1)
    return o


class Prog:
    def __init__(self, nseq=2, nlayer=DEPTH, dbg=()):
        self.c = Ctx()
        self.nseq = nseq
        self.nlayer = nlayer
        self.dbg = set(dbg)
        self.dbg_outs = {}
        c = self.c
        self.x = c.dram_in("x", (2, S, D))
        self.cT = c.dram_in("cT", (128, 32))
        self.w = {k: c.dram_in(k, v) for k, v in WSHAPES.items()}
        self.k_ident = c.dram_in("ident", (128, 128), BF16)
        self.k_identf = c.dram_in("identf", (128, 128))
        self.k_ropeA = c.dram_in("ropeA", (S, 2, 128))
        self.k_ropeB = c.dram_in("ropeB", (S, 2, 64))
        self.k_triF = c.dram_in("triF", (128, 128))
        self.k_triR = c.dram_in("triR", (128, 128))
        self.k_sel = c.dram_in("sel", (128, 24, 128))
        self.k_maskF = c.dram_in("maskF", (128, 384), BF16)
        self.k_maskB = c.dram_in("maskB", (128, 384), BF16)
        self.k_convT = c.dram_in("convT", (DEPTH, 128, 10, 6))
        self.out = c.dram_out("out", (2, S, D))

    def scr(self, name, shape, dt=F32):
        if name in self.dbg and name not in self.dbg_outs:
            return self.dbg_out(name, shape, dt)
        return self.c.scratch(shape, dt)

    def dump_mix(self, mixT, n=16):
        if "mix" in self.dbg and "mix" not in self.dbg_outs:
            d = self.dbg_out("mix", (16, 128, S), BF16)
            for k in range(n):
                self.c.dma(d.ap[k], mixT[k][:, :], [mixT[k]], [d.u(k)])

    def dbg_out(self, name, shape, dt=F32):
        d = self.c.dram_out("dbg_" + name, shape, dt)
        self.dbg_outs[name] = d
        return d

    def consts(self):
        c = self.c
        a = c.arena
        self.ident = a.alloc([128], BF16)
        c.dma(self.ident[:, :], self.k_ident[:, :], [], [self.ident])
        self.identf = a.alloc([128], F32)
        c.dma(self.identf[:, :], self.k_identf[:, :], [], [self.identf])
        self.eps = a.alloc([1], F32)
        self.one1 = a.alloc([1], F32)
        c.op("pool", lambda e: e.memset(self.one1[:, :], 1.0), [], [self.one1])
        self.ones = a.alloc([128], BF16)
        c.op("pool", lambda e: e.memset(self.ones[:, :], 1.0), [], [self.ones])
        c.op("pool", lambda e: e.memset(self.eps[:, :], EPS), [], [self.eps])

    def mod_phase(self):
        c = self.c
        a = c.arena
        m = a.mark()
        self.mod = c.scratch((DEPTH, 2, 6 * D), name="mod")
        ct = a.alloc([32], F32)
        c.dma(ct[:, :], self.cT[:, :], [], [ct])
        sg = a.alloc([32], F32)
        c.act(sg[:, :], ct[:, :], AF.Sigmoid, [ct], [sg])
        c.tt("dve", ct[:, :], ct[:, :], sg[:, :], ALU.mult, [ct, sg], [ct])
        ctp = a.alloc([16, 128], BF16)
        c.op("pool", lambda e: e.memset(ctp[:, :, :], 0.0), [], [ctp])
        c.copy("dve", ctp[:, :, 0:2], ct[:, :].rearrange("p (k b) -> p k b", b=2), [ct, ctp], [ctp])
        wb = [a.alloc([16, 512], BF16) for _ in range(3)]
        bb = [a.alloc([512], F32) for _ in range(2)]
        ob = [a.alloc([512], F32) for _ in range(2)]
        NB = 6 * D // 512
        i = 0
        for l in range(self.nlayer):
            wv = self.w["w_ada"].ap[l].rearrange("(k p) n -> p k n", p=128)
            for nb in range(NB):
                w_t, b_t, o_t = wb[i % 3], bb[i % 2], ob[i % 2]
                i += 1
                cs = slice(nb * 512, (nb + 1) * 512)
                c.dma(w_t[:, :, :], wv[:, :, cs], [], [w_t], q="pool")
                c.dma(b_t[0:2, :], self.w["b_ada"].ap[l:l + 1, cs].to_broadcast([2, 512]), [], [b_t])
                pb = c.bank()
                for k in range(16):
                    c.mm(pb[:, :], ctp[:, k, :], w_t[:, k, :], k == 0, k == 15, [ctp, w_t], [pb])
                c.tt("dve", o_t[0:2, :], pb[0:2, :], b_t[0:2, :], ALU.add, [pb, b_t], [o_t])
                c.dma(self.mod.ap[l, :, cs], o_t[0:2, :], [o_t], [self.mod.u((l, nb))])
        a.release(m)

    def load_bc(self, tile, l, b, j):
        c = self.c
        src = self.mod.ap[l, b:b + 1, j * D:(j + 1) * D].to_broadcast([128, D])
        c.dma(tile[:, :], src, [self.mod.u((l, 4 * j + q)) for q in range(4)], [tile])

    def load_vec_bc(self, tile, dram_ap_row, n):
        c = self.c
        c.dma(tile[:, 0:n], dram_ap_row.to_broadcast([128, n]), [], [tile])

    def bf_bank(self, pb, n):
        return pb.ap.bitcast(BF16).rearrange("p (a b) -> p a b", a=8)[:, 0:n, :]

    def rstd_from_ss(self, rstd, ss, n):
        c = self.c
        c.ts("dve", rstd, ss, 1.0 / n, EPS, ALU.mult, ALU.add, [], [])

    def norm_phase(self, b, l, xsrc, xkey, gname, jscale, jshift, hT, dst=None, after=None):
        c = self.c
        a = c.arena
        m = a.mark()
        G = a.alloc([D], F32)
        SH = a.alloc([D], F32)
        tmpg = a.alloc([D], F32)
        self.load_bc(G, l, b, jscale)
        self.load_vec_bc(tmpg, self.w[gname].ap[l:l + 1, :], D)
        c.stt("dve", G[:, :], G[:, :], 1.0, tmpg[:, :], ALU.add, ALU.mult, [G, tmpg], [G])
        self.load_bc(SH, l, b, jshift)
        xb = [a.alloc([D], F32) for _ in range(3)]
        junk = a.alloc([D], BF16)
        hb = [a.alloc([D], BF16) for _ in range(2)]
        st = [a.alloc([4], F32) for _ in range(2)]

        def load(t):
            c.dma(xb[t % 3][:, :], xsrc[t * 128:(t + 1) * 128, :], [xkey(t)], [xb[t % 3]])

        def stage_a(t):
            xt, ht, s_ = xb[t % 3], hb[t % 2], st[t % 2]
            c.act(junk[:, :], xt[:, :], AF.Square, [xt], [junk, s_], accum_out=s_[:, 0:1])
            c.act(s_[:, 1:2], s_[:, 0:1], AF.Sqrt, [s_, self.eps], [s_], scale=1.0 / D, bias=self.eps[:, 0:1])
            c.op("dve", lambda e, s_=s_: e.reciprocal(out=s_[:, 2:3], in_=s_[:, 1:2]), [s_], [s_])
            c.stt("dve", xt[:, :], xt[:, :], s_[:, 2:3], G[:, :], ALU.mult, ALU.mult, [xt, s_, G], [xt])
            c.tt("dve", ht[:, :], xt[:, :], SH[:, :], ALU.add, [xt, SH], [ht])

        def stage_b(t):
            ht = hb[t % 2]
            for half in range(2):
                pb = c.bank()
                pv = self.bf_bank(pb, 8)
                for k in range(8):
                    kc = half * 8 + k
                    c.tr(pv[:, k, :], ht[:, kc * 128:(kc + 1) * 128], self.ident[:, :], [ht, self.ident], [pb])
                if dst is None:
                    c.copy("act" if half == 0 else "dve", hT[t][:, half * 8:(half + 1) * 8, :], pv, [pb], [hT[t]])
                else:
                    dt_, off = dst(t)
                    c.copy("act" if half == 0 else "dve", dt_[:, half * 8:(half + 1) * 8, off:off + 128], pv, [pb], [dt_])
            if after is not None:
                after(t)

        load(0)
        load(1)
        for t in range(NT):
            if t + 2 < NT:
                load(t + 2)
            stage_a(t)
            if t >= 1:
                stage_b(t - 1)
        stage_b(NT - 1)
        a.release(m)

    def load_w_bf16(self, wt, src):
        self.c.dma(wt, src, [], [], q="pool")

    def in_proj_tm(self, b, l, hT, proj):
        c = self.c
        a = c.arena
        m = a.mark()
        wv = self.w["w_in"].ap[l].rearrange("(k p) n -> p k n", p=128)
        blocks = [(j * 512, j * 512, 512) for j in range(8)] + [(4096, 4096, 88)]
        wb = [a.alloc([16, 512], BF16) for _ in range(2)]
        ob = [a.alloc([512], F32) for _ in range(3)]
        i = 0
        for bi, (wc, pc, n) in enumerate(blocks):
            w_t = wb[bi % 2]
            c.dma(w_t[:, :, 0:n], wv[:, :, wc:wc + n], [], [w_t], q="pool")
            for t in range(NT):
                pb = c.bank()
                for k in range(16):
                    c.mm(pb[:, 0:n], hT[t][:, k, :], w_t[:, k, 0:n], k == 0, k == 15, [hT[t], w_t], [pb])
                o_t = ob[i % 3]
                i += 1
                c.copy("act" if i % 2 else "dve", o_t[:, 0:n], pb[:, 0:n], [pb], [o_t])
                c.dma(proj.ap[t * 128:(t + 1) * 128, pc:pc + n], o_t[:, 0:n], [o_t], [proj.u((t, bi))])
        a.release(m)

    def rms_heads(self, x3, nh, hd, ss, junk3, eng="dve"):
        pass

    def rope_tm(self, x2d, tab, nh, hd, T1, T2, SW, out2d, units_r, units_w, out3=False):
        c = self.c
        q = hd // 4
        n = nh * hd
        x3 = x2d.rearrange("p (h d) -> p h d", h=nh)
        xq = x2d.rearrange("p (g b f) -> p g b f", b=2, f=q)
        swq = SW[:, 0:n].rearrange("p (g b f) -> p g b f", b=2, f=q)
        sw3 = SW[:, 0:n].rearrange("p (h d) -> p h d", h=nh)
        t13 = T1[:, 0:n].rearrange("p (h d) -> p h d", h=nh)
        t23 = T2[:, 0:n].rearrange("p (h d) -> p h d", h=nh)
        cosb = tab[:, 0:1, :].to_broadcast([128, nh, hd])
        sinb = tab[:, 1:2, :].to_broadcast([128, nh, hd])
        c.copy("act", swq[:, :, 0, :], xq[:, :, 1, :], units_r, [SW])
        c.copy("act", swq[:, :, 1, :], xq[:, :, 0, :], units_r, [SW])
        c.tt("dve", t13, x3, cosb, ALU.mult, units_r, [T1])
        c.tt("dve", t23, sw3, sinb, ALU.mult, [SW] + units_r, [T2])
        if out3:
            c.tt("dve", out2d, t13, t23, ALU.add, [T1, T2], units_w)
        else:
            c.tt("dve", out2d, T1[:, 0:n], T2[:, 0:n], ALU.add, [T1, T2], units_w)

    def attn_core(self, heads, mixT, scale):
        c = self.c
        a = c.arena
        m = a.mark()
        LOOK = 2
        pt = [a.alloc([512], BF16) for _ in range(4)]
        rec = [a.alloc([512], F32) for _ in range(2)]
        old_rot = c.rot
        c.rot = [0, 1, 2, 3]
        accs = [(c.banks[4], c.banks[5]), (c.banks[6], c.banks[7])]
        iters = [(hd, qt, kc) for hd in heads for qt in range(4) for kc in range(16)]
        sbs = {}

        def emit_s(i):
            hd, qt, kc = iters[i]
            sb = c.bank()
            np_ = len(hd["kparts"])
            for pi, (kf, qf) in enumerate(hd["kparts"]):
                c.mm(sb[:, :], kf(kc), qf(qt), pi == 0, pi == np_ - 1, hd["reads"], [sb])
            sbs[i] = sb

        def emit_rest(i):
            hd, qt, kc = iters[i]
            sb = sbs.pop(i)
            accO, accD = accs[(i // 16) % 2]
            p_t = pt[i % 4]
            c.act(p_t[:, :], sb[:, :], AF.Exp, [sb], [p_t], scale=scale)
            c.mm(accO[:, :], hd["v_fn"](kc), p_t[:, :], kc == 0, kc == 15, [p_t] + hd["reads"], [accO])
            c.mm(accD[:, :], self.ones[:, :], p_t[:, :], kc == 0, kc == 15, [p_t, self.ones], [accD])
            if kc == 15:
                r_t = rec[(i // 16) % 2]
                c.op("dve", lambda e, r_t=r_t, accD=accD: e.reciprocal(out=r_t[:, :], in_=accD[:, :]), [accD], [r_t])
                mo = mixT[hd["out"]]
                c.tt("dve", mo[:, qt * 512:(qt + 1) * 512], accO[:, :], r_t[:, :], ALU.mult, [accO, r_t], [mo])

        n = len(iters)
        for i in range(n + LOOK):
            if i < n:
                emit_s(i)
            if i - LOOK >= 0:
                emit_rest(i - LOOK)
        c.rot = old_rot
        a.release(m)

    def gqa_phase(self, b, l, proj, mixT):
        c = self.c
        a = c.arena
        m = a.mark()
        qk = a.alloc([8, S], BF16)
        vv = a.alloc([NT, 256], BF16)
        m2 = a.mark()
        gain = a.alloc([8, 128], F32)
        for h in range(8):
            nm = "q_norm_g" if h < 6 else "k_norm_g"
            c.dma(gain[:, h, :], self.w[nm].ap[l:l + 1, :].to_broadcast([128, 128]), [], [gain])
        xb = [a.alloc([1280], F32) for _ in range(2)]
        sqs = [a.alloc([1024], F32) for _ in range(2)]
        st = [a.alloc([32], F32) for _ in range(2)]
        T1s = [a.alloc([1024], F32) for _ in range(2)]
        T2s = [a.alloc([1024], F32) for _ in range(2)]
        SWs = [a.alloc([1024], F32) for _ in range(2)]
        ob = [a.alloc([1024], BF16) for _ in range(2)]
        tb = [a.alloc([256], F32) for _ in range(2)]
        def stage_a(t):
            xt, s_, o_t, tab = xb[t % 2], st[t % 2], ob[t % 2], tb[t % 2]
            sq, T1, T2, SW = sqs[t % 2], T1s[t % 2], T2s[t % 2], SWs[t % 2]
            c.dma(xt[:, :], proj.ap[t * 128:(t + 1) * 128, 0:1280], [proj.u((t, j)) for j in range(3)], [xt])
            c.dma(tab[:, :], self.k_ropeA.ap[t * 128:(t + 1) * 128].rearrange("p a f -> p (a f)"), [], [tab])
            x3 = xt[:, 0:1024].rearrange("p (h d) -> p h d", h=8)
            c.act(sq[:, :], xt[:, 0:1024], AF.Square, [xt], [sq])
            c.op("dve", lambda e, s_=s_, sq=sq: e.tensor_reduce(out=s_[:, 0:8], in_=sq[:, :].rearrange("p (h d) -> p h d", h=8),
                                                         axis=AX.X, op=ALU.add), [sq], [s_])
            c.act(s_[:, 8:16], s_[:, 0:8], AF.Sqrt, [s_, self.eps], [s_], scale=1.0 / 128, bias=self.eps[:, 0:1])
            c.op("dve", lambda e, s_=s_: e.reciprocal(out=s_[:, 16:24], in_=s_[:, 8:16]), [s_], [s_])
            c.tt("dve", x3, x3, s_[:, 16:24].unsqueeze(2).to_broadcast([128, 8, 128]), ALU.mult, [xt, s_], [xt])
            c.tt("dve", x3, x3, gain[:, :, :], ALU.mult, [xt, gain], [xt])
            self.rope_tm(xt[:, 0:1024], tab[:, :].rearrange("p (a f) -> p a f", a=2), 8, 128, T1, T2, SW, o_t[:, :], [xt, tab], [o_t])
            c.copy("act", vv[:, t, :], xt[:, 1024:1280], [xt], [vv])

        def stage_b(t):
            o_t = ob[t % 2]
            pb = c.bank()
            pv = self.bf_bank(pb, 8)
            for h in range(8):
                c.tr(pv[:, h, :], o_t[:, h * 128:(h + 1) * 128], self.ident[:, :], [o_t, self.ident], [pb])
            c.copy("act", qk[:, :, t * 128:(t + 1) * 128], pv, [pb], [qk])

        stage_a(0)
        for t in range(NT):
            if t + 1 < NT:
                stage_a(t + 1)
            stage_b(t)
        a.release(m2)
        heads = []
        for g in range(2):
            for r in range(3):
                h = g * 3 + r
                heads.append(dict(
                    kparts=[(lambda kc, g=g: qk[:, 6 + g, kc * 128:(kc + 1) * 128],
                             lambda qt, h=h: qk[:, h, qt * 512:(qt + 1) * 512])],
                    v_fn=lambda kc, g=g: vv[:, kc, g * 128:(g + 1) * 128],
                    reads=[qk, vv], out=h))
        self.attn_core(heads, mixT, 128 ** -0.5)
        a.release(m)

    def mla_phase(self, b, l, proj, mixT):
        c = self.c
        a = c.arena
        m = a.mark()
        qn = a.alloc([4, S], BF16)
        qp = a.alloc([4, S], BF16)
        kn = a.alloc([4, S], BF16)
        kp = a.alloc([S], BF16)
        vb = a.alloc([NT, 512], BF16)
        m2 = a.mark()
        cqT = a.alloc([4, S], BF16)
        ckvT = a.alloc([2, S], BF16)
        wuq = a.alloc([4, 768], BF16)
        wukv = a.alloc([2, 1024], BF16)
        c.dma(wuq[:, :, :], self.w["w_uq"].ap[l].rearrange("(k p) n -> p k n", p=128), [], [wuq], q="pool")
        c.dma(wukv[:, :, :], self.w["w_ukv"].ap[l].rearrange("(k p) n -> p k n", p=128), [], [wukv], q="pool")
        gq = a.alloc([512], F32)
        gkv = a.alloc([256], F32)
        self.load_vec_bc(gq, self.w["mla_q_norm_g"].ap[l:l + 1, :], 512)
        self.load_vec_bc(gkv, self.w["mla_kv_norm_g"].ap[l:l + 1, :], 256)
        xb = [a.alloc([832], F32) for _ in range(2)]
        junks = [a.alloc([512], BF16) for _ in range(2)]
        st = [a.alloc([8], F32) for _ in range(2)]
        TK = [[a.alloc([64], F32) for _ in range(3)] for _ in range(2)]
        TQ = [[a.alloc([256], F32) for _ in range(3)] for _ in range(2)]
        ob = [a.alloc([896], BF16) for _ in range(2)]
        tb = [a.alloc([128], F32) for _ in range(2)]
        qpb = [a.alloc([256], F32) for _ in range(2)]
        qpo = [a.alloc([4, 128], BF16) for _ in range(2)]
        for o_ in ob:
            c.op("pool", lambda e, o_=o_: e.memset(o_[:, 832:896], 0.0), [o_], [o_])
        for o_ in qpo:
            c.op("pool", lambda e, o_=o_: e.memset(o_[:, :, :], 0.0), [o_], [o_])
        wqpe = a.alloc([4, 256], BF16)
        wv = a.alloc([2, 512], BF16)
        for h in range(4):
            c.copy("dve", wqpe[:, :, h * 64:(h + 1) * 64], wuq[:, :, h * 192 + 128:h * 192 + 192], [wuq, wqpe], [wqpe])
            c.copy("dve", wv[:, :, h * 128:(h + 1) * 128], wukv[:, :, h * 256 + 128:h * 256 + 256], [wukv, wv], [wv])
        def stage_a(t):
            xt, s_, o_t, tab = xb[t % 2], st[t % 2], ob[t % 2], tb[t % 2]
            junk = junks[t % 2]
            c.dma(xt[:, :], proj.ap[t * 128:(t + 1) * 128, 1280:2112], [proj.u((t, j)) for j in (2, 3, 4)], [xt])
            c.dma(tab[:, :], self.k_ropeB.ap[t * 128:(t + 1) * 128].rearrange("p a f -> p (a f)"), [], [tab])
            c.act(junk[:, 0:512], xt[:, 0:512], AF.Square, [xt], [junk, s_], accum_out=s_[:, 0:1])
            c.act(junk[:, 0:256], xt[:, 512:768], AF.Square, [xt], [junk, s_], accum_out=s_[:, 1:2])
            c.ts("dve", s_[:, 0:1], s_[:, 0:1], 0.5, None, ALU.mult, None, [s_], [s_])
            c.act(s_[:, 2:4], s_[:, 0:2], AF.Sqrt, [s_, self.eps], [s_], scale=1.0 / 256, bias=self.eps[:, 0:1])
            c.op("dve", lambda e, s_=s_: e.reciprocal(out=s_[:, 4:6], in_=s_[:, 2:4]), [s_], [s_])
            c.stt("dve", o_t[:, 0:512], xt[:, 0:512], s_[:, 4:5], gq[:, 0:512], ALU.mult, ALU.mult, [xt, s_, gq], [o_t])
            c.stt("dve", o_t[:, 512:768], xt[:, 512:768], s_[:, 5:6], gkv[:, 0:256], ALU.mult, ALU.mult, [xt, s_, gkv], [o_t])
            tabv = tab[:, :].rearrange("p (a f) -> p a f", a=2)
            T1, T2, SW = TK[t % 2]
            self.rope_tm(xt[:, 768:832], tabv, 1, 64, T1, T2, SW, o_t[:, 768:832], [xt, tab], [o_t])

        def stage_b(t):
            o_t, tab = ob[t % 2], tb[t % 2]
            tabv = tab[:, :].rearrange("p (a f) -> p a f", a=2)
            pb = c.bank()
            pv = self.bf_bank(pb, 8)
            for k in range(6):
                c.tr(pv[:, k, :], o_t[:, k * 128:(k + 1) * 128], self.ident[:, :], [o_t, self.ident], [pb])
            c.tr(pv[:, 6, :], o_t[:, 768:896], self.ident[:, :], [o_t, self.ident], [pb])
            tc_ = slice(t * 128, (t + 1) * 128)
            c.copy("act", cqT[:, :, tc_], pv[:, 0:4, :], [pb], [cqT])
            c.copy("dve", ckvT[:, :, tc_], pv[:, 4:6, :], [pb], [ckvT])
            c.copy("act", kp[:, tc_], pv[:, 6, :], [pb], [kp])
            pq = c.bank()
            for k in range(4):
                c.mm(pq[:, 0:256], cqT[:, k, tc_], wqpe[:, k, :], k == 0, k == 3, [cqT, wqpe], [pq])
            qb_, qo_ = qpb[t % 2], qpo[t % 2]
            c.copy("act", qb_[:, :], pq[:, 0:256], [pq], [qb_])
            T1, T2, SW = TQ[t % 2]
            self.rope_tm(qb_[:, :], tabv, 4, 64, T1, T2, SW, qo_[:, :, 0:64], [qb_, tab], [qo_], out3=True)
            pb2 = c.bank()
            pv2 = self.bf_bank(pb2, 8)
            for h in range(4):
                c.tr(pv2[:, h, :], qo_[:, h, :], self.ident[:, :], [qo_, self.ident], [pb2])
            c.copy("dve", qp[:, :, tc_], pv2[:, 0:4, :], [pb2], [qp])
            pvb = c.bank()
            for k in range(2):
                c.mm(pvb[:, :], ckvT[:, k, tc_], wv[:, k, :], k == 0, k == 1, [ckvT, wv], [pvb])
            c.copy("act", vb[:, t, :], pvb[:, :], [pvb], [vb])
        stage_a(0)
        for t in range(NT):
            if t + 1 < NT:
                stage_a(t + 1)
            stage_b(t)
        i = 0
        for h in range(4):
            for tq in range(4):
                ts_ = slice(tq * 512, (tq + 1) * 512)
                p1 = c.bank()
                for k in range(4):
                    c.mm(p1[:, :], wuq[:, k, h * 192:h * 192 + 128], cqT[:, k, ts_], k == 0, k == 3, [wuq, cqT], [p1])
                c.copy("act", qn[:, h, ts_], p1[:, :], [p1], [qn])
                p2 = c.bank()
                for k in range(2):
                    c.mm(p2[:, :], wukv[:, k, h * 256:h * 256 + 128], ckvT[:, k, ts_], k == 0, k == 1, [wukv, ckvT], [p2])
                c.copy("dve", kn[:, h, ts_], p2[:, :], [p2], [kn])
        a.release(m2)
        heads = []
        for h in range(4):
            heads.append(dict(
                kparts=[(lambda kc, h=h: kn[:, h, kc * 128:(kc + 1) * 128], lambda qt, h=h: qn[:, h, qt * 512:(qt + 1) * 512]),
                        (lambda kc: kp[:, kc * 128:(kc + 1) * 128], lambda qt, h=h: qp[:, h, qt * 512:(qt + 1) * 512])],
                v_fn=lambda kc, h=h: vb[:, kc, h * 128:(h + 1) * 128],
                reads=[qn, qp, kn, kp, vb], out=6 + h))
        self.attn_core(heads, mixT, 192 ** -0.5)
        a.release(m)

    def ssd_consts(self):
        c = self.c
        a = c.arena
        self.triF = a.alloc([128], F32)
        self.triR = a.alloc([128], F32)
        self.onesf = a.alloc([128], F32)
        self.sel = a.alloc([24, 128], F32)
        self.maskF = a.alloc([384], BF16)
        self.maskB = a.alloc([384], BF16)
        c.dma(self.triF[:, :], self.k_triF[:, :], [], [self.triF])
        c.dma(self.triR[:, :], self.k_triR[:, :], [], [self.triR])
        c.op("pool", lambda e: e.memset(self.onesf[:, :], 1.0), [], [self.onesf])
        c.dma(self.sel[:, :, :], self.k_sel[:, :, :], [], [self.sel])
        c.dma(self.maskF[:, :], self.k_maskF[:, :], [], [self.maskF])
        c.dma(self.maskB[:, :], self.k_maskB[:, :], [], [self.maskB])

    def ssd_phase(self, b, l, proj, mixT):
        c = self.c
        a = c.arena
        m = a.mark()
        yf_d = c.scratch((S, 768))
        aneg = a.alloc([24], F32)
        dtb = a.alloc([24], F32)
        dsk = a.alloc([12], F32)
        ng = a.alloc([768], F32)
        cw = a.alloc([10, 6], F32)
        c.dma(aneg[:, :], self.w["a_log"].ap[l:l + 1].rearrange("o a h -> o (a h)").to_broadcast([128, 24]), [], [aneg])
        c.act(aneg[:, :], aneg[:, :], AF.Exp, [aneg], [aneg])
        c.ts("dve", aneg[:, :], aneg[:, :], -1.0, None, ALU.mult, None, [aneg], [aneg])
        c.dma(dtb[:, :], self.w["dt_bias"].ap[l:l + 1].rearrange("o a h -> o (a h)").to_broadcast([128, 24]), [], [dtb])
        c.dma(dsk[:, :], self.w["d_skip"].ap[l:l + 1, :].to_broadcast([128, 12]), [], [dsk])
        self.load_vec_bc(ng, self.w["ssd_norm_g"].ap[l:l + 1, :], 768)
        c.dma(cw[:, :, :], self.k_convT.ap[l], [], [cw])
        dt = a.alloc([NT, 24], F32)
        af = a.alloc([NT, 24], F32)
        ab = a.alloc([NT, 24], F32)
        c.op("pool", lambda e: e.memset(af[:, :, :], 0.0), [], [af])
        c.op("pool", lambda e: e.memset(ab[:, :, :], 0.0), [], [ab])
        for t in range(NT):
            c.dma(dt[:, t, :], proj.ap[t * 128:(t + 1) * 128, 4160:4184], [proj.u((t, 8))], [dt])
        c.tt("dve", dt[:, :, :], dt[:, :, :], dtb[:, :].unsqueeze(1).to_broadcast([128, NT, 24]), ALU.add, [dt, dtb], [dt])
        c.act(dt[:, :, :], dt[:, :, :], AF.Exp, [dt], [dt])
        c.act(dt[:, :, :], dt[:, :, :], AF.Ln, [dt, self.one1], [dt], bias=self.one1[:, 0:1])
        c.tt("dve", af[:, :, 0:12], dt[:, :, 0:12], aneg[:, 0:12].unsqueeze(1).to_broadcast([128, NT, 12]), ALU.mult, [dt, aneg, af], [af])
        c.tt("dve", ab[:, :, 12:24], dt[:, :, 12:24], aneg[:, 12:24].unsqueeze(1).to_broadcast([128, NT, 12]), ALU.mult, [dt, aneg, ab], [ab])
        bcT = a.alloc([4, S], BF16)
        xs_tm = a.alloc([NT, 768], BF16)
        b_tm = a.alloc([NT, 256], BF16)
        m2 = a.mark()
        xin = [a.alloc([8, 128], F32) for _ in range(2)]
        xT = [a.alloc([S + 4], F32) for _ in range(2)]
        acc = [a.alloc([S], F32)] * 2
        cvo = [a.alloc([S], BF16) for _ in range(2)]
        for xt_ in xT:
            c.op("pool", lambda e, xt_=xt_: e.memset(xt_[:, 0:2], 0.0), [xt_], [xt_])
            c.op("pool", lambda e, xt_=xt_: e.memset(xt_[:, S + 2:S + 4], 0.0), [xt_], [xt_])
        ii = 0
        for blk in range(10):
            x_t, ac, co = xT[blk % 2], acc[blk % 2], cvo[blk % 2]
            cs = 2880 + blk * 128
            for t8 in range(2):
                xi = xin[ii % 2]
                ii += 1
                c.dma(xi[:, :, :], proj.ap[t8 * 1024:(t8 + 1) * 1024, cs:cs + 128].rearrange("(t p) c -> p t c", p=128),
                      [proj.u((t, j)) for t in range(t8 * 8, t8 * 8 + 8) for j in (5, 6, 7, 8)], [xi])
                for t4 in range(2):
                    pb = c.bank()
                    for q in range(4):
                        c.mm(pb[:, q * 128:(q + 1) * 128], xi[:, t4 * 4 + q, :], self.identf[:, :], True, True, [xi, self.identf], [pb])
                    o0 = 2 + (t8 * 2 + t4) * 512
                    c.copy("act", x_t[:, o0:o0 + 512], pb[:, :], [pb], [x_t])
            c.ts("dve", ac[:, :], x_t[:, 0:S], cw[:, blk, 0:1], None, ALU.mult, None, [x_t, cw], [ac])
            for j in range(1, 5):
                c.stt("dve", ac[:, :], x_t[:, j:j + S], cw[:, blk, j:j + 1], ac[:, :], ALU.mult, ALU.add, [x_t, cw, ac], [ac])
            if blk < 6 or blk in (6, 7):
                c.act(co[:, :], ac[:, :], AF.Silu, [ac, cw], [co], bias=cw[:, blk, 5:6])
                for t8 in range(2):
                    pb = c.bank()
                    pv = self.bf_bank(pb, 8)
                    for q in range(8):
                        t = t8 * 8 + q
                        c.tr(pv[:, q, :], co[:, t * 128:(t + 1) * 128], self.ident[:, :], [co, self.ident], [pb])
                    if blk < 6:
                        c.copy("act" if t8 else "dve", xs_tm[:, t8 * 8:(t8 + 1) * 8, blk * 128:(blk + 1) * 128], pv, [pb], [xs_tm])
                    else:
                        c.copy("act" if t8 else "dve", b_tm[:, t8 * 8:(t8 + 1) * 8, (blk - 6) * 128:(blk - 5) * 128], pv, [pb], [b_tm])
            if blk >= 6:
                c.act(bcT[:, blk - 6, :], ac[:, :], AF.Silu, [ac, cw], [bcT], bias=cw[:, blk, 5:6])
        a.release(m2)
        H = [[a.alloc([384], F32) for _ in range(2)] for _ in range(2)]
        Hb = [[a.alloc([384], BF16) for _ in range(2)] for _ in range(2)]
        NS = 4
        apad = [a.alloc([128], F32) for _ in range(NS)]
        for ap_ in apad:
            c.op("pool", lambda e, ap_=ap_: e.memset(ap_[:, :], 0.0), [], [ap_])
        cs_ = [a.alloc([72], F32) for _ in range(NS)]
        ex = [a.alloc([72], F32) for _ in range(NS)]
        pfm = [a.alloc([256], F32) for _ in range(NS)]
        xd = [a.alloc([768], BF16) for _ in range(NS)]
        xdw = [a.alloc([768], BF16) for _ in range(2)] * 2
        xdw = [xdw[0], xdw[0], xdw[1], xdw[1]]
        yo = [a.alloc([768], F32) for _ in range(NS)]
        cbT = [[a.alloc([128], F32) for _ in range(2)] for _ in range(2)]
        eD = [[a.alloc([384], F32) for _ in range(2)] for _ in range(2)]
        wT = [[a.alloc([384], BF16) for _ in range(3)] for _ in range(2)]
        ytmp = [[a.alloc([384], F32)] * 2 for _ in range(2)]
        zb = [a.alloc([768], F32) for _ in range(2)]
        yfb = [a.alloc([768], F32) for _ in range(2)]
        yn = [a.alloc([768], BF16) for _ in range(2)]
        st = [a.alloc([8], F32) for _ in range(2)]
        junk = a.alloc([384], BF16)
        accA, accB = c.banks[6], c.banks[7]
        for d in range(2):
            for g in range(2):
                c.op("pool", lambda e, d=d, g=g: e.memset(H[d][g][:, :], 0.0), [H[d][g]], [H[d][g]])
                c.op("pool", lambda e, d=d, g=g: e.memset(Hb[d][g][:, :], 0.0), [Hb[d][g]], [Hb[d][g]])
        cnt = {"w": [0, 0], "f": 0}
        pend = []

        def chunk_step(d, ch, k, finalize):
            mask = self.maskF if d == 0 else self.maskB
            a_own = af if d == 0 else ab
            tri = self.triF if d == 0 else self.triR
            cst, ext, pf, xd_, xw_ = cs_[k], ex[k], pfm[k], xd[k], xdw[k]
            tc_ = slice(ch * 128, (ch + 1) * 128)
            d0 = d * 12
            ap_ = apad[k]
            c.copy("act", ap_[:, 0:24], a_own[:, ch, :], [a_own, ap_], [ap_])
            pb = c.bank()
            c.mm(pb[:, 0:24], tri[:, :], a_own[:, ch, 0:24], True, True, [tri, a_own], [pb])
            c.mm(pb[:, 24:48], self.onesf[:, :], a_own[:, ch, 0:24], True, True, [self.onesf, a_own], [pb])
            c.mm(pb[:, 128:256], ap_[:, :], tri[:, :], True, True, [tri, ap_], [pb])
            yield
            c.copy("dve", cst[:, 0:48], pb[:, 0:48], [pb], [cst])
            c.copy("act", pf[:, 0:128], pb[:, 128:256], [pb], [pf])
            c.ts("dve", pf[:, 128:256], pb[:, 128:256], -1.0, None, ALU.mult, None, [pb], [pf])
            c.tt("dve", cst[:, 48:72], cst[:, 24:48], cst[:, 0:24], ALU.subtract, [cst], [cst])
            c.act(ext[:, 0:24], cst[:, 0:24], AF.Exp, [cst], [ext])
            c.act(ext[:, 24:48], cst[:, 48:72], AF.Exp, [cst], [ext])
            c.act(ext[:, 48:72], cst[:, 24:48], AF.Exp, [cst], [ext])
            xs3 = xs_tm[:, ch, :].rearrange("p (h q) -> p h q", h=12)
            xd3 = xd_[:, :].rearrange("p (h q) -> p h q", h=12)
            xw3 = xw_[:, :].rearrange("p (h q) -> p h q", h=12)
            c.tt("pool", xd3, xs3, dt[:, ch, d0:d0 + 12].unsqueeze(2).to_broadcast([128, 12, 64]), ALU.mult, [xs_tm, dt], [xd_])
            c.tt("pool", xw3, xd3, ext[:, 24 + d0:36 + d0].unsqueeze(2).to_broadcast([128, 12, 64]), ALU.mult, [xd_, ext], [xw_])
            yield
            y_t = yo[k]
            for g in range(2):
                cb = cbT[d][g]
                Hg, Hbg = H[d][g], Hb[d][g]
                pcb = c.bank()
                c.mm(pcb[:, 0:128], bcT[:, g, tc_], bcT[:, 2 + g, tc_], True, True, [bcT], [pcb])
                yield
                c.copy("act", cb[:, :], pcb[:, 0:128], [pcb], [cb])
                accY = (accA if g == 0 else accB) if d == 0 else (c.banks[4] if g == 0 else c.banks[5])
                for half in range(2):
                    pd = c.bank()
                    for j3 in range(3):
                        hh = d0 + g * 6 + half * 3 + j3
                        reg = pd[:, j3 * 128:(j3 + 1) * 128]
                        c.mm(reg, self.sel[:, hh, :], pf[:, 0:128], j3 == 0, False, [self.sel, pf], [pd])
                        c.mm(reg, pf[:, 128:256], self.sel[:, hh, :], False, False, [self.sel, pf], [pd])
                    c.mm(pd[:, 0:384], self.ident[:, :], mask[:, :], False, True, [self.ident, mask], [pd])
                    yield
                    e_t = eD[d][half]
                    c.act(e_t[:, :], pd[:, 0:384], AF.Exp, [pd], [e_t])
                    w_t = wT[d][cnt["w"][d] % 3]
                    cnt["w"][d] += 1
                    c.tt("dve", w_t[:, :].rearrange("p (j s) -> p j s", j=3), e_t[:, :].rearrange("p (j s) -> p j s", j=3),
                         cb[:, :].unsqueeze(1).to_broadcast([128, 3, 128]), ALU.mult, [e_t, cb], [w_t])
                    yield
                    for j3 in range(3):
                        hl = g * 6 + half * 3 + j3
                        c.mm(accY[:, (half * 3 + j3) * 64:(half * 3 + j3 + 1) * 64], w_t[:, j3 * 128:(j3 + 1) * 128],
                             xd_[:, hl * 64:(hl + 1) * 64], True, True, [w_t, xd_], [accY])
                yield
                po = c.bank()
                c.mm(po[:, 0:384], bcT[:, 2 + g, tc_], Hbg[:, :], True, True, [bcT, Hbg], [po])
                yt = ytmp[d][g]
                c.tt("dve", yt[:, :].rearrange("p (h q) -> p h q", h=6), po[:, 0:384].rearrange("p (h q) -> p h q", h=6),
                     ext[:, d0 + g * 6:d0 + g * 6 + 6].unsqueeze(2).to_broadcast([128, 6, 64]), ALU.mult, [po, ext], [yt])
                c.tt("dve", y_t[:, g * 384:(g + 1) * 384], accY[:, 0:384], yt[:, :], ALU.add, [accY, yt], [y_t])
                ps_ = c.bank()
                c.mm(ps_[:, 0:384], b_tm[:, ch, g * 128:(g + 1) * 128], xw_[:, g * 384:(g + 1) * 384], True, True,
                     [b_tm, xw_], [ps_])
                yield
                c.tt("dve", Hg[:, :].rearrange("p (h q) -> p h q", h=6), Hg[:, :].rearrange("p (h q) -> p h q", h=6),
                     ext[:, 48 + d0 + g * 6:48 + d0 + g * 6 + 6].unsqueeze(2).to_broadcast([128, 6, 64]), ALU.mult,
                     [Hg, ext], [Hg])
                c.tt("dve", Hg[:, :], Hg[:, :], ps_[:, 0:384], ALU.add, [Hg, ps_], [Hg])
                c.copy("act", Hbg[:, :], Hg[:, :], [Hg], [Hbg])
                yield
            if not finalize:
                c.dma(yf_d.ap[tc_, :], y_t[:, :], [y_t], [yf_d.u(ch)])
                return
            pend.append((d, ch, k))

        def fin_step(d, ch, k):
            y_t = yo[k]
            tc_ = slice(ch * 128, (ch + 1) * 128)
            kf = cnt["f"] % 2
            cnt["f"] += 1
            z_t, yf_t, yn_t, s_ = zb[kf], yfb[kf], yn[kf], st[kf]
            c.dma(yf_t[:, :], yf_d.ap[tc_, :], [yf_d.u(ch)], [yf_t])
            c.dma(z_t[:, :], proj.ap[tc_, 2112:2880], [proj.u((ch, 4)), proj.u((ch, 5))], [z_t])
            c.tt("dve", y_t[:, :], y_t[:, :], yf_t[:, :], ALU.add, [y_t, yf_t], [y_t])
            c.tt("pool", yf_t[:, :].rearrange("p (h q) -> p h q", h=12), xs_tm[:, ch, :].rearrange("p (h q) -> p h q", h=12),
                 dsk[:, :].unsqueeze(2).to_broadcast([128, 12, 64]), ALU.mult, [xs_tm, dsk], [yf_t])
            c.tt("dve", y_t[:, :], y_t[:, :], yf_t[:, :], ALU.add, [y_t, yf_t], [y_t])
            c.act(z_t[:, :], z_t[:, :], AF.Silu, [z_t], [z_t])
            yield
            c.tt("dve", y_t[:, :], y_t[:, :], z_t[:, :], ALU.mult, [y_t, z_t], [y_t])
            for g in range(2):
                c.act(junk[:, :], y_t[:, g * 384:(g + 1) * 384], AF.Square, [y_t], [junk, s_], accum_out=s_[:, g:g + 1])
            yield
            c.act(s_[:, 2:4], s_[:, 0:2], AF.Sqrt, [s_, self.eps], [s_], scale=1.0 / 384, bias=self.eps[:, 0:1])
            c.op("dve", lambda e, s_=s_: e.reciprocal(out=s_[:, 4:6], in_=s_[:, 2:4]), [s_], [s_])
            for g in range(2):
                c.stt("dve", yn_t[:, g * 384:(g + 1) * 384], y_t[:, g * 384:(g + 1) * 384], s_[:, 4 + g:5 + g],
                      ng[:, g * 384:(g + 1) * 384], ALU.mult, ALU.mult, [y_t, s_, ng], [yn_t])
            yield
            pb = c.bank()
            pv = self.bf_bank(pb, 8)
            for j in range(6):
                c.tr(pv[:, j, :], yn_t[:, j * 128:(j + 1) * 128], self.ident[:, :], [yn_t, self.ident], [pb])
            yield
            for j in range(6):
                c.copy("act" if j % 2 else "dve", mixT[10 + j][:, tc_], pv[:, j, :], [pb], [mixT[10 + j]])

        def run_group(gens):
            alive = [True] * len(gens)
            while any(alive):
                for i_ in range(len(gens)):
                    if alive[i_]:
                        try:
                            next(gens[i_])
                        except StopIteration:
                            alive[i_] = False

        old_rot = c.rot
        c.rot = [0, 1, 2, 3]
        for step in range(NT):
            todo = list(pend)
            del pend[:]
            run_group([chunk_step(0, step, (step % 2), step >= NT // 2),
                       chunk_step(1, NT - 1 - step, 2 + (step % 2), step >= NT // 2)] +
                      [fin_step(*p) for p in todo])
        run_group([fin_step(*p) for p in pend])
        c.rot = old_rot
        a.release(m)

    def out_phase(self, b, l, mixT, xsrc, xkey, x1):
        c = self.c
        a = c.arena
        m = a.mark()
        GT = a.alloc([D], F32)
        self.load_bc(GT, l, b, 2)
        wv = self.w["w_out"].ap[l].rearrange("(k p) n -> p k n", p=128)
        wb = [a.alloc([16, 512], BF16) for _ in range(2)]
        xb = [a.alloc([512], F32) for _ in range(4)]
        ob = [a.alloc([512], F32) for _ in range(3)]
        its = [(n, t) for n in range(4) for t in range(NT)]

        def load(i):
            n, t = its[i]
            c.dma(xb[i % 4][:, :], xsrc[t * 128:(t + 1) * 128, n * 512:(n + 1) * 512], [xkey(t)], [xb[i % 4]])

        load(0)
        load(1)
        for i, (n, t) in enumerate(its):
            if i + 2 < len(its):
                load(i + 2)
            w_t = wb[n % 2]
            ns = slice(n * 512, (n + 1) * 512)
            if t == 0:
                c.dma(w_t[:, :, :], wv[:, :, ns], [], [w_t], q="pool")
            x_t, o_t = xb[i % 4], ob[i % 3]
            ts_ = slice(t * 128, (t + 1) * 128)
            pb = c.bank()
            for k in range(16):
                c.mm(pb[:, :], mixT[k][:, ts_], w_t[:, k, :], k == 0, k == 15, [mixT[k], w_t], [pb])
            c.tt("dve", o_t[:, :], pb[:, :], GT[:, ns], ALU.mult, [pb, GT], [o_t])
            c.tt("dve", o_t[:, :], o_t[:, :], x_t[:, :], ALU.add, [o_t, x_t], [o_t])
            c.dma(x1.ap[ts_, ns], o_t[:, :], [o_t], [x1.u((t, n))])
        a.release(m)

    def norm2_phase(self, b, l, x1, h2T_d):
        c = self.c
        a = c.arena
        m = a.mark()
        stg = [a.alloc([16, 512], BF16) for _ in range(2)]

        def dst(t):
            return stg[(t // 4) % 2], (t % 4) * 128

        def after(t):
            if t % 4 == 3:
                s_ = stg[(t // 4) % 2]
                q = t // 4
                c.dma(h2T_d.ap[:, :, q * 512:(q + 1) * 512], s_[:, :, :], [s_], [h2T_d.u(q)])

        self.norm_phase(b, l, x1.ap, lambda t: [x1.u((t, n)) for n in range(4)], "norm2_g", 4, 3, None, dst=dst, after=after)
        a.release(m)

    def ffn_phase(self, b, l, x1, h2T_d, x2):
        c = self.c
        a = c.arena
        m = a.mark()
        TB = 1024
        wgu = self.w["w_gate_up"].ap[l].rearrange("(k p) n -> p k n", p=128)
        wdn = self.w["w_down"].ap[l].rearrange("(j p) n -> p j n", p=128)
        g2 = a.alloc([16], F32)
        g2row = a.alloc([128], F32)
        c.op("pool", lambda e: e.memset(g2row[:, :], 0.0), [], [g2row])
        c.dma(g2row[0:16, :], self.mod.ap[l, b, 5 * D:6 * D].rearrange("(k p) -> k p", p=128),
              [self.mod.u((l, 20 + q)) for q in range(4)] + [g2row], [g2row])
        pg2 = c.bank()
        c.mm(pg2[:, 0:16], g2row[:, :], self.identf[:, 0:16], True, True, [g2row, self.identf], [pg2])
        c.copy("dve", g2[:, :], pg2[:, 0:16], [pg2], [g2])
        actT = a.alloc([44, TB], BF16)
        sgb = [a.alloc([512], F32) for _ in range(2)]
        xb = [a.alloc([256], F32) for _ in range(4)]
        ob = [a.alloc([256], F32) for _ in range(2)]
        gtb = [a.alloc([512], F32) for _ in range(4)]
        h2 = a.alloc([16, TB], BF16)

        def load_h2(tb):
            for q in range(TB // 512):
                qq = tb * (TB // 512) + q
                c.dma(h2[:, :, q * 512:(q + 1) * 512], h2T_d.ap[:, :, qq * 512:(qq + 1) * 512], [h2T_d.u(qq)], [h2])

        load_h2(0)
        for tb in range(S // TB):
            m2 = a.mark()
            wg = [a.alloc([16, 256], BF16) for _ in range(3)]
            wu = [a.alloc([16, 256], BF16) for _ in range(3)]
            isg = 0
            for jg in range(22):
                g_t, u_t = wg[jg % 3], wu[jg % 3]
                c.dma(g_t[:, :, :], wgu[:, :, jg * 256:(jg + 1) * 256], [], [g_t], q="pool")
                c.dma(u_t[:, :, :], wgu[:, :, FFN + jg * 256:FFN + (jg + 1) * 256], [], [u_t], q="pool")
                for jj in range(2):
                    j = jg * 2 + jj
                    for tq in range(TB // 512):
                        tqs = slice(tq * 512, (tq + 1) * 512)
                        pg = c.bank()
                        for k in range(16):
                            c.mm(pg[:, :], g_t[:, k, jj * 128:(jj + 1) * 128], h2[:, k, tqs], k == 0, k == 15, [g_t, h2], [pg])
                        pu = c.bank()
                        for k in range(16):
                            c.mm(pu[:, :], u_t[:, k, jj * 128:(jj + 1) * 128], h2[:, k, tqs], k == 0, k == 15, [u_t, h2], [pu])
                        sg = sgb[isg % 2]
                        isg += 1
                        c.act(sg[:, :], pg[:, :], AF.Silu, [pg], [sg])
                        c.tt("dve", actT[:, j, tqs], pu[:, :], sg[:, :], ALU.mult, [pu, sg], [actT])
            a.release(m2)
            if tb + 1 < S // TB:
                load_h2(tb + 1)
            m2 = a.mark()
            wd = [a.alloc([44, 256], BF16) for _ in range(3)]
            igt = 0
            dits = [(sl, tq, tt_) for sl in range(8) for tq in range(TB // 512) for tt_ in range(4)]

            def dload(i):
                sl, tq, tt_ = dits[i]
                tok0 = tb * TB + tq * 512 + tt_ * 128
                c.dma(xb[i % 4][:, 0:256], x1.ap[tok0:tok0 + 128, sl * 256:(sl + 1) * 256], [x1.u((tok0 // 128, sl // 2))], [xb[i % 4]])

            dload(0)
            dload(1)
            gts = []
            for i, (sl, tq, tt_) in enumerate(dits):
                if i + 2 < len(dits):
                    dload(i + 2)
                w_t = wd[sl % 3]
                tqs = slice(tq * 512, (tq + 1) * 512)
                if tq == 0 and tt_ == 0:
                    c.dma(w_t[:, :, :], wdn[:, :, sl * 256:(sl + 1) * 256], [], [w_t], q="pool")
                if tt_ == 0:
                    gts = []
                    for dd in range(2):
                        dblk = sl * 2 + dd
                        pb = c.bank()
                        for j in range(44):
                            c.mm(pb[:, :], w_t[:, j, dd * 128:(dd + 1) * 128], actT[:, j, tqs], j == 0, j == 43, [w_t, actT], [pb])
                        gt = gtb[igt % 4]
                        igt += 1
                        c.act(gt[:, :], pb[:, :], AF.Identity, [pb, g2], [gt], scale=g2[:, dblk:dblk + 1])
                        gts.append(gt)
                tok0 = tb * TB + tq * 512 + tt_ * 128
                tki = tok0 // 128
                x_t, o_t = xb[i % 4], ob[i % 2]
                cs = slice(sl * 256, (sl + 1) * 256)
                pt = c.bank()
                for dd in range(2):
                    c.mm(pt[:, dd * 128:(dd + 1) * 128], gts[dd][:, tt_ * 128:(tt_ + 1) * 128], self.identf[:, :], True, True,
                         [gts[dd], self.identf], [pt])
                c.tt("dve", o_t[:, 0:256], pt[:, 0:256], x_t[:, 0:256], ALU.add, [pt, x_t], [o_t])
                c.dma(x2.ap[tok0:tok0 + 128, cs], o_t[:, 0:256], [o_t], [x2.u((tki, sl))])
            a.release(m2)
        a.release(m)

    def final_phase(self, b, x2, x2key):
        c = self.c
        a = c.arena
        m = a.mark()
        G = a.alloc([D], F32)
        self.load_vec_bc(G, self.w["final_norm_g"].ap.rearrange("(o d) -> o d", o=1), D)
        xb = [a.alloc([D], F32) for _ in range(2)]
        junk = a.alloc([D], BF16)
        st = [a.alloc([4], F32) for _ in range(2)]
        for t in range(NT):
            xt, s_ = xb[t % 2], st[t % 2]
            ts_ = slice(t * 128, (t + 1) * 128)
            c.dma(xt[:, :], x2.ap[ts_, :], x2key(t), [xt])
            c.act(junk[:, :], xt[:, :], AF.Square, [xt], [junk, s_], accum_out=s_[:, 0:1])
            c.act(s_[:, 1:2], s_[:, 0:1], AF.Sqrt, [s_, self.eps], [s_], scale=1.0 / D, bias=self.eps[:, 0:1])
            c.op("dve", lambda e, s_=s_: e.reciprocal(out=s_[:, 2:3], in_=s_[:, 1:2]), [s_], [s_])
            c.stt("dve", xt[:, :], xt[:, :], s_[:, 2:3], G[:, :], ALU.mult, ALU.mult, [xt, s_, G], [xt])
            c.dma(self.out.ap[b, ts_, :], xt[:, :], [xt], [self.out.u((b, t))])
        a.release(m)

    def build(self, stop_after=None):
        c = self.c
        a = c.arena
        self.consts()
        self.mod_phase()
        for b in range(self.nseq):
            xcur = self.x.ap[b]
            xkey = lambda t: []
            for l in range(self.nlayer):
                m = a.mark()
                proj = self.scr("proj", (S, IN_COLS))
                x1 = self.scr("x1", (S, D))
                x2 = self.scr("x2", (S, D))
                h2T_d = self.scr("h2T", (128, 16, S), BF16)
                self.last = dict(proj=proj, x1=x1, x2=x2, h2T_d=h2T_d)
                mixT = [a.alloc([S], BF16) for _ in range(16)]
                self.mixT = mixT
                m1 = a.mark()
                hT = [a.alloc([16, 128], BF16) for _ in range(NT)]
                self.norm_phase(b, l, xcur, xkey, "norm1_g", 1, 0, hT)
                self.in_proj_tm(b, l, hT, proj)
                a.release(m1)
                if stop_after == "in":
                    return
                self.gqa_phase(b, l, proj, mixT)
                if stop_after == "gqa":
                    self.dump_mix(mixT, 6)
                    return
                self.mla_phase(b, l, proj, mixT)
                if stop_after == "mla":
                    self.dump_mix(mixT, 10)
                    return
                m1 = a.mark()
                self.ssd_consts()
                self.ssd_phase(b, l, proj, mixT)
                a.release(m1)
                self.dump_mix(mixT)
                if stop_after == "ssd":
                    return
                self.out_phase(b, l, mixT, xcur, xkey, x1)
                a.release(m)
                if stop_after == "out":
                    return
                self.norm2_phase(b, l, x1, h2T_d)
                self.ffn_phase(b, l, x1, h2T_d, x2)
                xcur = x2.ap
                xkey = (lambda x2: (lambda t: [x2.u((t, n)) for n in range(8)]))(x2)
                if stop_after == "ffn":
                    return
            self.final_phase(b, x2, xkey)


_PROG = None


def _get_prog():
    global _PROG
    if _PROG is None:
        p = Prog()
        p.build()
        p.c.sch.emit()
        _PROG = p
    return _PROG


def make_in_maps(inputs, ncores=NCORES):
    consts = host_consts()
    consts["convT"] = layout_conv(np.asarray(inputs["conv_w"], np.float32), np.asarray(inputs["conv_b"], np.float32))
    shared = {k: np.ascontiguousarray(np.asarray(inputs[k], np.float32)) for k in WSHAPES}
    shared.update(consts)
    x = np.asarray(inputs["x"], np.float32)
    cc = np.asarray(inputs["c"], np.float32)
    maps = []
    for i in range(ncores):
        m = dict(shared)
        m["x"] = np.ascontiguousarray(x[2 * i:2 * i + 2])
        m["cT"] = np.ascontiguousarray(cc[2 * i:2 * i + 2].reshape(2, 16, 128).transpose(2, 1, 0).reshape(128, 32))
        maps.append(m)
    return maps


def kernel(**inputs):
    p = _get_prog()
    maps = make_in_maps(inputs)
    res = run_bass_kernel_spmd(p.c.nc, maps, core_ids=list(range(NCORES)))
    out = np.concatenate([np.asarray(r["out"], np.float32) for r in res.results], axis=0)
    return out
```

```python
import numpy as np
import concourse.bass as bass
import concourse.mybir as mybir
from concourse.bass_utils import run_bass_kernel_spmd

F32 = mybir.dt.float32
BF16 = mybir.dt.bfloat16
AF = mybir.ActivationFunctionType
ALU = mybir.AluOpType
AX = mybir.AxisListType

D = 2048
S = 2048
NT = S // 128
DEPTH = 2
EPS = 1e-6
IN_COLS = 4184
FFN = 5632
NCORES = 8


class U:
    __slots__ = ("w", "r", "name", "excl")

    def __init__(self, name="", excl=False):
        self.w = None
        self.r = {}
        self.name = name
        self.excl = excl


class Op:
    __slots__ = ("eng", "fn", "deps", "dma", "sig", "seq", "dsem", "dval", "dprev")

    def __init__(self, eng, fn, deps, dma):
        self.eng = eng
        self.fn = fn
        self.deps = deps
        self.dma = dma
        self.sig = False
        self.seq = 0
        self.dsem = None
        self.dval = 0
        self.dprev = None


ENGS = ("pe", "act", "dve", "pool", "sp")


class Sched:
    def __init__(self, nc, nds=8):
        self.nc = nc
        self.ops = []
        self.nds = nds

    def add(self, eng, fn, reads=(), writes=(), dma=False):
        i = len(self.ops)
        deps = set()
        xr = [u for u in reads if u.excl]
        if xr:
            reads = [u for u in reads if not u.excl]
            writes = list(writes) + [u for u in xr if u not in writes]
        for u in reads:
            if u.w is not None:
                deps.add(u.w)
        for u in writes:
            if u.w is not None:
                deps.add(u.w)
            deps.update(u.r.values())
        for u in reads:
            u.r[eng if not dma else ("dma", eng, i)] = i
        for u in writes:
            u.w = i
            u.r = {}
        deps.discard(i)
        self.ops.append(Op(eng, fn, deps, dma))
        return i

    def emit(self):
        nc = self.nc
        ops = self.ops
        for o in ops:
            real = []
            for d in o.deps:
                od = ops[d]
                if (not od.dma) and od.eng == o.eng and o.eng == "pe" and not o.dma:
                    continue
                real.append(d)
                if not od.dma:
                    od.sig = True
            o.deps = sorted(real)
        cnt = {e: 0 for e in ENGS}
        dcnt = {e: 0 for e in ENGS}
        for o in ops:
            if o.dma:
                n = dcnt[o.eng]
                dcnt[o.eng] += 1
                o.dsem = (o.eng, n % self.nds)
                o.dval = 16 * (n // self.nds + 1)
            elif o.sig:
                cnt[o.eng] += 1
                o.seq = cnt[o.eng]
        self.counts = (cnt, dcnt)
        sems = {e: nc.alloc_semaphore("s_" + e) for e in ENGS}
        dsems = {}
        for e in ENGS:
            if dcnt[e]:
                for k in range(self.nds):
                    dsems[(e, k)] = nc.alloc_semaphore("d_%s%d" % (e, k))
        per = {e: [o for o in ops if o.eng == e] for e in ENGS}

        def run(ename, eng):
            known = {}
            for o in per[ename]:
                if o.dma and o.dval > 16:
                    key = ("d",) + o.dsem
                    v = o.dval - 16
                    if known.get(key, 0) < v:
                        eng.wait_ge(dsems[o.dsem], v)
                        known[key] = v
                for d in o.deps:
                    od = ops[d]
                    if od.dma:
                        key = ("d",) + od.dsem
                        if known.get(key, 0) < od.dval:
                            eng.wait_ge(dsems[od.dsem], od.dval)
                            known[key] = od.dval
                    else:
                        key = od.eng
                        if known.get(key, 0) < od.seq:
                            eng.wait_ge(sems[od.eng], od.seq)
                            known[key] = od.seq
                ins = o.fn(eng)
                if o.dma:
                    ins.then_inc(dsems[o.dsem], 16)
                elif o.sig:
                    ins.then_inc(sems[ename], 1)
            for k in range(self.nds):
                if (ename, k) in dsems:
                    n = dcnt[ename]
                    last = (n - 1 - k) // self.nds + 1 if n - 1 >= k else 0
                    if last > 0:
                        eng.wait_ge(dsems[(ename, k)], 16 * last)

        with nc.Block() as block:
            @block.tensor
            def _(e):
                run("pe", e)

            @block.scalar
            def _(e):
                run("act", e)

            @block.vector
            def _(e):
                run("dve", e)

            @block.gpsimd
            def _(e):
                run("pool", e)

            @block.sync
            def _(e):
                run("sp", e)


def _units(xs):
    out = []
    for x in xs:
        if x is None:
            continue
        if isinstance(x, U):
            out.append(x)
        elif isinstance(x, (list, tuple)):
            out.extend(_units(x))
        else:
            out.extend(x.u)
    return out


class T:
    def __init__(self, ap, units):
        self.ap = ap
        self.u = units

    def __getitem__(self, k):
        return self.ap[k]


class Arena:
    def __init__(self, nc, nbytes):
        self.t = nc.alloc_sbuf_tensor("arena", [128, nbytes // 4], F32)
        self.top = 0
        self.nbytes = nbytes
        self.recs = []

    def alloc(self, free_shape, dtype):
        esz = 2 if dtype == BF16 else 4
        n = int(np.prod(free_shape))
        nb = (n * esz + 63) // 64 * 64
        off = self.top
        assert off + nb <= self.nbytes, ("arena overflow", off, nb, self.nbytes)
        self.top = off + nb
        ap = self.t[:, off // 4:(off + nb) // 4]
        if esz == 2:
            ap = ap.bitcast(BF16)
        ap = ap[:, 0:n]
        if len(free_shape) == 2:
            ap = ap.rearrange("p (a b) -> p a b", a=free_shape[0])
        elif len(free_shape) == 3:
            ap = ap.rearrange("p (a b c) -> p a b c", a=free_shape[0], b=free_shape[1])
        u = U()
        end = off + nb
        keep = []
        for (o, e, uo) in self.recs:
            if o < end and off < e:
                if uo.w is not None:
                    u.r[("inh", uo.w)] = uo.w
                for v in uo.r.values():
                    u.r[("inh", v)] = v
                if off <= o and e <= end:
                    continue
            keep.append((o, e, uo))
        keep.append((off, end, u))
        self.recs = keep
        return T(ap, [u])

    def mark(self):
        return self.top

    def release(self, m):
        self.top = m


class DT:
    def __init__(self, ap):
        self.ap = ap
        self.us = {}

    def u(self, key=0):
        if key not in self.us:
            self.us[key] = U()
        return self.us[key]

    def __getitem__(self, k):
        return self.ap[k]


class Ctx:
    def __init__(self):
        self.nc = bass.Bass("TRN2", target_bir_lowering=False)
        self.sch = Sched(self.nc)
        self.arena = Arena(self.nc, 207 * 1024)
        self.banks = []
        for i in range(8):
            t = self.nc.alloc_psum_tensor("pb%d" % i, [128, 512], F32)
            self.banks.append(T(t[:, :], [U("pb%d" % i, excl=True)]))
        self.bi = 0
        self.rot = [0, 1, 2, 3, 4, 5]
        self.ins = {}
        self.nscr = 0

    def bank(self):
        b = self.banks[self.rot[self.bi % len(self.rot)]]
        self.bi += 1
        return b

    def dram_in(self, name, shape, dt=F32):
        t = self.nc.dram_tensor(name, list(shape), dt, kind="ExternalInput")
        d = DT(t.ap())
        self.ins[name] = d
        return d

    def dram_out(self, name, shape, dt=F32):
        t = self.nc.dram_tensor(name, list(shape), dt, kind="ExternalOutput")
        return DT(t.ap())

    def scratch(self, shape, dt=F32, name=None):
        self.nscr += 1
        t = self.nc.dram_tensor(name or ("scr%d" % self.nscr), list(shape), dt, kind="Internal")
        return DT(t.ap())

    def dma(self, out, in_, reads, writes, q="sp", **kw):
        self.sch.add(q, lambda e: e.dma_start(out=out, in_=in_, **kw), _units(reads), _units(writes), dma=True)

    def mm(self, out, lhsT, rhs, start, stop, reads, writes):
        self.sch.add("pe", lambda e: e.matmul(out, lhsT, rhs, start=start, stop=stop), _units(reads), _units(writes))

    def tr(self, out, in_, ident, reads, writes):
        self.sch.add("pe", lambda e: e.transpose(out, in_, ident), _units(reads), _units(writes))

    def op(self, eng, fn, reads, writes):
        self.sch.add(eng, fn, _units(reads), _units(writes))

    def act(self, out, in_, func, reads, writes, eng="act", **kw):
        self.sch.add("act", lambda e: e.activation(out=out, in_=in_, func=func, **kw), _units(reads), _units(writes))

    def copy(self, eng, out, in_, reads, writes):
        if eng == "act":
            self.sch.add("act", lambda e: e.copy(out=out, in_=in_), _units(reads), _units(writes))
        else:
            self.sch.add(eng, lambda e: e.tensor_copy(out=out, in_=in_), _units(reads), _units(writes))

    def tt(self, eng, out, in0, in1, op, reads, writes):
        self.sch.add(eng, lambda e: e.tensor_tensor(out=out, in0=in0, in1=in1, op=op), _units(reads), _units(writes))

    def ts(self, eng, out, in0, s1, s2, op0, op1, reads, writes, **kw):
        if op1 is None:
            self.sch.add(eng, lambda e: e.tensor_scalar(out=out, in0=in0, scalar1=s1, scalar2=s2, op0=op0, **kw),
                         _units(reads), _units(writes))
        else:
            self.sch.add(eng, lambda e: e.tensor_scalar(out=out, in0=in0, scalar1=s1, scalar2=s2, op0=op0, op1=op1, **kw),
                         _units(reads), _units(writes))

    def stt(self, eng, out, in0, scalar, in1, op0, op1, reads, writes):
        self.sch.add(eng, lambda e: e.scalar_tensor_tensor(out=out, in0=in0, scalar=scalar, in1=in1, op0=op0, op1=op1),
                     _units(reads), _units(writes))


WSHAPES = {
    "w_ada": (DEPTH, D, 6 * D), "b_ada": (DEPTH, 6 * D), "norm1_g": (DEPTH, D), "norm2_g": (DEPTH, D),
    "w_in": (DEPTH, D, IN_COLS), "q_norm_g": (DEPTH, 128), "k_norm_g": (DEPTH, 128),
    "mla_q_norm_g": (DEPTH, 512), "w_uq": (DEPTH, 512, 768), "mla_kv_norm_g": (DEPTH, 256),
    "w_ukv": (DEPTH, 256, 1024), "conv_w": (DEPTH, 5, 1280), "conv_b": (DEPTH, 1280),
    "dt_bias": (DEPTH, 2, 12), "a_log": (DEPTH, 2, 12), "d_skip": (DEPTH, 12), "ssd_norm_g": (DEPTH, 768),
    "w_out": (DEPTH, D, D), "w_gate_up": (DEPTH, D, 2 * FFN), "w_down": (DEPTH, FFN, D),
    "final_norm_g": (D,),
}


def host_consts():
    import ml_dtypes
    c = {}
    c["ident"] = np.eye(128, dtype=np.float32).astype(ml_dtypes.bfloat16)
    c["identf"] = np.eye(128, dtype=np.float32)
    pos = np.arange(S)
    row = (pos // 64).astype(np.float32)
    col = (pos % 64).astype(np.float32)

    def tab(rot_dim):
        axis_dim = rot_dim // 2
        inv = np.power(np.float32(10000.0), -np.arange(0, axis_dim, 2, dtype=np.float32) / np.float32(axis_dim)).astype(np.float32)
        ar = (row[:, None] * inv[None, :]).astype(np.float32)
        ac = (col[:, None] * inv[None, :]).astype(np.float32)
        cr, sr, cc_, sc = np.cos(ar), np.sin(ar), np.cos(ac), np.sin(ac)
        cosx = np.concatenate([cr, cr, cc_, cc_], axis=1)
        sinx = np.concatenate([-sr, sr, -sc, sc], axis=1)
        return np.stack([cosx, sinx], axis=1).astype(np.float32)

    c["ropeA"] = tab(128)
    c["ropeB"] = tab(64)
    k = np.arange(128)
    c["triF"] = (k[:, None] <= k[None, :]).astype(np.float32)
    c["triR"] = (k[:, None] >= k[None, :]).astype(np.float32)
    sel = np.zeros((128, 24, 128), np.float32)
    for j in range(24):
        sel[j, j, :] = 1.0
    c["sel"] = sel
    mf = np.where(k[None, :] >= k[:, None], 0.0, -30000.0).astype(np.float32)
    mb = np.where(k[None, :] <= k[:, None], 0.0, -30000.0).astype(np.float32)
    c["maskF"] = np.tile(mf, (1, 3)).astype(ml_dtypes.bfloat16)
    c["maskB"] = np.tile(mb, (1, 3)).astype(ml_dtypes.bfloat16)
    return c


def layout_conv(conv_w, conv_b):
    L = conv_w.shape[0]
    o = np.zeros((L, 128, 10, 6), np.float32)
    o[:, :, :, 0:5] = conv_w.reshape(L, 5, 10, 128).transpose(0, 3, 2, 1)
    o[:, :, :, 5] = conv_b.reshape(L, 10, 128).transpose(0, 2, 1)
    return o


class Prog:
    def __init__(self, nseq=2, nlayer=DEPTH, dbg=()):
        self.c = Ctx()
        self.nseq = nseq
        self.nlayer = nlayer
        self.dbg = set(dbg)
        self.dbg_outs = {}
        c = self.c
        self.x = c.dram_in("x", (2, S, D))
        self.cT = c.dram_in("cT", (128, 32))
        self.w = {k: c.dram_in(k, v) for k, v in WSHAPES.items()}
        self.k_ident = c.dram_in("ident", (128, 128), BF16)
        self.k_identf = c.dram_in("identf", (128, 128))
        self.k_ropeA = c.dram_in("ropeA", (S, 2, 128))
        self.k_ropeB = c.dram_in("ropeB", (S, 2, 64))
        self.k_triF = c.dram_in("triF", (128, 128))
        self.k_triR = c.dram_in("triR", (128, 128))
        self.k_sel = c.dram_in("sel", (128, 24, 128))
        self.k_maskF = c.dram_in("maskF", (128, 384), BF16)
        self.k_maskB = c.dram_in("maskB", (128, 384), BF16)
        self.k_convT = c.dram_in("convT", (DEPTH, 128, 10, 6))
        self.out = c.dram_out("out", (2, S, D))

    def scr(self, name, shape, dt=F32):
        if name in self.dbg and name not in self.dbg_outs:
            return self.dbg_out(name, shape, dt)
        return self.c.scratch(shape, dt)

    def dump_mix(self, mixT, n=16):
        if "mix" in self.dbg and "mix" not in self.dbg_outs:
            d = self.dbg_out("mix", (16, 128, S), BF16)
            for k in range(n):
                self.c.dma(d.ap[k], mixT[k][:, :], [mixT[k]], [d.u(k)])

    def dbg_out(self, name, shape, dt=F32):
        d = self.c.dram_out("dbg_" + name, shape, dt)
        self.dbg_outs[name] = d
        return d

    def consts(self):
        c = self.c
        a = c.arena
        self.ident = a.alloc([128], BF16)
        c.dma(self.ident[:, :], self.k_ident[:, :], [], [self.ident])
        self.identf = a.alloc([128], F32)
        c.dma(self.identf[:, :], self.k_identf[:, :], [], [self.identf])
        self.eps = a.alloc([1], F32)
        self.one1 = a.alloc([1], F32)
        c.op("pool", lambda e: e.memset(self.one1[:, :], 1.0), [], [self.one1])
        self.ones = a.alloc([128], BF16)
        c.op("pool", lambda e: e.memset(self.ones[:, :], 1.0), [], [self.ones])
        c.op("pool", lambda e: e.memset(self.eps[:, :], EPS), [], [self.eps])

    def mod_phase(self):
        c = self.c
        a = c.arena
        m = a.mark()
        self.mod = c.scratch((DEPTH, 2, 6 * D), name="mod")
        ct = a.alloc([32], F32)
        c.dma(ct[:, :], self.cT[:, :], [], [ct])
        sg = a.alloc([32], F32)
        c.act(sg[:, :], ct[:, :], AF.Sigmoid, [ct], [sg])
        c.tt("dve", ct[:, :], ct[:, :], sg[:, :], ALU.mult, [ct, sg], [ct])
        ctp = a.alloc([16, 128], BF16)
        c.op("pool", lambda e: e.memset(ctp[:, :, :], 0.0), [], [ctp])
        c.copy("dve", ctp[:, :, 0:2], ct[:, :].rearrange("p (k b) -> p k b", b=2), [ct, ctp], [ctp])
        wb = [a.alloc([16, 512], BF16) for _ in range(3)]
        bb = [a.alloc([512], F32) for _ in range(2)]
        ob = [a.alloc([512], F32) for _ in range(2)]
        NB = 6 * D // 512
        i = 0
        for l in range(self.nlayer):
            wv = self.w["w_ada"].ap[l].rearrange("(k p) n -> p k n", p=128)
            for nb in range(NB):
                w_t, b_t, o_t = wb[i % 3], bb[i % 2], ob[i % 2]
                i += 1
                cs = slice(nb * 512, (nb + 1) * 512)
                c.dma(w_t[:, :, :], wv[:, :, cs], [], [w_t], q="pool")
                c.dma(b_t[0:2, :], self.w["b_ada"].ap[l:l + 1, cs].to_broadcast([2, 512]), [], [b_t])
                pb = c.bank()
                for k in range(16):
                    c.mm(pb[:, :], ctp[:, k, :], w_t[:, k, :], k == 0, k == 15, [ctp, w_t], [pb])
                c.tt("dve", o_t[0:2, :], pb[0:2, :], b_t[0:2, :], ALU.add, [pb, b_t], [o_t])
                c.dma(self.mod.ap[l, :, cs], o_t[0:2, :], [o_t], [self.mod.u((l, nb))])
        a.release(m)

    def load_bc(self, tile, l, b, j):
        c = self.c
        src = self.mod.ap[l, b:b + 1, j * D:(j + 1) * D].to_broadcast([128, D])
        c.dma(tile[:, :], src, [self.mod.u((l, 4 * j + q)) for q in range(4)], [tile])

    def load_vec_bc(self, tile, dram_ap_row, n):
        c = self.c
        c.dma(tile[:, 0:n], dram_ap_row.to_broadcast([128, n]), [], [tile])

    def bf_bank(self, pb, n):
        return pb.ap.bitcast(BF16).rearrange("p (a b) -> p a b", a=8)[:, 0:n, :]

    def rstd_from_ss(self, rstd, ss, n):
        c = self.c
        c.ts("dve", rstd, ss, 1.0 / n, EPS, ALU.mult, ALU.add, [], [])

    def norm_phase(self, b, l, xsrc, xkey, gname, jscale, jshift, hT, dst=None, after=None):
        c = self.c
        a = c.arena
        m = a.mark()
        G = a.alloc([D], F32)
        SH = a.alloc([D], F32)
        tmpg = a.alloc([D], F32)
        self.load_bc(G, l, b, jscale)
        self.load_vec_bc(tmpg, self.w[gname].ap[l:l + 1, :], D)
        c.stt("dve", G[:, :], G[:, :], 1.0, tmpg[:, :], ALU.add, ALU.mult, [G, tmpg], [G])
        self.load_bc(SH, l, b, jshift)
        xb = [a.alloc([D], F32) for _ in range(3)]
        junk = a.alloc([D], BF16)
        hb = [a.alloc([D], BF16) for _ in range(2)]
        st = [a.alloc([4], F32) for _ in range(2)]

        def load(t):
            c.dma(xb[t % 3][:, :], xsrc[t * 128:(t + 1) * 128, :], [xkey(t)], [xb[t % 3]])

        def stage_a(t):
            xt, ht, s_ = xb[t % 3], hb[t % 2], st[t % 2]
            c.act(junk[:, :], xt[:, :], AF.Square, [xt], [junk, s_], accum_out=s_[:, 0:1])
            c.act(s_[:, 1:2], s_[:, 0:1], AF.Sqrt, [s_, self.eps], [s_], scale=1.0 / D, bias=self.eps[:, 0:1])
            c.op("dve", lambda e, s_=s_: e.reciprocal(out=s_[:, 2:3], in_=s_[:, 1:2]), [s_], [s_])
            c.stt("dve", xt[:, :], xt[:, :], s_[:, 2:3], G[:, :], ALU.mult, ALU.mult, [xt, s_, G], [xt])
            c.tt("dve", ht[:, :], xt[:, :], SH[:, :], ALU.add, [xt, SH], [ht])

        def stage_b(t):
            ht = hb[t % 2]
            for half in range(2):
                pb = c.bank()
                pv = self.bf_bank(pb, 8)
                for k in range(8):
                    kc = half * 8 + k
                    c.tr(pv[:, k, :], ht[:, kc * 128:(kc + 1) * 128], self.ident[:, :], [ht, self.ident], [pb])
                if dst is None:
                    c.copy("act" if half == 0 else "dve", hT[t][:, half * 8:(half + 1) * 8, :], pv, [pb], [hT[t]])
                else:
                    dt_, off = dst(t)
                    c.copy("act" if half == 0 else "dve", dt_[:, half * 8:(half + 1) * 8, off:off + 128], pv, [pb], [dt_])
            if after is not None:
                after(t)

        load(0)
        load(1)
        for t in range(NT):
            if t + 2 < NT:
                load(t + 2)
            stage_a(t)
            if t >= 1:
                stage_b(t - 1)
        stage_b(NT - 1)
        a.release(m)

    def load_w_bf16(self, wt, src):
        self.c.dma(wt, src, [], [], q="pool")

    def in_proj_tm(self, b, l, hT, proj):
        c = self.c
        a = c.arena
        m = a.mark()
        wv = self.w["w_in"].ap[l].rearrange("(k p) n -> p k n", p=128)
        blocks = [(j * 512, j * 512, 512) for j in range(8)] + [(4096, 4096, 88)]
        wb = [a.alloc([16, 512], BF16) for _ in range(2)]
        ob = [a.alloc([512], F32) for _ in range(3)]
        i = 0
        for bi, (wc, pc, n) in enumerate(blocks):
            w_t = wb[bi % 2]
            c.dma(w_t[:, :, 0:n], wv[:, :, wc:wc + n], [], [w_t], q="pool")
            for t in range(NT):
                pb = c.bank()
                for k in range(16):
                    c.mm(pb[:, 0:n], hT[t][:, k, :], w_t[:, k, 0:n], k == 0, k == 15, [hT[t], w_t], [pb])
                o_t = ob[i % 3]
                i += 1
                c.copy("act" if i % 2 else "dve", o_t[:, 0:n], pb[:, 0:n], [pb], [o_t])
                c.dma(proj.ap[t * 128:(t + 1) * 128, pc:pc + n], o_t[:, 0:n], [o_t], [proj.u((t, bi))])
        a.release(m)

    def rms_heads(self, x3, nh, hd, ss, junk3, eng="dve"):
        pass

    def rope_tm(self, x2d, tab, nh, hd, T1, T2, SW, out2d, units_r, units_w, out3=False):
        c = self.c
        q = hd // 4
        n = nh * hd
        x3 = x2d.rearrange("p (h d) -> p h d", h=nh)
        xq = x2d.rearrange("p (g b f) -> p g b f", b=2, f=q)
        swq = SW[:, 0:n].rearrange("p (g b f) -> p g b f", b=2, f=q)
        sw3 = SW[:, 0:n].rearrange("p (h d) -> p h d", h=nh)
        t13 = T1[:, 0:n].rearrange("p (h d) -> p h d", h=nh)
        t23 = T2[:, 0:n].rearrange("p (h d) -> p h d", h=nh)
        cosb = tab[:, 0:1, :].to_broadcast([128, nh, hd])
        sinb = tab[:, 1:2, :].to_broadcast([128, nh, hd])
        c.copy("act", swq[:, :, 0, :], xq[:, :, 1, :], units_r, [SW])
        c.copy("act", swq[:, :, 1, :], xq[:, :, 0, :], units_r, [SW])
        c.tt("pool", t13, x3, cosb, ALU.mult, units_r, [T1])
        c.tt("dve", t23, sw3, sinb, ALU.mult, [SW] + units_r, [T2])
        if out3:
            c.tt("dve", out2d, t13, t23, ALU.add, [T1, T2], units_w)
        else:
            c.tt("dve", out2d, T1[:, 0:n], T2[:, 0:n], ALU.add, [T1, T2], units_w)

    def attn_core(self, heads, mixT, scale):
        c = self.c
        a = c.arena
        m = a.mark()
        LOOK = 2
        pt = [a.alloc([512], BF16) for _ in range(4)]
        rec = [a.alloc([512], F32) for _ in range(2)]
        old_rot = c.rot
        c.rot = [0, 1, 2, 3]
        accs = [(c.banks[4], c.banks[5]), (c.banks[6], c.banks[7])]
        iters = [(hd, qt, kc) for hd in heads for qt in range(4) for kc in range(16)]
        sbs = {}

        def emit_s(i):
            hd, qt, kc = iters[i]
            sb = c.bank()
            np_ = len(hd["kparts"])
            for pi, (kf, qf) in enumerate(hd["kparts"]):
                c.mm(sb[:, :], kf(kc), qf(qt), pi == 0, pi == np_ - 1, hd["reads"], [sb])
            sbs[i] = sb

        def emit_rest(i):
            hd, qt, kc = iters[i]
            sb = sbs.pop(i)
            accO, accD = accs[(i // 16) % 2]
            p_t = pt[i % 4]
            c.act(p_t[:, :], sb[:, :], AF.Exp, [sb], [p_t], scale=scale)
            c.mm(accO[:, :], hd["v_fn"](kc), p_t[:, :], kc == 0, kc == 15, [p_t] + hd["reads"], [accO])
            c.mm(accD[:, :], self.ones[:, :], p_t[:, :], kc == 0, kc == 15, [p_t, self.ones], [accD])
            if kc == 15:
                r_t = rec[(i // 16) % 2]
                c.op("dve", lambda e, r_t=r_t, accD=accD: e.reciprocal(out=r_t[:, :], in_=accD[:, :]), [accD], [r_t])
                mo = mixT[hd["out"]]
                c.tt("dve", mo[:, qt * 512:(qt + 1) * 512], accO[:, :], r_t[:, :], ALU.mult, [accO, r_t], [mo])

        n = len(iters)
        for i in range(n + LOOK):
            if i < n:
                emit_s(i)
            if i - LOOK >= 0:
                emit_rest(i - LOOK)
        c.rot = old_rot
        a.release(m)

    def gqa_phase(self, b, l, proj, mixT):
        c = self.c
        a = c.arena
        m = a.mark()
        qk = a.alloc([8, S], BF16)
        vv = a.alloc([NT, 256], BF16)
        m2 = a.mark()
        gain = a.alloc([8, 128], F32)
        for h in range(8):
            nm = "q_norm_g" if h < 6 else "k_norm_g"
            c.dma(gain[:, h, :], self.w[nm].ap[l:l + 1, :].to_broadcast([128, 128]), [], [gain])
        xb = [a.alloc([1280], F32) for _ in range(2)]
        sqs = [a.alloc([1024], F32) for _ in range(2)]
        st = [a.alloc([32], F32) for _ in range(2)]
        T1s = [a.alloc([1024], F32) for _ in range(2)]
        T2s = [a.alloc([1024], F32) for _ in range(2)]
        SWs = [a.alloc([1024], F32) for _ in range(2)]
        ob = [a.alloc([1024], BF16) for _ in range(2)]
        tb = [a.alloc([256], F32) for _ in range(2)]
        def stage_a(t):
            xt, s_, o_t, tab = xb[t % 2], st[t % 2], ob[t % 2], tb[t % 2]
            sq, T1, T2, SW = sqs[t % 2], T1s[t % 2], T2s[t % 2], SWs[t % 2]
            c.dma(xt[:, :], proj.ap[t * 128:(t + 1) * 128, 0:1280], [proj.u((t, j)) for j in range(3)], [xt])
            c.dma(tab[:, :], self.k_ropeA.ap[t * 128:(t + 1) * 128].rearrange("p a f -> p (a f)"), [], [tab])
            x3 = xt[:, 0:1024].rearrange("p (h d) -> p h d", h=8)
            c.act(sq[:, :], xt[:, 0:1024], AF.Square, [xt], [sq])
            c.op("dve", lambda e, s_=s_, sq=sq: e.tensor_reduce(out=s_[:, 0:8], in_=sq[:, :].rearrange("p (h d) -> p h d", h=8),
                                                         axis=AX.X, op=ALU.add), [sq], [s_])
            c.act(s_[:, 8:16], s_[:, 0:8], AF.Sqrt, [s_, self.eps], [s_], scale=1.0 / 128, bias=self.eps[:, 0:1])
            c.op("dve", lambda e, s_=s_: e.reciprocal(out=s_[:, 16:24], in_=s_[:, 8:16]), [s_], [s_])
            c.tt("dve", x3, x3, s_[:, 16:24].unsqueeze(2).to_broadcast([128, 8, 128]), ALU.mult, [xt, s_], [xt])
            c.tt("dve", x3, x3, gain[:, :, :], ALU.mult, [xt, gain], [xt])
            self.rope_tm(xt[:, 0:1024], tab[:, :].rearrange("p (a f) -> p a f", a=2), 8, 128, T1, T2, SW, o_t[:, :], [xt, tab], [o_t])
            c.copy("act", vv[:, t, :], xt[:, 1024:1280], [xt], [vv])

        def stage_b(t):
            o_t = ob[t % 2]
            pb = c.bank()
            pv = self.bf_bank(pb, 8)
            for h in range(8):
                c.tr(pv[:, h, :], o_t[:, h * 128:(h + 1) * 128], self.ident[:, :], [o_t, self.ident], [pb])
            c.copy("act", qk[:, :, t * 128:(t + 1) * 128], pv, [pb], [qk])

        stage_a(0)
        for t in range(NT):
            if t + 1 < NT:
                stage_a(t + 1)
            stage_b(t)
        a.release(m2)
        heads = []
        for g in range(2):
            for r in range(3):
                h = g * 3 + r
                heads.append(dict(
                    kparts=[(lambda kc, g=g: qk[:, 6 + g, kc * 128:(kc + 1) * 128],
                             lambda qt, h=h: qk[:, h, qt * 512:(qt + 1) * 512])],
                    v_fn=lambda kc, g=g: vv[:, kc, g * 128:(g + 1) * 128],
                    reads=[qk, vv], out=h))
        self.attn_core(heads, mixT, 128 ** -0.5)
        a.release(m)

    def mla_phase(self, b, l, proj, mixT):
        c = self.c
        a = c.arena
        m = a.mark()
        qn = a.alloc([4, S], BF16)
        qp = a.alloc([4, S], BF16)
        kn = a.alloc([4, S], BF16)
        kp = a.alloc([S], BF16)
        vb = a.alloc([NT, 512], BF16)
        m2 = a.mark()
        cqT = a.alloc([4, S], BF16)
        ckvT = a.alloc([2, S], BF16)
        wuq = a.alloc([4, 768], BF16)
        wukv = a.alloc([2, 1024], BF16)
        c.dma(wuq[:, :, :], self.w["w_uq"].ap[l].rearrange("(k p) n -> p k n", p=128), [], [wuq], q="pool")
        c.dma(wukv[:, :, :], self.w["w_ukv"].ap[l].rearrange("(k p) n -> p k n", p=128), [], [wukv], q="pool")
        gq = a.alloc([512], F32)
        gkv = a.alloc([256], F32)
        self.load_vec_bc(gq, self.w["mla_q_norm_g"].ap[l:l + 1, :], 512)
        self.load_vec_bc(gkv, self.w["mla_kv_norm_g"].ap[l:l + 1, :], 256)
        xb = [a.alloc([832], F32) for _ in range(2)]
        junks = [a.alloc([512], BF16) for _ in range(2)]
        st = [a.alloc([8], F32) for _ in range(2)]
        TK = [[a.alloc([64], F32) for _ in range(3)] for _ in range(2)]
        TQ = [[a.alloc([256], F32) for _ in range(3)] for _ in range(2)]
        ob = [a.alloc([896], BF16) for _ in range(2)]
        tb = [a.alloc([128], F32) for _ in range(2)]
        qpb = [a.alloc([256], F32) for _ in range(2)]
        qpo = [a.alloc([4, 128], BF16) for _ in range(2)]
        for o_ in ob:
            c.op("pool", lambda e, o_=o_: e.memset(o_[:, 832:896], 0.0), [o_], [o_])
        for o_ in qpo:
            c.op("pool", lambda e, o_=o_: e.memset(o_[:, :, :], 0.0), [o_], [o_])
        wqpe = a.alloc([4, 256], BF16)
        wv = a.alloc([2, 512], BF16)
        for h in range(4):
            c.copy("dve", wqpe[:, :, h * 64:(h + 1) * 64], wuq[:, :, h * 192 + 128:h * 192 + 192], [wuq, wqpe], [wqpe])
            c.copy("dve", wv[:, :, h * 128:(h + 1) * 128], wukv[:, :, h * 256 + 128:h * 256 + 256], [wukv, wv], [wv])
        def stage_a(t):
            xt, s_, o_t, tab = xb[t % 2], st[t % 2], ob[t % 2], tb[t % 2]
            junk = junks[t % 2]
            c.dma(xt[:, :], proj.ap[t * 128:(t + 1) * 128, 1280:2112], [proj.u((t, j)) for j in (2, 3, 4)], [xt])
            c.dma(tab[:, :], self.k_ropeB.ap[t * 128:(t + 1) * 128].rearrange("p a f -> p (a f)"), [], [tab])
            c.act(junk[:, 0:512], xt[:, 0:512], AF.Square, [xt], [junk, s_], accum_out=s_[:, 0:1])
            c.act(junk[:, 0:256], xt[:, 512:768], AF.Square, [xt], [junk, s_], accum_out=s_[:, 1:2])
            c.ts("dve", s_[:, 0:1], s_[:, 0:1], 0.5, None, ALU.mult, None, [s_], [s_])
            c.act(s_[:, 2:4], s_[:, 0:2], AF.Sqrt, [s_, self.eps], [s_], scale=1.0 / 256, bias=self.eps[:, 0:1])
            c.op("dve", lambda e, s_=s_: e.reciprocal(out=s_[:, 4:6], in_=s_[:, 2:4]), [s_], [s_])
            c.stt("dve", o_t[:, 0:512], xt[:, 0:512], s_[:, 4:5], gq[:, 0:512], ALU.mult, ALU.mult, [xt, s_, gq], [o_t])
            c.stt("dve", o_t[:, 512:768], xt[:, 512:768], s_[:, 5:6], gkv[:, 0:256], ALU.mult, ALU.mult, [xt, s_, gkv], [o_t])
            tabv = tab[:, :].rearrange("p (a f) -> p a f", a=2)
            T1, T2, SW = TK[t % 2]
            self.rope_tm(xt[:, 768:832], tabv, 1, 64, T1, T2, SW, o_t[:, 768:832], [xt, tab], [o_t])

        def stage_b(t):
            o_t, tab = ob[t % 2], tb[t % 2]
            tabv = tab[:, :].rearrange("p (a f) -> p a f", a=2)
            pb = c.bank()
            pv = self.bf_bank(pb, 8)
            for k in range(6):
                c.tr(pv[:, k, :], o_t[:, k * 128:(k + 1) * 128], self.ident[:, :], [o_t, self.ident], [pb])
            c.tr(pv[:, 6, :], o_t[:, 768:896], self.ident[:, :], [o_t, self.ident], [pb])
            tc_ = slice(t * 128, (t + 1) * 128)
            c.copy("act", cqT[:, :, tc_], pv[:, 0:4, :], [pb], [cqT])
            c.copy("dve", ckvT[:, :, tc_], pv[:, 4:6, :], [pb], [ckvT])
            c.copy("act", kp[:, tc_], pv[:, 6, :], [pb], [kp])
            pq = c.bank()
            for k in range(4):
                c.mm(pq[:, 0:256], cqT[:, k, tc_], wqpe[:, k, :], k == 0, k == 3, [cqT, wqpe], [pq])
            qb_, qo_ = qpb[t % 2], qpo[t % 2]
            c.copy("act", qb_[:, :], pq[:, 0:256], [pq], [qb_])
            T1, T2, SW = TQ[t % 2]
            self.rope_tm(qb_[:, :], tabv, 4, 64, T1, T2, SW, qo_[:, :, 0:64], [qb_, tab], [qo_], out3=True)
            pb2 = c.bank()
            pv2 = self.bf_bank(pb2, 8)
            for h in range(4):
                c.tr(pv2[:, h, :], qo_[:, h, :], self.ident[:, :], [qo_, self.ident], [pb2])
            c.copy("dve", qp[:, :, tc_], pv2[:, 0:4, :], [pb2], [qp])
            pvb = c.bank()
            for k in range(2):
                c.mm(pvb[:, :], ckvT[:, k, tc_], wv[:, k, :], k == 0, k == 1, [ckvT, wv], [pvb])
            c.copy("act", vb[:, t, :], pvb[:, :], [pvb], [vb])
        stage_a(0)
        for t in range(NT):
            if t + 1 < NT:
                stage_a(t + 1)
            stage_b(t)
        i = 0
        for h in range(4):
            for tq in range(4):
                ts_ = slice(tq * 512, (tq + 1) * 512)
                p1 = c.bank()
                for k in range(4):
                    c.mm(p1[:, :], wuq[:, k, h * 192:h * 192 + 128], cqT[:, k, ts_], k == 0, k == 3, [wuq, cqT], [p1])
                c.copy("act", qn[:, h, ts_], p1[:, :], [p1], [qn])
                p2 = c.bank()
                for k in range(2):
                    c.mm(p2[:, :], wukv[:, k, h * 256:h * 256 + 128], ckvT[:, k, ts_], k == 0, k == 1, [wukv, ckvT], [p2])
                c.copy("dve", kn[:, h, ts_], p2[:, :], [p2], [kn])
        a.release(m2)
        heads = []
        for h in range(4):
            heads.append(dict(
                kparts=[(lambda kc, h=h: kn[:, h, kc * 128:(kc + 1) * 128], lambda qt, h=h: qn[:, h, qt * 512:(qt + 1) * 512]),
                        (lambda kc: kp[:, kc * 128:(kc + 1) * 128], lambda qt, h=h: qp[:, h, qt * 512:(qt + 1) * 512])],
                v_fn=lambda kc, h=h: vb[:, kc, h * 128:(h + 1) * 128],
                reads=[qn, qp, kn, kp, vb], out=6 + h))
        self.attn_core(heads, mixT, 192 ** -0.5)
        a.release(m)

    def ssd_consts(self):
        c = self.c
        a = c.arena
        self.triF = a.alloc([128], F32)
        self.triR = a.alloc([128], F32)
        self.onesf = a.alloc([128], F32)
        self.sel = a.alloc([24, 128], F32)
        self.maskF = a.alloc([384], BF16)
        self.maskB = a.alloc([384], BF16)
        c.dma(self.triF[:, :], self.k_triF[:, :], [], [self.triF])
        c.dma(self.triR[:, :], self.k_triR[:, :], [], [self.triR])
        c.op("pool", lambda e: e.memset(self.onesf[:, :], 1.0), [], [self.onesf])
        c.dma(self.sel[:, :, :], self.k_sel[:, :, :], [], [self.sel])
        c.dma(self.maskF[:, :], self.k_maskF[:, :], [], [self.maskF])
        c.dma(self.maskB[:, :], self.k_maskB[:, :], [], [self.maskB])

    def ssd_phase(self, b, l, proj, mixT):
        c = self.c
        a = c.arena
        m = a.mark()
        yf_d = c.scratch((S, 768))
        aneg = a.alloc([24], F32)
        dtb = a.alloc([24], F32)
        dsk = a.alloc([12], F32)
        ng = a.alloc([768], F32)
        cw = a.alloc([10, 6], F32)
        c.dma(aneg[:, :], self.w["a_log"].ap[l:l + 1].rearrange("o a h -> o (a h)").to_broadcast([128, 24]), [], [aneg])
        c.act(aneg[:, :], aneg[:, :], AF.Exp, [aneg], [aneg])
        c.ts("dve", aneg[:, :], aneg[:, :], -1.0, None, ALU.mult, None, [aneg], [aneg])
        c.dma(dtb[:, :], self.w["dt_bias"].ap[l:l + 1].rearrange("o a h -> o (a h)").to_broadcast([128, 24]), [], [dtb])
        c.dma(dsk[:, :], self.w["d_skip"].ap[l:l + 1, :].to_broadcast([128, 12]), [], [dsk])
        self.load_vec_bc(ng, self.w["ssd_norm_g"].ap[l:l + 1, :], 768)
        c.dma(cw[:, :, :], self.k_convT.ap[l], [], [cw])
        dt = a.alloc([NT, 24], F32)
        af = a.alloc([NT, 24], F32)
        ab = a.alloc([NT, 24], F32)
        c.op("pool", lambda e: e.memset(af[:, :, :], 0.0), [], [af])
        c.op("pool", lambda e: e.memset(ab[:, :, :], 0.0), [], [ab])
        for t in range(NT):
            c.dma(dt[:, t, :], proj.ap[t * 128:(t + 1) * 128, 4160:4184], [proj.u((t, 8))], [dt])
        c.tt("dve", dt[:, :, :], dt[:, :, :], dtb[:, :].unsqueeze(1).to_broadcast([128, NT, 24]), ALU.add, [dt, dtb], [dt])
        c.act(dt[:, :, :], dt[:, :, :], AF.Exp, [dt], [dt])
        c.act(dt[:, :, :], dt[:, :, :], AF.Ln, [dt, self.one1], [dt], bias=self.one1[:, 0:1])
        c.tt("dve", af[:, :, 0:12], dt[:, :, 0:12], aneg[:, 0:12].unsqueeze(1).to_broadcast([128, NT, 12]), ALU.mult, [dt, aneg, af], [af])
        c.tt("dve", ab[:, :, 12:24], dt[:, :, 12:24], aneg[:, 12:24].unsqueeze(1).to_broadcast([128, NT, 12]), ALU.mult, [dt, aneg, ab], [ab])
        bcT = a.alloc([4, S], BF16)
        xs_tm = a.alloc([NT, 768], BF16)
        b_tm = a.alloc([NT, 256], BF16)
        m2 = a.mark()
        xin = [a.alloc([8, 128], F32) for _ in range(2)]
        xT = [a.alloc([S + 4], F32) for _ in range(2)]
        acc = [a.alloc([S], F32)] * 2
        cvo = [a.alloc([S], BF16) for _ in range(2)]
        for xt_ in xT:
            c.op("pool", lambda e, xt_=xt_: e.memset(xt_[:, 0:2], 0.0), [xt_], [xt_])
            c.op("pool", lambda e, xt_=xt_: e.memset(xt_[:, S + 2:S + 4], 0.0), [xt_], [xt_])
        ii = 0
        for blk in range(10):
            x_t, ac, co = xT[blk % 2], acc[blk % 2], cvo[blk % 2]
            cs = 2880 + blk * 128
            for t8 in range(2):
                xi = xin[ii % 2]
                ii += 1
                c.dma(xi[:, :, :], proj.ap[t8 * 1024:(t8 + 1) * 1024, cs:cs + 128].rearrange("(t p) c -> p t c", p=128),
                      [proj.u((t, j)) for t in range(t8 * 8, t8 * 8 + 8) for j in (5, 6, 7, 8)], [xi])
                for t4 in range(2):
                    pb = c.bank()
                    for q in range(4):
                        c.mm(pb[:, q * 128:(q + 1) * 128], xi[:, t4 * 4 + q, :], self.identf[:, :], True, True, [xi, self.identf], [pb])
                    o0 = 2 + (t8 * 2 + t4) * 512
                    c.copy("act", x_t[:, o0:o0 + 512], pb[:, :], [pb], [x_t])
            c.ts("dve", ac[:, :], x_t[:, 0:S], cw[:, blk, 0:1], None, ALU.mult, None, [x_t, cw], [ac])
            for j in range(1, 5):
                c.stt("dve", ac[:, :], x_t[:, j:j + S], cw[:, blk, j:j + 1], ac[:, :], ALU.mult, ALU.add, [x_t, cw, ac], [ac])
            if blk < 6 or blk in (6, 7):
                c.act(co[:, :], ac[:, :], AF.Silu, [ac, cw], [co], bias=cw[:, blk, 5:6])
                for t8 in range(2):
                    pb = c.bank()
                    pv = self.bf_bank(pb, 8)
                    for q in range(8):
                        t = t8 * 8 + q
                        c.tr(pv[:, q, :], co[:, t * 128:(t + 1) * 128], self.ident[:, :], [co, self.ident], [pb])
                    if blk < 6:
                        c.copy("act" if t8 else "dve", xs_tm[:, t8 * 8:(t8 + 1) * 8, blk * 128:(blk + 1) * 128], pv, [pb], [xs_tm])
                    else:
                        c.copy("act" if t8 else "dve", b_tm[:, t8 * 8:(t8 + 1) * 8, (blk - 6) * 128:(blk - 5) * 128], pv, [pb], [b_tm])
            if blk >= 6:
                c.act(bcT[:, blk - 6, :], ac[:, :], AF.Silu, [ac, cw], [bcT], bias=cw[:, blk, 5:6])
        a.release(m2)
        H = [[a.alloc([384], F32) for _ in range(2)] for _ in range(2)]
        Hb = [[a.alloc([384], BF16) for _ in range(2)] for _ in range(2)]
        NS = 4
        apad = [a.alloc([128], F32) for _ in range(NS)]
        for ap_ in apad:
            c.op("pool", lambda e, ap_=ap_: e.memset(ap_[:, :], 0.0), [], [ap_])
        cs_ = [a.alloc([72], F32) for _ in range(NS)]
        ex = [a.alloc([72], F32) for _ in range(NS)]
        pfm = [a.alloc([256], F32) for _ in range(NS)]
        xd = [a.alloc([768], BF16) for _ in range(NS)]
        xdw = [a.alloc([768], BF16) for _ in range(2)] * 2
        xdw = [xdw[0], xdw[0], xdw[1], xdw[1]]
        yo = [a.alloc([768], F32) for _ in range(NS)]
        cbT = [[a.alloc([128], F32) for _ in range(2)] for _ in range(2)]
        eD = [[a.alloc([384], F32) for _ in range(2)] for _ in range(2)]
        wT = [[a.alloc([384], BF16) for _ in range(3)] for _ in range(2)]
        ytmp = [[a.alloc([384], F32)] * 2 for _ in range(2)]
        zb = [a.alloc([768], F32) for _ in range(2)]
        yfb = [a.alloc([768], F32) for _ in range(2)]
        yn = [a.alloc([768], BF16) for _ in range(2)]
        st = [a.alloc([8], F32) for _ in range(2)]
        junk = a.alloc([384], BF16)
        accA, accB = c.banks[6], c.banks[7]
        for d in range(2):
            for g in range(2):
                c.op("pool", lambda e, d=d, g=g: e.memset(H[d][g][:, :], 0.0), [H[d][g]], [H[d][g]])
                c.op("pool", lambda e, d=d, g=g: e.memset(Hb[d][g][:, :], 0.0), [Hb[d][g]], [Hb[d][g]])
        cnt = {"w": [0, 0], "f": 0}
        pend = []

        def chunk_step(d, ch, k, finalize):
            mask = self.maskF if d == 0 else self.maskB
            a_own = af if d == 0 else ab
            tri = self.triF if d == 0 else self.triR
            cst, ext, pf, xd_, xw_ = cs_[k], ex[k], pfm[k], xd[k], xdw[k]
            tc_ = slice(ch * 128, (ch + 1) * 128)
            d0 = d * 12
            ap_ = apad[k]
            c.copy("act", ap_[:, 0:24], a_own[:, ch, :], [a_own, ap_], [ap_])
            pb = c.bank()
            c.mm(pb[:, 0:24], tri[:, :], a_own[:, ch, 0:24], True, True, [tri, a_own], [pb])
            c.mm(pb[:, 24:48], self.onesf[:, :], a_own[:, ch, 0:24], True, True, [self.onesf, a_own], [pb])
            c.mm(pb[:, 128:256], ap_[:, :], tri[:, :], True, True, [tri, ap_], [pb])
            yield
            c.copy("dve", cst[:, 0:48], pb[:, 0:48], [pb], [cst])
            c.copy("act", pf[:, 0:128], pb[:, 128:256], [pb], [pf])
            c.ts("dve", pf[:, 128:256], pb[:, 128:256], -1.0, None, ALU.mult, None, [pb], [pf])
            c.tt("dve", cst[:, 48:72], cst[:, 24:48], cst[:, 0:24], ALU.subtract, [cst], [cst])
            c.act(ext[:, 0:24], cst[:, 0:24], AF.Exp, [cst], [ext])
            c.act(ext[:, 24:48], cst[:, 48:72], AF.Exp, [cst], [ext])
            c.act(ext[:, 48:72], cst[:, 24:48], AF.Exp, [cst], [ext])
            xs3 = xs_tm[:, ch, :].rearrange("p (h q) -> p h q", h=12)
            xd3 = xd_[:, :].rearrange("p (h q) -> p h q", h=12)
            xw3 = xw_[:, :].rearrange("p (h q) -> p h q", h=12)
            c.tt("pool", xd3, xs3, dt[:, ch, d0:d0 + 12].unsqueeze(2).to_broadcast([128, 12, 64]), ALU.mult, [xs_tm, dt], [xd_])
            c.tt("pool", xw3, xd3, ext[:, 24 + d0:36 + d0].unsqueeze(2).to_broadcast([128, 12, 64]), ALU.mult, [xd_, ext], [xw_])
            yield
            y_t = yo[k]
            for g in range(2):
                cb = cbT[d][g]
                Hg, Hbg = H[d][g], Hb[d][g]
                pcb = c.bank()
                c.mm(pcb[:, 0:128], bcT[:, g, tc_], bcT[:, 2 + g, tc_], True, True, [bcT], [pcb])
                yield
                c.copy("act", cb[:, :], pcb[:, 0:128], [pcb], [cb])
                accY = (accA if g == 0 else accB) if d == 0 else (c.banks[4] if g == 0 else c.banks[5])
                for half in range(2):
                    pd = c.bank()
                    for j3 in range(3):
                        hh = d0 + g * 6 + half * 3 + j3
                        reg = pd[:, j3 * 128:(j3 + 1) * 128]
                        c.mm(reg, self.sel[:, hh, :], pf[:, 0:128], j3 == 0, False, [self.sel, pf], [pd])
                        c.mm(reg, pf[:, 128:256], self.sel[:, hh, :], False, False, [self.sel, pf], [pd])
                    c.mm(pd[:, 0:384], self.ident[:, :], mask[:, :], False, True, [self.ident, mask], [pd])
                    yield
                    e_t = eD[d][half]
                    c.act(e_t[:, :], pd[:, 0:384], AF.Exp, [pd], [e_t])
                    w_t = wT[d][cnt["w"][d] % 3]
                    cnt["w"][d] += 1
                    c.tt("dve", w_t[:, :].rearrange("p (j s) -> p j s", j=3), e_t[:, :].rearrange("p (j s) -> p j s", j=3),
                         cb[:, :].unsqueeze(1).to_broadcast([128, 3, 128]), ALU.mult, [e_t, cb], [w_t])
                    yield
                    for j3 in range(3):
                        hl = g * 6 + half * 3 + j3
                        c.mm(accY[:, (half * 3 + j3) * 64:(half * 3 + j3 + 1) * 64], w_t[:, j3 * 128:(j3 + 1) * 128],
                             xd_[:, hl * 64:(hl + 1) * 64], True, True, [w_t, xd_], [accY])
                yield
                po = c.bank()
                c.mm(po[:, 0:384], bcT[:, 2 + g, tc_], Hbg[:, :], True, True, [bcT, Hbg], [po])
                yt = ytmp[d][g]
                c.tt("dve", yt[:, :].rearrange("p (h q) -> p h q", h=6), po[:, 0:384].rearrange("p (h q) -> p h q", h=6),
                     ext[:, d0 + g * 6:d0 + g * 6 + 6].unsqueeze(2).to_broadcast([128, 6, 64]), ALU.mult, [po, ext], [yt])
                c.tt("dve", y_t[:, g * 384:(g + 1) * 384], accY[:, 0:384], yt[:, :], ALU.add, [accY, yt], [y_t])
                ps_ = c.bank()
                c.mm(ps_[:, 0:384], b_tm[:, ch, g * 128:(g + 1) * 128], xw_[:, g * 384:(g + 1) * 384], True, True,
                     [b_tm, xw_], [ps_])
                yield
                c.tt("dve", Hg[:, :].rearrange("p (h q) -> p h q", h=6), Hg[:, :].rearrange("p (h q) -> p h q", h=6),
                     ext[:, 48 + d0 + g * 6:48 + d0 + g * 6 + 6].unsqueeze(2).to_broadcast([128, 6, 64]), ALU.mult,
                     [Hg, ext], [Hg])
                c.tt("dve", Hg[:, :], Hg[:, :], ps_[:, 0:384], ALU.add, [Hg, ps_], [Hg])
                c.copy("act", Hbg[:, :], Hg[:, :], [Hg], [Hbg])
                yield
            if not finalize:
                c.dma(yf_d.ap[tc_, :], y_t[:, :], [y_t], [yf_d.u(ch)])
                return
            pend.append((d, ch, k))

        def fin_step(d, ch, k):
            y_t = yo[k]
            tc_ = slice(ch * 128, (ch + 1) * 128)
            kf = cnt["f"] % 2
            cnt["f"] += 1
            z_t, yf_t, yn_t, s_ = zb[kf], yfb[kf], yn[kf], st[kf]
            c.dma(yf_t[:, :], yf_d.ap[tc_, :], [yf_d.u(ch)], [yf_t])
            c.dma(z_t[:, :], proj.ap[tc_, 2112:2880], [proj.u((ch, 4)), proj.u((ch, 5))], [z_t])
            c.tt("dve", y_t[:, :], y_t[:, :], yf_t[:, :], ALU.add, [y_t, yf_t], [y_t])
            c.tt("pool", yf_t[:, :].rearrange("p (h q) -> p h q", h=12), xs_tm[:, ch, :].rearrange("p (h q) -> p h q", h=12),
                 dsk[:, :].unsqueeze(2).to_broadcast([128, 12, 64]), ALU.mult, [xs_tm, dsk], [yf_t])
            c.tt("dve", y_t[:, :], y_t[:, :], yf_t[:, :], ALU.add, [y_t, yf_t], [y_t])
            c.act(z_t[:, :], z_t[:, :], AF.Silu, [z_t], [z_t])
            yield
            c.tt("dve", y_t[:, :], y_t[:, :], z_t[:, :], ALU.mult, [y_t, z_t], [y_t])
            for g in range(2):
                c.act(junk[:, :], y_t[:, g * 384:(g + 1) * 384], AF.Square, [y_t], [junk, s_], accum_out=s_[:, g:g + 1])
            yield
            c.act(s_[:, 2:4], s_[:, 0:2], AF.Sqrt, [s_, self.eps], [s_], scale=1.0 / 384, bias=self.eps[:, 0:1])
            c.op("dve", lambda e, s_=s_: e.reciprocal(out=s_[:, 4:6], in_=s_[:, 2:4]), [s_], [s_])
            for g in range(2):
                c.stt("dve", yn_t[:, g * 384:(g + 1) * 384], y_t[:, g * 384:(g + 1) * 384], s_[:, 4 + g:5 + g],
                      ng[:, g * 384:(g + 1) * 384], ALU.mult, ALU.mult, [y_t, s_, ng], [yn_t])
            yield
            pb = c.bank()
            pv = self.bf_bank(pb, 8)
            for j in range(6):
                c.tr(pv[:, j, :], yn_t[:, j * 128:(j + 1) * 128], self.ident[:, :], [yn_t, self.ident], [pb])
            yield
            for j in range(6):
                c.copy("act" if j % 2 else "dve", mixT[10 + j][:, tc_], pv[:, j, :], [pb], [mixT[10 + j]])

        def run_group(gens):
            alive = [True] * len(gens)
            while any(alive):
                for i_ in range(len(gens)):
                    if alive[i_]:
                        try:
                            next(gens[i_])
                        except StopIteration:
                            alive[i_] = False

        old_rot = c.rot
        c.rot = [0, 1, 2, 3]
        for step in range(NT):
            todo = list(pend)
            del pend[:]
            run_group([chunk_step(0, step, (step % 2), step >= NT // 2),
                       chunk_step(1, NT - 1 - step, 2 + (step % 2), step >= NT // 2)] +
                      [fin_step(*p) for p in todo])
        run_group([fin_step(*p) for p in pend])
        c.rot = old_rot
        a.release(m)

    def out_phase(self, b, l, mixT, xsrc, xkey, x1):
        c = self.c
        a = c.arena
        m = a.mark()
        GT = a.alloc([D], F32)
        self.load_bc(GT, l, b, 2)
        wv = self.w["w_out"].ap[l].rearrange("(k p) n -> p k n", p=128)
        wb = [a.alloc([16, 512], BF16) for _ in range(2)]
        xb = [a.alloc([512], F32) for _ in range(4)]
        ob = [a.alloc([512], F32) for _ in range(3)]
        its = [(n, t) for n in range(4) for t in range(NT)]

        def load(i):
            n, t = its[i]
            c.dma(xb[i % 4][:, :], xsrc[t * 128:(t + 1) * 128, n * 512:(n + 1) * 512], [xkey(t)], [xb[i % 4]])

        load(0)
        load(1)
        for i, (n, t) in enumerate(its):
            if i + 2 < len(its):
                load(i + 2)
            w_t = wb[n % 2]
            ns = slice(n * 512, (n + 1) * 512)
            if t == 0:
                c.dma(w_t[:, :, :], wv[:, :, ns], [], [w_t], q="pool")
            x_t, o_t = xb[i % 4], ob[i % 3]
            ts_ = slice(t * 128, (t + 1) * 128)
            pb = c.bank()
            for k in range(16):
                c.mm(pb[:, :], mixT[k][:, ts_], w_t[:, k, :], k == 0, k == 15, [mixT[k], w_t], [pb])
            c.tt("dve", o_t[:, :], pb[:, :], GT[:, ns], ALU.mult, [pb, GT], [o_t])
            c.tt("dve", o_t[:, :], o_t[:, :], x_t[:, :], ALU.add, [o_t, x_t], [o_t])
            c.dma(x1.ap[ts_, ns], o_t[:, :], [o_t], [x1.u((t, n))])
        a.release(m)

    def norm2_phase(self, b, l, x1, h2T_d):
        c = self.c
        a = c.arena
        m = a.mark()
        stg = [a.alloc([16, 512], BF16) for _ in range(2)]

        def dst(t):
            return stg[(t // 4) % 2], (t % 4) * 128

        def after(t):
            if t % 4 == 3:
                s_ = stg[(t // 4) % 2]
                q = t // 4
                c.dma(h2T_d.ap[:, :, q * 512:(q + 1) * 512], s_[:, :, :], [s_], [h2T_d.u(q)])

        self.norm_phase(b, l, x1.ap, lambda t: [x1.u((t, n)) for n in range(4)], "norm2_g", 4, 3, None, dst=dst, after=after)
        a.release(m)

    def ffn_phase(self, b, l, x1, h2T_d, x2):
        c = self.c
        a = c.arena
        m = a.mark()
        TB = 1024
        wgu = self.w["w_gate_up"].ap[l].rearrange("(k p) n -> p k n", p=128)
        wdn = self.w["w_down"].ap[l].rearrange("(j p) n -> p j n", p=128)
        g2 = a.alloc([16], F32)
        g2row = a.alloc([128], F32)
        c.op("pool", lambda e: e.memset(g2row[:, :], 0.0), [], [g2row])
        c.dma(g2row[0:16, :], self.mod.ap[l, b, 5 * D:6 * D].rearrange("(k p) -> k p", p=128),
              [self.mod.u((l, 20 + q)) for q in range(4)] + [g2row], [g2row])
        pg2 = c.bank()
        c.mm(pg2[:, 0:16], g2row[:, :], self.identf[:, 0:16], True, True, [g2row, self.identf], [pg2])
        c.copy("dve", g2[:, :], pg2[:, 0:16], [pg2], [g2])
        actT = a.alloc([44, TB], BF16)
        sgb = [a.alloc([512], F32) for _ in range(2)]
        xb = [a.alloc([256], F32) for _ in range(4)]
        ob = [a.alloc([256], F32) for _ in range(2)]
        gtb = [a.alloc([512], F32) for _ in range(4)]
        h2 = a.alloc([16, TB], BF16)

        def load_h2(tb):
            for q in range(TB // 512):
                qq = tb * (TB // 512) + q
                c.dma(h2[:, :, q * 512:(q + 1) * 512], h2T_d.ap[:, :, qq * 512:(qq + 1) * 512], [h2T_d.u(qq)], [h2])

        load_h2(0)
        for tb in range(S // TB):
            m2 = a.mark()
            wg = [a.alloc([16, 256], BF16) for _ in range(3)]
            wu = [a.alloc([16, 256], BF16) for _ in range(3)]
            isg = 0
            for jg in range(22):
                g_t, u_t = wg[jg % 3], wu[jg % 3]
                c.dma(g_t[:, :, :], wgu[:, :, jg * 256:(jg + 1) * 256], [], [g_t], q="pool")
                c.dma(u_t[:, :, :], wgu[:, :, FFN + jg * 256:FFN + (jg + 1) * 256], [], [u_t], q="pool")
                for jj in range(2):
                    j = jg * 2 + jj
                    for tq in range(TB // 512):
                        tqs = slice(tq * 512, (tq + 1) * 512)
                        pg = c.bank()
                        for k in range(16):
                            c.mm(pg[:, :], g_t[:, k, jj * 128:(jj + 1) * 128], h2[:, k, tqs], k == 0, k == 15, [g_t, h2], [pg])
                        pu = c.bank()
                        for k in range(16):
                            c.mm(pu[:, :], u_t[:, k, jj * 128:(jj + 1) * 128], h2[:, k, tqs], k == 0, k == 15, [u_t, h2], [pu])
                        sg = sgb[isg % 2]
                        isg += 1
                        c.act(sg[:, :], pg[:, :], AF.Silu, [pg], [sg])
                        c.tt("dve", actT[:, j, tqs], pu[:, :], sg[:, :], ALU.mult, [pu, sg], [actT])
            a.release(m2)
            if tb + 1 < S // TB:
                load_h2(tb + 1)
            m2 = a.mark()
            wd = [a.alloc([44, 256], BF16) for _ in range(3)]
            igt = 0
            dits = [(sl, tq, tt_) for sl in range(8) for tq in range(TB // 512) for tt_ in range(4)]

            def dload(i):
                sl, tq, tt_ = dits[i]
                tok0 = tb * TB + tq * 512 + tt_ * 128
                c.dma(xb[i % 4][:, 0:256], x1.ap[tok0:tok0 + 128, sl * 256:(sl + 1) * 256], [x1.u((tok0 // 128, sl // 2))], [xb[i % 4]])

            dload(0)
            dload(1)
            gts = []
            for i, (sl, tq, tt_) in enumerate(dits):
                if i + 2 < len(dits):
                    dload(i + 2)
                w_t = wd[sl % 3]
                tqs = slice(tq * 512, (tq + 1) * 512)
                if tq == 0 and tt_ == 0:
                    c.dma(w_t[:, :, :], wdn[:, :, sl * 256:(sl + 1) * 256], [], [w_t], q="pool")
                if tt_ == 0:
                    gts = []
                    for dd in range(2):
                        dblk = sl * 2 + dd
                        pb = c.bank()
                        for j in range(44):
                            c.mm(pb[:, :], w_t[:, j, dd * 128:(dd + 1) * 128], actT[:, j, tqs], j == 0, j == 43, [w_t, actT], [pb])
                        gt = gtb[igt % 4]
                        igt += 1
                        c.act(gt[:, :], pb[:, :], AF.Identity, [pb, g2], [gt], scale=g2[:, dblk:dblk + 1])
                        gts.append(gt)
                tok0 = tb * TB + tq * 512 + tt_ * 128
                tki = tok0 // 128
                x_t, o_t = xb[i % 4], ob[i % 2]
                cs = slice(sl * 256, (sl + 1) * 256)
                pt = c.bank()
                for dd in range(2):
                    c.mm(pt[:, dd * 128:(dd + 1) * 128], gts[dd][:, tt_ * 128:(tt_ + 1) * 128], self.identf[:, :], True, True,
                         [gts[dd], self.identf], [pt])
                c.tt("dve", o_t[:, 0:256], pt[:, 0:256], x_t[:, 0:256], ALU.add, [pt, x_t], [o_t])
                c.dma(x2.ap[tok0:tok0 + 128, cs], o_t[:, 0:256], [o_t], [x2.u((tki, sl))])
            a.release(m2)
        a.release(m)

    def final_phase(self, b, x2, x2key):
        c = self.c
        a = c.arena
        m = a.mark()
        G = a.alloc([D], F32)
        self.load_vec_bc(G, self.w["final_norm_g"].ap.rearrange("(o d) -> o d", o=1), D)
        xb = [a.alloc([D], F32) for _ in range(2)]
        junk = a.alloc([D], BF16)
        st = [a.alloc([4], F32) for _ in range(2)]
        for t in range(NT):
            xt, s_ = xb[t % 2], st[t % 2]
            ts_ = slice(t * 128, (t + 1) * 128)
            c.dma(xt[:, :], x2.ap[ts_, :], x2key(t), [xt])
            c.act(junk[:, :], xt[:, :], AF.Square, [xt], [junk, s_], accum_out=s_[:, 0:1])
            c.act(s_[:, 1:2], s_[:, 0:1], AF.Sqrt, [s_, self.eps], [s_], scale=1.0 / D, bias=self.eps[:, 0:1])
            c.op("dve", lambda e, s_=s_: e.reciprocal(out=s_[:, 2:3], in_=s_[:, 1:2]), [s_], [s_])
            c.stt("dve", xt[:, :], xt[:, :], s_[:, 2:3], G[:, :], ALU.mult, ALU.mult, [xt, s_, G], [xt])
            c.dma(self.out.ap[b, ts_, :], xt[:, :], [xt], [self.out.u((b, t))])
        a.release(m)

    def build(self, stop_after=None):
        c = self.c
        a = c.arena
        self.consts()
        self.mod_phase()
        for b in range(self.nseq):
            xcur = self.x.ap[b]
            xkey = lambda t: []
            for l in range(self.nlayer):
                m = a.mark()
                proj = self.scr("proj", (S, IN_COLS))
                x1 = self.scr("x1", (S, D))
                x2 = self.scr("x2", (S, D))
                h2T_d = self.scr("h2T", (128, 16, S), BF16)
                self.last = dict(proj=proj, x1=x1, x2=x2, h2T_d=h2T_d)
                mixT = [a.alloc([S], BF16) for _ in range(16)]
                self.mixT = mixT
                m1 = a.mark()
                hT = [a.alloc([16, 128], BF16) for _ in range(NT)]
                self.norm_phase(b, l, xcur, xkey, "norm1_g", 1, 0, hT)
                self.in_proj_tm(b, l, hT, proj)
                a.release(m1)
                if stop_after == "in":
                    return
                self.gqa_phase(b, l, proj, mixT)
                if stop_after == "gqa":
                    self.dump_mix(mixT, 6)
                    return
                self.mla_phase(b, l, proj, mixT)
                if stop_after == "mla":
                    self.dump_mix(mixT, 10)
                    return
                m1 = a.mark()
                self.ssd_consts()
                self.ssd_phase(b, l, proj, mixT)
                a.release(m1)
                self.dump_mix(mixT)
                if stop_after == "ssd":
                    return
                self.out_phase(b, l, mixT, xcur, xkey, x1)
                a.release(m)
                if stop_after == "out":
                    return
                self.norm2_phase(b, l, x1, h2T_d)
                self.ffn_phase(b, l, x1, h2T_d, x2)
                xcur = x2.ap
                xkey = (lambda x2: (lambda t: [x2.u((t, n)) for n in range(8)]))(x2)
                if stop_after == "ffn":
                    return
            self.final_phase(b, x2, xkey)


_PROG = None


def _get_prog():
    global _PROG
    if _PROG is None:
        p = Prog()
        p.build()
        p.c.sch.emit()
        _PROG = p
    return _PROG


def make_in_maps(inputs, ncores=NCORES):
    consts = host_consts()
    consts["convT"] = layout_conv(np.asarray(inputs["conv_w"], np.float32), np.asarray(inputs["conv_b"], np.float32))
    shared = {k: np.ascontiguousarray(np.asarray(inputs[k], np.float32)) for k in WSHAPES}
    shared.update(consts)
    x = np.asarray(inputs["x"], np.float32)
    cc = np.asarray(inputs["c"], np.float32)
    maps = []
    for i in range(ncores):
        m = dict(shared)
        m["x"] = np.ascontiguousarray(x[2 * i:2 * i + 2])
        m["cT"] = np.ascontiguousarray(cc[2 * i:2 * i + 2].reshape(2, 16, 128).transpose(2, 1, 0).reshape(128, 32))
        maps.append(m)
    return maps


def kernel(**inputs):
    p = _get_prog()
    maps = make_in_maps(inputs)
    res = run_bass_kernel_spmd(p.c.nc, maps, core_ids=list(range(NCORES)))
    out = np.concatenate([np.asarray(r["out"], np.float32) for r in res.results], axis=0)
    return out
```
